# Optimizing a Trainium2 kernel written in Bass

```python
import math
import jax, jax.numpy as jnp
from jax import lax
import numpy as np

D_MODEL = 1024
BATCH = 32
SEQ = 256
DEPTH = 1
DEC_BATCH = 2
DEC_SEQ = 1024
PAST_LEN = 512

GRID_W = 64
MIX_WIDTH = D_MODEL
H_A = 4
DK_A = MIX_WIDTH // 8
DV_A = MIX_WIDTH // 8
H_B = 4
D_HEAD_B = MIX_WIDTH // 16
D_FF = ((8 * D_MODEL // 3 + 127) // 128) * 128
HGRN_CHUNK = 32
Q_BLOCK = 128
ROPE_THETA = 10000.0
N_MOD = 9
EPS = 1e-6

kernel_name = "hymba_hgrn2_diffattn_macaron_dit_step"


def rms_norm(x, g):
    xf = x.astype(jnp.float32)
    y = xf * lax.rsqrt(jnp.mean(xf * xf, axis=-1, keepdims=True) + EPS)
    return (y * g.astype(jnp.float32)).astype(x.dtype)


def swiglu(h, w_in, w_out):
    gate, up = jnp.split(h @ w_in, 2, axis=-1)
    return (jax.nn.silu(gate) * up) @ w_out


def rope_tables(n_tokens):
    rows = n_tokens // GRID_W
    pos_row = jnp.repeat(jnp.arange(rows, dtype=jnp.float32), GRID_W)
    pos_col = jnp.tile(jnp.arange(GRID_W, dtype=jnp.float32), rows)
    half = D_HEAD_B // 2
    inv = ROPE_THETA ** (-jnp.arange(0, half, 2, dtype=jnp.float32) / half)
    ang_r = pos_row[:, None] * inv
    ang_c = pos_col[:, None] * inv
    return (jnp.cos(ang_r), jnp.sin(ang_r), jnp.cos(ang_c), jnp.sin(ang_c))


def apply_rope_2d(x, tables):
    cr, sr, cc, sc = [t[None, :, None, None, :].astype(x.dtype) for t in tables]
    xr, xc = jnp.split(x, 2, axis=-1)

    def rot(a, cos, sin):
        a1, a2 = jnp.split(a, 2, axis=-1)
        return jnp.concatenate([a1 * cos - a2 * sin, a2 * cos + a1 * sin], axis=-1)

    return jnp.concatenate([rot(xr, cr, sr), rot(xc, cc, sc)], axis=-1)


def hgrn_scan(q, k, log_f, v, s0):
    B, T, H, _ = q.shape
    dv = v.shape[-1]
    n = T // HGRN_CHUNK

    def chunks(a):
        return a.astype(jnp.float32).reshape(B, n, HGRN_CHUNK, H, a.shape[-1]).swapaxes(0, 1)

    tri = jnp.tril(jnp.ones((HGRN_CHUNK, HGRN_CHUNK), dtype=bool))[None, :, :, None, None]

    def body(s, blk):
        qc, kc, lf, vc = blk
        b = jnp.cumsum(lf, axis=1)
        decay = jnp.exp(jnp.where(tri, b[:, :, None] - b[:, None, :], -jnp.inf))
        scores = jnp.einsum('bthk,bshk,btshk->bhts', qc, kc, decay)
        o = (jnp.einsum('bhts,bshv->bthv', scores, vc)
             + jnp.einsum('bthk,bhkv->bthv', qc * jnp.exp(b), s))
        b_end = b[:, -1]
        s = (jnp.exp(b_end)[..., None] * s
             + jnp.einsum('bshk,bshv->bhkv', kc * jnp.exp(b_end[:, None] - b), vc))
        return s, o

    s_fin, o = lax.scan(body, s0.astype(jnp.float32), (chunks(q), chunks(k), chunks(log_f), chunks(v)))
    return o.swapaxes(0, 1).reshape(B, T, H, dv), s_fin


def diff_attention(q, k, v, lam):
    B, Tq = q.shape[:2]
    scale = D_HEAD_B ** -0.5
    qb = q.reshape(B, Tq // Q_BLOCK, Q_BLOCK, H_B, 2, D_HEAD_B).swapaxes(0, 1)

    def one_block(qi):
        s = jnp.einsum('bqhmd,bkhmd->bhmqk', qi, k).astype(jnp.float32) * scale
        p = jax.nn.softmax(s, axis=-1)
        a = p[:, :, 0] - lam * p[:, :, 1]
        return jnp.einsum('bhqk,bkhe->bqhe', a.astype(v.dtype), v)

    o = lax.map(one_block, qb)
    return o.swapaxes(0, 1).reshape(B, Tq, H_B, 2 * D_HEAD_B)


def token_mix(h, p, lb, lam_init, ctx_cache=None, rope=None):
    B, T, _ = h.shape
    sizes = [H_A * DK_A, H_A * DV_A, H_A * DK_A, H_A * DK_A, H_A * DV_A,
             H_B * 2 * D_HEAD_B, H_B * 2 * D_HEAD_B, H_B * 2 * D_HEAD_B]
    idx = []
    acc = 0
    for s in sizes[:-1]:
        acc += s
        idx.append(acc)
    proj = h @ p['w_mix_in']
    hq, hi, hf_fwd, hf_bwd, hg, aq, ak, av = jnp.split(proj, idx, axis=-1)

    def heads(a, d):
        return a.reshape(B, T, H_A, d)

    q_rec = heads(hq, DK_A)
    i_rec = heads(hi, DV_A)
    lb_h = lb.reshape(H_A, DK_A)

    def forget(hf):
        f = lb_h + (1.0 - lb_h) * jax.nn.sigmoid(heads(hf, DK_A).astype(jnp.float32))
        return jnp.log(f), 1.0 - f

    lf_f, k_f = forget(hf_fwd)
    lf_b, k_b = forget(hf_bwd)
    if ctx_cache is None:
        s0_f = jnp.zeros((B, H_A, DK_A, DV_A), jnp.float32)
        s0_b = s0_f
    else:
        s0_f = ctx_cache[2][:, 0]
        s0_b = ctx_cache[2][:, 1]
    o_f, s_f = hgrn_scan(q_rec, k_f, lf_f, i_rec, s0_f)
    rev = lambda a: jnp.flip(a, axis=1)
    o_b, s_b = hgrn_scan(rev(q_rec), rev(k_b), rev(lf_b), rev(i_rec), s0_b)
    o_rec = rms_norm(o_f + rev(o_b), p['hgrn_out_norm']).astype(h.dtype) * jax.nn.silu(heads(hg, DV_A))

    q_att = rms_norm(aq.reshape(B, T, H_B, 2, D_HEAD_B), p['attn_q_norm'])
    k_att = rms_norm(ak.reshape(B, T, H_B, 2, D_HEAD_B), p['attn_k_norm'])
    v_att = av.reshape(B, T, H_B, 2 * D_HEAD_B)
    if ctx_cache is None:
        keys, vals = k_att, v_att
    else:
        q_att = apply_rope_2d(q_att, rope)
        k_lat = apply_rope_2d(k_att, rope)
        keys = jnp.concatenate([ctx_cache[0].astype(h.dtype), k_lat], axis=1)
        vals = jnp.concatenate([ctx_cache[1].astype(h.dtype), v_att], axis=1)
    lq1, lk1, lq2, lk2 = p['attn_lambda'].astype(jnp.float32)
    lam = jnp.exp(jnp.sum(lq1 * lk1)) - jnp.exp(jnp.sum(lq2 * lk2)) + lam_init
    o_att = diff_attention(q_att, keys, vals, lam)
    o_att = rms_norm(o_att, p['attn_subln']) * (1.0 - lam_init)

    o = jnp.concatenate([o_rec.reshape(B, T, -1), o_att.reshape(B, T, -1)], axis=-1) @ p['w_mix_out']
    return o, k_att, v_att, jnp.stack([s_f, s_b], axis=1)


def layer(x, cond, p, lb, lam_init, ctx_cache=None, rope=None):
    m = (jax.nn.silu(cond) @ p['w_ada'] + p['b_ada']).reshape(cond.shape[0], N_MOD, D_MODEL)[:, :, None, :]
    sh1, sc1, g1, sh2, sc2, g2, sh3, sc3, g3 = [m[:, j] for j in range(N_MOD)]
    h = rms_norm(x, p['norm_ffn1']) * (1.0 + sc1) + sh1
    x = x + 0.5 * g1 * swiglu(h, p['w_ffn1_in'], p['w_ffn1_out'])
    h = rms_norm(x, p['norm_mix']) * (1.0 + sc2) + sh2
    mix, k_ctx, v_ctx, s_ctx = token_mix(h, p, lb, lam_init, ctx_cache, rope)
    x = x + g2 * mix
    h = rms_norm(x, p['norm_ffn2']) * (1.0 + sc3) + sh3
    x = x + 0.5 * g3 * swiglu(h, p['w_ffn2_in'], p['w_ffn2_out'])
    return x, k_ctx, v_ctx, s_ctx


def setup_inputs(seed: int = 0) -> dict:
    key = jax.random.key(seed)
    ks = jax.random.split(key, 24)
    f32 = jnp.float32

    def nrm(k, shape, s=1.0):
        return s * jax.random.normal(k, shape, f32)

    mix_in = 3 * H_A * DK_A + 2 * H_A * DV_A + 3 * H_B * 2 * D_HEAD_B
    mix_out = H_A * DV_A + H_B * 2 * D_HEAD_B
    return {
        "x_prompt": nrm(ks[0], (BATCH, SEQ, D_MODEL)),
        "x_sample": nrm(ks[1], (DEC_BATCH, DEC_SEQ, D_MODEL)),
        "c": nrm(ks[2], (DEC_BATCH, D_MODEL)),
        "cache_attn_k": nrm(ks[3], (DEC_BATCH, DEPTH, PAST_LEN, H_B, 2, D_HEAD_B)),
        "cache_attn_v": nrm(ks[4], (DEC_BATCH, DEPTH, PAST_LEN, H_B, 2 * D_HEAD_B), 0.5),
        "state_hgrn": nrm(ks[5], (DEC_BATCH, DEPTH, 2, H_A, DK_A, DV_A), 0.5),
        "c_ctx": nrm(ks[6], (D_MODEL,)),
        "w_ada": nrm(ks[7], (DEPTH, D_MODEL, N_MOD * D_MODEL), 0.5 * D_MODEL ** -0.5),
        "b_ada": nrm(ks[8], (DEPTH, N_MOD * D_MODEL), 0.02),
        "norm_ffn1": 1.0 + nrm(ks[9], (DEPTH, D_MODEL), 0.05),
        "w_ffn1_in": nrm(ks[10], (DEPTH, D_MODEL, 2 * D_FF), D_MODEL ** -0.5),
        "w_ffn1_out": nrm(ks[11], (DEPTH, D_FF, D_MODEL), D_FF ** -0.5),
        "norm_mix": 1.0 + nrm(ks[12], (DEPTH, D_MODEL), 0.05),
        "w_mix_in": nrm(ks[13], (DEPTH, D_MODEL, mix_in), D_MODEL ** -0.5),
        "w_mix_out": nrm(ks[14], (DEPTH, mix_out, D_MODEL), mix_out ** -0.5),
        "hgrn_lb_logits": nrm(ks[15], (DEPTH + 1, H_A * DK_A), 0.5),
        "hgrn_out_norm": 1.0 + nrm(ks[16], (DEPTH, DV_A), 0.05),
        "attn_q_norm": 1.0 + nrm(ks[17], (DEPTH, D_HEAD_B), 0.05),
        "attn_k_norm": 1.0 + nrm(ks[18], (DEPTH, D_HEAD_B), 0.05),
        "attn_lambda": nrm(ks[19], (DEPTH, 4, D_HEAD_B), 0.1),
        "attn_subln": 1.0 + nrm(ks[20], (DEPTH, 2 * D_HEAD_B), 0.05),
        "norm_ffn2": 1.0 + nrm(ks[21], (DEPTH, D_MODEL), 0.05),
        "w_ffn2_in": nrm(ks[22], (DEPTH, D_MODEL, 2 * D_FF), D_MODEL ** -0.5),
        "w_ffn2_out": nrm(ks[23], (DEPTH, D_FF, D_MODEL), D_FF ** -0.5),
    }


def reference(x_prompt, x_sample, c, cache_attn_k, cache_attn_v, state_hgrn, c_ctx,
              w_ada, b_ada, norm_ffn1, w_ffn1_in, w_ffn1_out, norm_mix, w_mix_in, w_mix_out,
              hgrn_lb_logits, hgrn_out_norm, attn_q_norm, attn_k_norm, attn_lambda, attn_subln,
              norm_ffn2, w_ffn2_in, w_ffn2_out):
    lb_all = jnp.cumsum(jax.nn.softmax(hgrn_lb_logits.astype(jnp.float32), axis=0), axis=0)
    rope = rope_tables(x_sample.shape[1])
    y_prompt = x_prompt
    y_sample = x_sample
    ks_list, vs_list, ss_list = [], [], []
    for l in range(DEPTH):
        p = {
            'w_ada': w_ada[l], 'b_ada': b_ada[l],
            'norm_ffn1': norm_ffn1[l], 'w_ffn1_in': w_ffn1_in[l], 'w_ffn1_out': w_ffn1_out[l],
            'norm_mix': norm_mix[l], 'w_mix_in': w_mix_in[l], 'w_mix_out': w_mix_out[l],
            'hgrn_out_norm': hgrn_out_norm[l], 'attn_q_norm': attn_q_norm[l],
            'attn_k_norm': attn_k_norm[l], 'attn_lambda': attn_lambda[l], 'attn_subln': attn_subln[l],
            'norm_ffn2': norm_ffn2[l], 'w_ffn2_in': w_ffn2_in[l], 'w_ffn2_out': w_ffn2_out[l],
        }
        lam_init = 0.8 - 0.6 * math.exp(-0.3 * l)
        y_prompt, k_l, v_l, s_l = layer(y_prompt, c_ctx[None], p, lb_all[l], lam_init)
        y_sample, _, _, _ = layer(y_sample, c, p, lb_all[l], lam_init,
                                  (cache_attn_k[:, l], cache_attn_v[:, l], state_hgrn[:, l]), rope)
        ks_list.append(k_l)
        vs_list.append(v_l)
        ss_list.append(s_l)
    new_attn_k = jnp.stack(ks_list, axis=1)
    new_attn_v = jnp.stack(vs_list, axis=1)
    new_state = jnp.stack(ss_list, axis=1)
    return (y_prompt, y_sample, new_attn_k, new_attn_v, new_state)
```

```python
import contextlib
import numpy as np
import concourse.bass as bass
import concourse.mybir as mybir
from concourse.bass_utils import run_bass_kernel_spmd

F32 = mybir.dt.float32
BF16 = mybir.dt.bfloat16
AF = mybir.ActivationFunctionType
ALU = mybir.AluOpType

NCORES = 8
D = 1024
T = 1280
NBLK = 5
DFF = 2816
NJ = DFF // 128
EPS = 1e-6
TT = [(0, 512, 0), (512, 512, 0), (1024, 256, 1)]
CH = 64
NCH = T // CH
LAM_INIT = 0.2
STAGE = 99
MIXL = 8


class Tr:
    def __init__(self, nc, stack):
        self.nc = nc
        self.stack = stack
        self.E = {'pe': nc.tensor, 'act': nc.scalar, 'dve': nc.vector, 'pool': nc.gpsimd, 'sp': nc.sync}
        self.sem = {k: stack.enter_context(nc.semaphore("sem_" + k)) for k in self.E}
        self.cnt = {k: 0 for k in self.E}
        self.lastw = {}
        self.rd = {}
        self.waited = {k: {} for k in self.E}
        self.dsem = {}

    def _wait(self, eng, dep, kind):
        if dep is None:
            return
        sem, val, src, name = dep
        if src == eng:
            if eng in ('pe', 'sp'):
                return
        if self.waited[eng].get(name, 0) >= val:
            return
        self.E[eng].wait_ge(sem, val)
        self.waited[eng][name] = val

    def _pre(self, eng, reads, writes):
        for r in reads:
            self._wait(eng, self.lastw.get(r), 'raw')
            if len(r) == 2 and r[0] == 'P':
                for d in self.rd.get(r, ()):
                    if d[2] != eng:
                        self._wait(eng, d, 'war')
        for w in writes:
            self._wait(eng, self.lastw.get(w), 'waw')
            for d in self.rd.get(w, ()):
                self._wait(eng, d, 'war')

    def _post(self, dep, reads, writes):
        for r in reads:
            self.rd.setdefault(r, []).append(dep)
        for w in writes:
            self.lastw[w] = dep
            self.rd[w] = []

    def op(self, eng, fn, reads=(), writes=()):
        self._pre(eng, reads, writes)
        ins = fn(self.E[eng])
        self.cnt[eng] += 1
        ins.then_inc(self.sem[eng], 1)
        self._post((self.sem[eng], self.cnt[eng], eng, eng), reads, writes)

    def dma(self, q, out, in_, semname, reads=(), writes=()):
        if semname not in self.dsem:
            self.dsem[semname] = [self.stack.enter_context(self.nc.semaphore("d_" + semname)), 0]
        ds = self.dsem[semname]
        self._pre(q, reads, writes)
        self.E[q].dma_start(out=out, in_=in_).then_inc(ds[0], 16)
        ds[1] += 16
        self._post((ds[0], ds[1], 'dma', "d_" + semname), reads, writes)

    def finish(self):
        for name, (sem, val) in self.dsem.items():
            if val:
                self.nc.sync.wait_ge(sem, val)


def build_program():
    nc = bass.Bass("TRN2", target_bir_lowering=False)

    def din(name, shape, dt=F32):
        return nc.dram_tensor(name, list(shape), dt, kind="ExternalInput").ap()

    def dout(name, shape, dt=F32):
        return nc.dram_tensor(name, list(shape), dt, kind="ExternalOutput").ap()

    xT_d = din("xT", [128, 8, T])
    condT_d = din("condT", [128, 8, 2])
    badaT_d = din("badaT", [128, 72])
    gT_d = din("gT", [128, 3, 8])
    wada_d = din("w_ada", [D, 9 * D])
    w1in_d = din("w1in", [NJ, 128, 8, 256])
    w1out_d = din("w1out", [8, 128, NJ, 128])
    w2in_d = din("w2in", [NJ, 128, 8, 256])
    w2out_d = din("w2out", [8, 128, NJ, 128])
    wmix_d = din("wmix", [14, 128, 8, 256])
    wav_d = din("wav", [128, 8, 512])
    wmo_d = din("wmo", [8, 128, 8, 128])
    lbl_d = din("lbl", [128, 2, 4])
    small_d = din("small", [128, 4])
    lamp_d = din("lamp", [128, 4, 64])
    subln_d = din("subln", [128, 128])
    ropeC_d = din("ropeC", [128, T])
    ropeS_d = din("ropeS", [128, T])
    perm_d = din("perm", [128, 128])
    blk64_d = din("blk64", [128, 128])
    ident_d = din("ident", [128, 128])
    tri_d = din("tri", [64, 2, 64])
    mres_d = din("mres", [128, T])
    maskb_d = din("maskb", [128, 48])
    flags_d = din("flags", [128, 10])
    kcT_d = din("kcT", [128, 4, 512])
    vc_d = din("vc", [128, 4, 4, 128])
    s0_d = din("s0", [128, 2, 4, 128])

    yT_d = dout("yT", [128, 8, T])
    kTo_d = dout("kTo", [128, 4, T])
    vo_d = dout("vo", [128, 10, 512])
    so_d = dout("so", [128, NBLK, 2, 4, 128])

    with contextlib.ExitStack() as st:
        tr = Tr(nc, st)

        def sb(name, shape, dt=F32):
            return st.enter_context(nc.sbuf_tensor(name, list(shape), dt))

        P = [st.enter_context(nc.psum_tensor("ps%d" % i, [128, 512], F32)) for i in range(8)]

        xT = sb("xT_s", [128, 8, T])
        hT = sb("hT_s", [128, 8, T], BF16)
        arena = sb("arena", [128, NJ * T], BF16)
        actT = arena[:, :].rearrange("p (j t) -> p j t", j=NJ)
        NWB = 3
        wbuf = [sb("wbuf%d" % i, [128, 8, 256], BF16) for i in range(NWB)]
        wobuf = [sb("wobuf%d" % i, [128, NJ, 128], BF16) for i in range(2)]
        condT = sb("condT_s", [128, 8, 2])
        scond = sb("scond", [128, 8, 2], BF16)
        badaT = sb("badaT_s", [128, 72])
        gT = sb("gT_s", [128, 3, 8])
        modT = sb("modT", [128, 72, 2])
        Amod = sb("Amod", [128, 3, 2, 8])
        Gmod = sb("Gmod", [128, 3, 2, 8])
        ones_bf = sb("ones_bf", [128, 128], BF16)
        scratch = sb("scratch", [128, 10240], BF16)
        sq = scratch[:, 0:4096].rearrange("p (c n) -> p c n", c=8)
        sd = scratch[:, 4096:5120].bitcast(F32)
        rstd = scratch[:, 5120:6144].bitcast(F32)
        tmpA = [scratch[:, 6144 + 1024 * i:7168 + 1024 * i].bitcast(F32) for i in range(2)]
        sg = [scratch[:, 8192 + 1024 * i:9216 + 1024 * i].bitcast(F32) for i in range(2)]

        tr.dma('sp', condT[:], condT_d, "in0", writes=["condT"])
        tr.dma('sp', badaT[:], badaT_d, "in0b", writes=["badaT"])
        tr.dma('sp', gT[:], gT_d, "in0c", writes=["gT"])
        for c in range(8):
            tr.dma('sp', xT[:, c, :], xT_d[:, c, :], "x%d" % c, writes=["xT.%d.%d" % (c, t) for t in range(3)])
        tr.op('dve', lambda e: e.memset(ones_bf[:], 1.0), writes=["ones"])
        epsc0 = sb("epsc0", [128, 1])
        tr.op('dve', lambda e: e.memset(epsc0[:], EPS), writes=["epsc0"])
        tr.op('act', lambda e: e.activation(out=scond[:], in_=condT[:], func=AF.Silu),
              reads=["condT"], writes=["scond"])

        RS = [rstd, sg[0], sg[1]]
        RSN = ["rstd", "sg0", "sg1"]

        def norm_stats():
            for ti, (t0, N, ci) in enumerate(TT):
                xr = ["xT.%d.%d" % (c, ti) for c in range(8)]
                tr.op('dve', lambda e: e.tensor_tensor(out=sq[:, 0:4, 0:N], in0=xT[:, 0:4, t0:t0 + N],
                                                       in1=xT[:, 0:4, t0:t0 + N], op=ALU.mult),
                      reads=xr[0:4], writes=["sq.a"])
                tr.op('act', lambda e: e.activation(out=sq[:, 4:8, 0:N], in_=xT[:, 4:8, t0:t0 + N], func=AF.Square),
                      reads=xr[4:8], writes=["sq.b"])

                def f(e):
                    ins = None
                    for c in range(8):
                        ins = e.matmul(P[4][:, 0:N], ones_bf[:], sq[:, c, 0:N], start=(c == 0), stop=(c == 7))
                    return ins
                tr.op('pe', f, reads=["sq.a", "sq.b", "ones"], writes=["P4"])
                tr.op('act', lambda e: e.activation(out=sd[:, 0:N], in_=P[4][:, 0:N], func=AF.Ln,
                                                    bias=epsc0[:, 0:1], scale=1.0 / D), reads=["P4", "epsc0"], writes=["sd"])
                tr.op('act', lambda e: e.activation(out=RS[ti][:, 0:N], in_=sd[:, 0:N], func=AF.Exp, scale=-0.5),
                      reads=["sd"], writes=[RSN[ti]])

        norm_stats()

        wada_v = wada_d.rearrange("(kc p) n -> p kc n", p=128)
        wada_buf = [arena[:, i * 9216:(i + 1) * 9216].rearrange("p (k n) -> p k n", k=8) for i in range(2)]
        psm = P[7][:, 0:144].rearrange("p (c i) -> p c i", i=2)
        for s in range(3):
            bi = s % 2
            tr.dma('pool', wada_buf[bi], wada_v[:, :, s * 1152:(s + 1) * 1152], "wada%d" % bi,
                   writes=["wada%d" % bi])

            def f(e, s=s, bi=bi):
                ins = None
                for cc in range(9):
                    ch = s * 9 + cc
                    for kc in range(8):
                        ins = e.matmul(psm[:, ch, :], wada_buf[bi][:, kc, cc * 128:(cc + 1) * 128],
                                       scond[:, kc, :], start=(kc == 0), stop=(kc == 7))
                return ins
            tr.op('pe', f, reads=["wada%d" % bi, "scond"], writes=["P7"])
        def mods_finalize(n):
            lo, hi = 24 * n, 24 * (n + 1)
            for i in range(2):
                tr.op('dve', lambda e, i=i: e.tensor_tensor(out=modT[:, lo:hi, i], in0=psm[:, lo:hi, i], in1=badaT[:, lo:hi],
                                                           op=ALU.add),
                      reads=["P7", "badaT"], writes=["modT"])
            for i in range(2):
                tr.op('dve', lambda e, i=i: e.scalar_tensor_tensor(
                    out=Amod[:, n, i, :], in0=modT[:, (3 * n + 1) * 8:(3 * n + 2) * 8, i], scalar=1.0,
                    in1=gT[:, n, :], op0=ALU.add, op1=ALU.mult), reads=["modT", "gT"], writes=["Amod"])
                tr.op('dve', lambda e, i=i: e.tensor_scalar(
                    out=Gmod[:, n, i, :], in0=modT[:, (3 * n + 2) * 8:(3 * n + 3) * 8, i],
                    scalar1=(1.0 if n == 1 else 0.5), scalar2=None, op0=ALU.mult),
                    reads=["modT"], writes=["Gmod"])

        mods_finalize(0)

        ADA_B0 = 3456
        ADA_NSLAB = (9 * D - ADA_B0 + 255) // 256

        def ada_slab(sbi):
            c0 = ADA_B0 + 256 * sbi
            ncol = min(256, 9 * D - c0)
            bi = cnt['wo'] % 2
            cnt['wo'] += 1
            wob = wobuf[bi][:, :, :].rearrange("p a b -> p (a b)")[:, 0:2048].rearrange("p (k n) -> p k n", k=8)
            tr.dma('pool', wob[:, :, 0:ncol], wada_v[:, :, c0:c0 + ncol], "wo%d" % bi, writes=["wo%d" % bi])

            def f(e):
                ins = None
                for cc in range(ncol // 128):
                    ch = c0 // 128 + cc
                    for kc in range(8):
                        ins = e.matmul(psm[:, ch, :], wob[:, kc, cc * 128:(cc + 1) * 128], scond[:, kc, :],
                                       start=(kc == 0), stop=(kc == 7))
                return ins
            tr.op('pe', f, reads=["wo%d" % bi, "scond"], writes=["P7"])

        def Bmod(n, i, c):
            return modT[:, 3 * n * 8 + c, i:i + 1]

        cnt = {'io': 0, 'oo': 0, 'tmp': 0, 'wb': 0, 'wo': 0}

        def norm_apply(n):
            for ti, (t0, N, ci) in enumerate(TT):
                for c in range(8):
                    k = cnt['tmp'] % 2
                    cnt['tmp'] += 1
                    tr.op('dve', lambda e, c=c, k=k: e.scalar_tensor_tensor(
                        out=tmpA[k][:, 0:N], in0=xT[:, c, t0:t0 + N], scalar=Amod[:, n, ci, c:c + 1],
                        in1=RS[ti][:, 0:N], op0=ALU.mult, op1=ALU.mult),
                        reads=["xT.%d.%d" % (c, ti), "Amod", RSN[ti]], writes=["tmpA%d" % k])
                    tr.op('act', lambda e, c=c, k=k: e.activation(
                        out=hT[:, c, t0:t0 + N], in_=tmpA[k][:, 0:N], func=AF.Identity, bias=Bmod(n, ci, c)),
                        reads=["tmpA%d" % k, "modT"], writes=["hT.%d.%d" % (ti, c)])

        def norm_mod(n):
            if n > 0:
                norm_stats()
            norm_apply(n)

        def ffn(n, win_d, wout_d, hook=None):
            pre = {}

            def issue(j):
                bi = cnt['wb'] % NWB
                cnt['wb'] += 1
                tr.dma('pool', wbuf[bi][:], win_d[j], "wb%d" % bi, writes=["wb%d" % bi])
                return bi
            for j in range(NWB - 1):
                pre[j] = issue(j)
            norm_mod(n)
            for j in range(NJ):
                bi = pre[j] if j in pre else issue(j)
                for ti, (t0, N, ci) in enumerate(TT):
                    k = cnt['io'] % 2
                    cnt['io'] += 1
                    pg, pu = P[2 * k], P[2 * k + 1]

                    def f(e, pg=pg, pu=pu, bi=bi):
                        ins = None
                        for kc in range(8):
                            e.matmul(pg[:, 0:N], wbuf[bi][:, kc, 0:128], hT[:, kc, t0:t0 + N],
                                     start=(kc == 0), stop=(kc == 7))
                        for kc in range(8):
                            ins = e.matmul(pu[:, 0:N], wbuf[bi][:, kc, 128:256], hT[:, kc, t0:t0 + N],
                                           start=(kc == 0), stop=(kc == 7))
                        return ins
                    tr.op('pe', f, reads=["wb%d" % bi] + ["hT.%d.%d" % (ti, c_) for c_ in range(8)], writes=["P%d" % (2 * k), "P%d" % (2 * k + 1)])
                    tr.op('act', lambda e, pg=pg, k=k: e.activation(out=sg[k][:, 0:N], in_=pg[:, 0:N], func=AF.Silu),
                          reads=["P%d" % (2 * k)], writes=["sg%d" % k])
                    tr.op('dve', lambda e, pu=pu, k=k, j=j: e.tensor_tensor(
                        out=actT[:, j, t0:t0 + N], in0=sg[k][:, 0:N], in1=pu[:, 0:N], op=ALU.mult),
                        reads=["sg%d" % k, "P%d" % (2 * k + 1)], writes=["actT.%d" % ti])
                if hook is not None:
                    hook(j)
            for o in range(8):
                bi = cnt['wo'] % 2
                cnt['wo'] += 1
                tr.dma('pool', wobuf[bi][:], wout_d[o], "wo%d" % bi, writes=["wo%d" % bi])
                for ti, (t0, N, ci) in enumerate(TT):
                    k = 5 + cnt['oo'] % 2
                    cnt['oo'] += 1

                    def f(e, k=k, bi=bi, o=o):
                        ins = None
                        for kc in range(NJ):
                            ins = e.matmul(P[k][:, 0:N], wobuf[bi][:, kc, :], actT[:, kc, t0:t0 + N],
                                           start=(kc == 0), stop=(kc == NJ - 1))
                        return ins
                    tr.op('pe', f, reads=["wo%d" % bi, "actT.%d" % ti], writes=["P%d" % k])
                    tr.op('dve', lambda e, k=k, o=o: e.scalar_tensor_tensor(
                        out=xT[:, o, t0:t0 + N], in0=P[k][:, 0:N], scalar=Gmod[:, n, ci, o:o + 1],
                        in1=xT[:, o, t0:t0 + N], op0=ALU.mult, op1=ALU.add),
                        reads=["P%d" % k, "Gmod", "xT.%d.%d" % (o, ti)], writes=["xT.%d.%d" % (o, ti)])
                if n == 2:
                    tr.dma('sp', yT_d[:, o, :], xT[:, o, :], "out_y", reads=["xT.%d.%d" % (o, t) for t in range(3)])


        def barrier():
            for e in ('pe', 'act', 'dve'):
                for o in ('pe', 'act', 'dve'):
                    if o != e and tr.cnt[o] > tr.waited[e].get(o, 0):
                        tr.E[e].wait_ge(tr.sem[o], tr.cnt[o])
                        tr.waited[e][o] = tr.cnt[o]
                for name, (sem, val) in tr.dsem.items():
                    if name.startswith("out_") and val > tr.waited[e].get("d_" + name, 0):
                        tr.E[e].wait_ge(sem, val)
                        tr.waited[e]["d_" + name] = val

        ropeC = sb("ropeC_s", [128, 1024])
        ropeS = sb("ropeS_s", [128, 1024])
        mres = sb("mres_s", [128, T], BF16)
        perm_bf = sb("perm_bf", [128, 128], BF16)
        blk_bf = sb("blk_bf", [128, 128], BF16)
        ident_bf = sb("ident_bf", [128, 128], BF16)
        tri = sb("tri_s", [64, 2, 64])
        maskb = sb("maskb_s", [128, 48])
        flags = sb("flags_s", [128, 10])
        s0in = sb("s0in", [128, 2, 4, 128])
        lbl = sb("lbl_s", [128, 2, 4])
        small = sb("small_s", [128, 4])
        lamp = sb("lamp_s", [128, 4, 64])
        subln = sb("subln_s", [128, 128])
        lb = sb("lb", [128, 4])
        oml = sb("oml", [128, 4])
        lamt = sb("lamt", [128, 8])
        lamj = sb("lamj", [128, 64])
        oF = sb("oF", [128, T])
        mt = [sb("mt%d" % i, [128, 512]) for i in range(4)]
        mtb = [sb("mtb%d" % i, [128, 512], BF16) for i in range(2)]
        Sfm = sb("Sfm", [128, 2, 2, 128])
        Sbm = sb("Sbm", [128, 2, 2, 128], BF16)
        Asm = sb("Asm", [64, 2, 2, 64], BF16)
        dec = [sb("dec%d" % d, [128, NCH]) for d in range(2)]
        bend = sb("bend", [128, NCH])
        om = [[sb("om%d%d" % (m, q), [128, 128]) for q in range(2)] for m in range(2)]
        odt = [sb("od%d" % q, [128, 128]) for q in range(2)]
        onb = [sb("on%d" % q, [128, 128], BF16) for q in range(2)]
        sm = [sb("sm%d" % i, [128, 4]) for i in range(2)]
        sqj = sb("sqj", [128, 128])
        epsc = sb("epsc", [128, 1])
        mhalf = sb("mhalf", [128, 512])

        def load_mixer_consts():
            tr.op('dve', lambda e: e.memset(epsc[:], EPS), writes=["epsc"])
            tr.op('dve', lambda e: e.memset(mhalf[:], -0.5), writes=["mhalf"])
            for (dst, src_, nm) in [(ropeC[:], ropeC_d[:, 0:1024], "ropeC"), (ropeS[:], ropeS_d[:, 0:1024], "ropeS"),
                                    (tri[:], tri_d, "tri"), (maskb[:], maskb_d, "maskb"), (flags[:], flags_d, "flags"),
                                    (s0in[:], s0_d, "s0in"), (lbl[:], lbl_d, "lbl"), (small[:], small_d, "small"),
                                    (lamp[:], lamp_d, "lamp"), (subln[:], subln_d, "subln")]:
                tr.dma('sp', dst, src_, "c_" + nm, writes=[nm])
            for (dst, src_, nm) in [(mres[:], mres_d, "mres"), (perm_bf[:], perm_d, "perm"),
                                    (blk_bf[:], blk64_d, "blk"), (ident_bf[:], ident_d, "ident")]:
                tr.dma('pool', dst, src_, "c_" + nm, writes=[nm])
            tr.op('dve', lambda e: e.tensor_tensor(out=lb[:], in0=lbl[:, 0, :], in1=lbl[:, 1, :], op=ALU.subtract),
                  reads=["lbl"], writes=["lb"])
            tr.op('act', lambda e: e.activation(out=lb[:], in_=lb[:], func=AF.Sigmoid), reads=["lb"], writes=["lb"])
            tr.op('dve', lambda e: e.tensor_scalar(out=oml[:], in0=lb[:], scalar1=-1.0, scalar2=1.0,
                                                   op0=ALU.mult, op1=ALU.add), reads=["lb"], writes=["oml"])
            for i in range(2):
                tr.op('dve', lambda e, i=i: e.tensor_tensor(out=lamj[:], in0=lamp[:, 2 * i, :], in1=lamp[:, 2 * i + 1, :],
                                                           op=ALU.mult), reads=["lamp"], writes=["lamj"])
                tr.op('act', lambda e, i=i: e.activation(out=sqj[:, 0:64], in_=lamj[:], func=AF.Identity,
                                                        accum_out=lamt[:, i:i + 1]), reads=["lamj"], writes=["lamt", "sqj"])
            tr.op('act', lambda e: e.activation(out=lamt[:, 4:6], in_=lamt[:, 0:2], func=AF.Exp), reads=["lamt"], writes=["lamt"])
            tr.op('dve', lambda e: e.tensor_tensor(out=lamt[:, 6:7], in0=lamt[:, 5:6], in1=lamt[:, 4:5], op=ALU.subtract),
                  reads=["lamt"], writes=["lamt"])
            tr.op('dve', lambda e: e.tensor_scalar(out=lamt[:, 2:3], in0=lamt[:, 6:7], scalar1=-LAM_INIT, scalar2=None,
                                                   op0=ALU.add), reads=["lamt"], writes=["lamt"])
            tr.op('dve', lambda e: e.tensor_scalar(out=subln[:], in0=subln[:], scalar1=1.0 - LAM_INIT, scalar2=None,
                                                   op0=ALU.mult), reads=["subln"], writes=["subln"])

        def mixer():
            barrier()
            norm_mod(1)
            barrier()
            if MIXL < 1:
                return
            FFN_SCR = ["sq.a", "sq.b", "sd", "rstd", "tmpA0", "tmpA1", "sg0", "sg1", "F2", "BB2", "EE2", "KK2"]

            def gran(g, n=1):
                return arena[:, g * 1280:(g + n) * 1280]

            def gn(g, n=1):
                return ["g%d" % i for i in range(g, g + n)]
            oT = arena[:, 0:8 * 1280].rearrange("p (c t) -> p c t", c=8)
            Fv, BBv, EEv = gran(8, 2).bitcast(F32), gran(10, 2).bitcast(F32), gran(12, 2).bitcast(F32)
            Qd = [gran(14), gran(15)]
            F2v, BB2v, EE2v = scratch[:, 0:2560].bitcast(F32), scratch[:, 2560:5120].bitcast(F32), scratch[:, 5120:7680].bitcast(F32)
            KK2 = scratch[:, 7680:8960]
            Ktok = [arena[0:64, (16 + 2 * d) * 1280:(18 + 2 * d) * 1280].rearrange("p (a b) -> p a b", a=NCH)
                    for d in range(2)]
            Vtok = arena[0:64, 20 * 1280:22 * 1280].rearrange("p (a b) -> p a b", a=NCH)
            wo0 = wobuf[0][:, :, :].rearrange("p a b -> p (a b)")
            wo1 = wobuf[1][:, :, :].rearrange("p a b -> p (a b)")
            QT = wo0[:, 0:2560].bitcast(F32)
            GS = wo1[:, 0:1280]
            KK = wo1[:, 1280:2560]
            Vaug = scratch[:, 0:7280].rearrange("p (k h e) -> p k h e", k=14, h=4)
            kcT = scratch[:, 7280:7280 + 2048].rearrange("p (h k) -> p h k", h=4)
            AQ, AK = Fv, BBv
            qrs, krs = [gran(12), gran(20)], [gran(13), gran(21)]
            qrns, krns = [gn(12), gn(20)], [gn(13), gn(21)]
            ET = [arena[:, 14 * 1280 + i * 3072: 14 * 1280 + (i + 1) * 3072].rearrange("p (k q) -> p k q", k=12)
                  for i in range(2)]
            ETn = [["ET0"], ["ET1"]]
            ptb = P[7][:, :].bitcast(BF16)
            pc = {'p': 0, 'mt': 0, 'a': 0, 'sp': 0, 'po': 0, 'wo': 0}
            gp = {'banks': [0, 1, 2, 3]}

            def next_bank():
                banks = gp['banks']
                if gp.get('by_ctx') is not None:
                    banks = gp['by_ctx'][gp['ctx']]
                k = banks[pc['p'] % len(banks)]
                pc['p'] += 1
                return k

            def next_mt():
                k = pc['mt'] % 4
                pc['mt'] += 1
                return k

            def load_slab(src_ap):
                bi = cnt['wb'] % NWB
                cnt['wb'] += 1
                tr.dma('pool', wbuf[bi][:], src_ap, "wb%d" % bi, writes=["wb%d" % bi])
                return bi

            def vproj_and_caches():
                import os
                DBG = os.environ.get("KDBG", "")
                if "a" not in DBG:
                    tr.dma('pool', kcT, kcT_d, "kc", writes=["kcT"] + FFN_SCR)
                if "b" not in DBG:
                    tr.dma('pool', Vaug[:, 0:4, :, 0:128], vc_d, "vc", writes=["Vaug.c"] + FFN_SCR)
                if "c" not in DBG:
                    tr.op('dve', lambda e: e.memset(Vaug[:, :, :, 128:130], 1.0), writes=["Vaug.1"] + FFN_SCR)

                wav_v = wav_d.rearrange("p k (s n) -> s p k n", s=2)
                for hh in range(0 if "d" in DBG else 2):
                    bi = load_slab(wav_v[hh])
                    for ti in range(10):
                        k = 5 + pc['po'] % 2
                        pc['po'] += 1

                        def f(e, k=k, bi=bi, ti=ti):
                            ins = None
                            for kc in range(8):
                                ins = e.matmul(P[k][:, 0:256], hT[:, kc, ti * 128:(ti + 1) * 128], wbuf[bi][:, kc, :],
                                               start=(kc == 0), stop=(kc == 7))
                            return ins
                        tr.op('pe', f, reads=["wb%d" % bi] + ["hT.%d.%d" % (min(ti // 4, 2), c_) for c_ in range(8)], writes=["P%d" % k])
                        m_ = next_mt()
                        tr.op('act', lambda e, k=k, m_=m_: e.copy(out=mt[m_][:, 0:256], in_=P[k][:, 0:256]),
                              reads=["P%d" % k], writes=["mt%d" % m_])
                        tr.dma('sp', vo_d[:, ti, hh * 256:(hh + 1) * 256], mt[m_][:, 0:256], "out_v%d" % m_, reads=["mt%d" % m_])
                        tr.op('dve', lambda e, k=k, ti=ti, hh=hh: e.tensor_copy(
                            out=Vaug[:, 4 + ti, 2 * hh:2 * hh + 2, 0:128],
                            in_=P[k][:, 0:256].rearrange("p (h e) -> p h e", h=2)),
                            reads=["P%d" % k], writes=["Vaug.%d" % ti] + FFN_SCR)
                    yield


            def proj(ci, evac):
                if getattr(proj, "cur", None) != ci // 2:
                    proj.bi = load_slab(wmix_d[ci // 2])
                    proj.cur = ci // 2
                bi, half = proj.bi, ci % 2
                for ti, (t0, N, _) in enumerate(TT):
                    k = next_bank()

                    def f(e, k=k, bi=bi):
                        ins = None
                        for kc in range(8):
                            ins = e.matmul(P[k][:, 0:N], wbuf[bi][:, kc, half * 128:(half + 1) * 128],
                                           hT[:, kc, t0:t0 + N], start=(kc == 0), stop=(kc == 7))
                        return ins
                    tr.op('pe', f, reads=["wb%d" % bi] + ["hT.%d.%d" % (ti, c_) for c_ in range(8)], writes=["P%d" % k])
                    evac(P[k], "P%d" % k, t0, N, ti)
                    yield

            def transposes(srcT, src_names, dst, dst_names):
                for g0 in range(0, NCH, 8):
                    n = min(8, NCH - g0)
                    kt = next_bank()
                    ptk = P[kt][:, :].bitcast(BF16)

                    def f(e, g0=g0, n=n, ptk=ptk):
                        ins = None
                        for c in range(n):
                            ins = e.transpose(out=ptk[0:64, c * 128:(c + 1) * 128],
                                              in_=srcT[:, (g0 + c) * CH:(g0 + c + 1) * CH], identity=ident_bf[:])
                        return ins
                    tr.op('pe', f, reads=src_names + ["ident"], writes=["P%d" % kt])
                    tr.op('act', lambda e, g0=g0, n=n, ptk=ptk: e.copy(
                        out=dst[:, g0:g0 + n, :], in_=ptk[0:64, 0:n * 128].rearrange("p (a b) -> p a b", a=n)),
                        reads=["P%d" % kt], writes=dst_names)

            def prep(d, h):
                Fx, BBx, EEx, KKx = (Fv, BBv, EEv, KK) if d == 0 else (F2v, BB2v, EE2v, KK2)
                Fn, BBn, EEn, KKn = (gn(8, 2), gn(10, 2), gn(12, 2), ["KK"]) if d == 0 else (["F2"], ["BB2"], ["EE2"], ["KK2"])
                KH = gran(12) if d == 0 else scratch[:, 5120:6400]
                tr.op('dve', lambda e: e.tensor_scalar(out=KKx, in0=Fx, scalar1=-1.0, scalar2=1.0, op0=ALU.mult, op1=ALU.add),
                      reads=Fn, writes=KKn)
                yield
                tr.op('act', lambda e: e.activation(out=Fx, in_=Fx, func=AF.Ln), reads=Fn, writes=Fn)
                yield
                tr.op('dve', lambda e: e.tensor_tensor_scan(out=BBx, data0=mres[:], data1=Fx, initial=0.0,
                                                            op0=ALU.mult, op1=ALU.add),
                      reads=Fn + ["mres"], writes=BBn)
                yield
                if d == 1:
                    tr.op('dve', lambda e: e.tensor_copy(out=bend[:], in_=BBx[:, CH - 1::CH]), reads=BBn, writes=["bend"])
                    tr.op('dve', lambda e: e.tensor_tensor(out=Fx, in0=Fx, in1=BBx, op=ALU.subtract),
                          reads=Fn + BBn, writes=Fn)
                    yield
                    tr.op('dve', lambda e: e.tensor_tensor(
                        out=BBx.rearrange("p (a b) -> p a b", a=NCH), in0=Fx.rearrange("p (a b) -> p a b", a=NCH),
                        in1=bend[:].unsqueeze(2).to_broadcast([128, NCH, CH]), op=ALU.add),
                        reads=Fn + ["bend"], writes=BBn)
                    yield
                tr.op('act', lambda e: e.activation(out=EEx, in_=BBx, func=AF.Exp), reads=BBn, writes=EEn)
                yield
                col = (CH - 1) if d == 0 else 0
                tr.op('dve', lambda e: e.tensor_copy(out=dec[d][:], in_=EEx[:, col::CH]), reads=EEn, writes=["dec%d" % d])
                tr.op('pool', lambda e: e.tensor_tensor(out=Qd[d], in0=QT, in1=EEx, op=ALU.mult),
                      reads=EEn + ["QT"], writes=gn(14 + d))
                yield
                tr.op('act', lambda e: e.activation(out=EEx, in_=BBx, func=AF.Exp, scale=-1.0), reads=BBn, writes=EEn)
                yield
                tr.op('dve', lambda e: e.tensor_tensor(out=KTf[d], in0=KKx, in1=EEx, op=ALU.mult),
                      reads=EEn + KKn, writes=["KTf%d" % d])
                yield
                tr.op('pool', lambda e: e.tensor_tensor(
                    out=KH.rearrange("p (a b) -> p a b", a=NCH), in0=KTf[d].rearrange("p (a b) -> p a b", a=NCH),
                    in1=dec[d][:].unsqueeze(2).to_broadcast([128, NCH, CH]), op=ALU.mult),
                    reads=["KTf%d" % d, "dec%d" % d], writes=EEn)
                yield
                transposes(KH, EEn, Ktok[d], gn(16 + 2 * d, 2))

            KTf = [mtbig[:, 0:1280], mtbig[:, 1280:2560]]

            def chains(h):
                SfT = lambda buf, d: Sfm[:, buf, d, :]
                SbT = lambda buf, d: Sbm[:, buf, d, :]
                tr.op('dve', lambda e: e.tensor_copy(out=SfT(0, 0), in_=s0in[:, 0, h, :]), reads=["s0in"], writes=["Sf00"])
                tr.op('dve', lambda e: e.memset(SfT(0, 1), 0.0), writes=["Sf10"])
                tr.op('act', lambda e: e.copy(out=Sbm[:, 0, :, :], in_=Sfm[:, 0, :, :]), reads=["Sf00", "Sf10"], writes=["Sb00", "Sb10"])

                def cidx(s, d):
                    return s if d == 0 else NCH - 1 - s

                def stage1(s):
                    ks, kS = 4 + s % 2, 6 + s % 2

                    def f(e):
                        ins = None
                        for d in range(2):
                            c = cidx(s, d)
                            cs = slice(c * CH, (c + 1) * CH)
                            e.matmul(P[ks][0:64, d * 64:(d + 1) * 64], KTf[d][:, cs], Qd[d][:, cs], start=True, stop=True)
                            ins = e.matmul(P[kS][:, d * 128:(d + 1) * 128], Ktok[d][:, c, :], Vtok[:, c, :], start=True, stop=True)
                        return ins
                    tr.op('pe', f, reads=["KTf0", "KTf1"] + gn(14, 8), writes=["P%d" % ks, "P%d" % kS])

                def stage2(s):
                    ks, kS = 4 + s % 2, 6 + s % 2
                    a, b = s % 2, 1 - s % 2
                    tr.op('dve', lambda e: e.tensor_tensor(
                        out=Asm[:, s % 2, :, :], in0=P[ks][0:64, 0:128].rearrange("p (d t) -> p d t", d=2), in1=tri[:, :, :], op=ALU.mult),
                        reads=["P%d" % ks, "tri"], writes=["A%d" % (s % 2)])
                    for d in range(2):
                        c = cidx(s, d)
                        blk = c // 4
                        sfn, sbn = "Sf%d%d" % (d, a), "Sb%d%d" % (d, a)
                        enter = (d == 0 and c % 4 == 0 and c > 0) or (d == 1 and c % 4 == 3 and c < NCH - 1)
                        if enter:
                            if d == 0 and blk == 4:
                                tr.op('dve', lambda e, d=d: e.memset(SfT(a, d), 0.0), writes=[sfn])
                                tr.op('dve', lambda e, d=d: e.memset(SbT(a, d), 0.0), writes=[sbn])
                            elif d == 1 and blk == 3:
                                tr.op('dve', lambda e, d=d: e.tensor_copy(out=SfT(a, d), in_=s0in[:, 1, h, :]), reads=["s0in"], writes=[sfn])
                                tr.op('act', lambda e, d=d: e.copy(out=SbT(a, d), in_=s0in[:, 1, h, :]), reads=["s0in"], writes=[sbn])
                            else:
                                fl = flags[:, blk:blk + 1] if d == 0 else flags[:, 5 + blk:6 + blk]
                                tr.op('dve', lambda e, d=d, fl=fl: e.tensor_scalar(out=SfT(a, d), in0=SfT(a, d), scalar1=fl, scalar2=None, op0=ALU.mult),
                                      reads=[sfn, "flags"], writes=[sfn])
                                tr.op('dve', lambda e, d=d, fl=fl: e.tensor_scalar(out=SbT(a, d), in0=SbT(a, d), scalar1=fl, scalar2=None, op0=ALU.mult),
                                      reads=[sbn, "flags"], writes=[sbn])
                        tr.op('dve', lambda e, d=d, c=c: e.scalar_tensor_tensor(
                            out=SfT(b, d), in0=SfT(a, d), scalar=dec[d][:, c:c + 1], in1=P[kS][:, d * 128:(d + 1) * 128],
                            op0=ALU.mult, op1=ALU.add),
                            reads=["P%d" % kS, sfn, "dec%d" % d], writes=["Sf%d%d" % (d, b)])
                    tr.op('act', lambda e: e.copy(out=Sbm[:, b, :, :], in_=Sfm[:, b, :, :]),
                          reads=["Sf0%d" % b, "Sf1%d" % b], writes=["Sb0%d" % b, "Sb1%d" % b])
                    for d in range(2):
                        c = cidx(s, d)
                        if (d == 0 and c % 4 == 3) or (d == 1 and c % 4 == 0):
                            tr.dma('sp', so_d[:, c // 4, d, h, :], SfT(b, d), "out_s%d%d" % (d, b), reads=["Sf%d%d" % (d, b)])

                def stage3(s):
                    a = s % 2
                    for d in range(2):
                        c = cidx(s, d)
                        cs = slice(c * CH, (c + 1) * CH)
                        ti = min(c // 8, 2)
                        pk = d
                        po = P[pk][:, (c % 8) * CH:(c % 8 + 1) * CH]

                        def fo(e, d=d, c=c, po=po, cs=cs):
                            e.matmul(po, Vtok[:, c, :], Asm[:, s % 2, d, :], start=True, stop=False)
                            return e.matmul(po, SbT(a, d), Qd[d][:, cs], start=False, stop=True)
                        tr.op('pe', fo, reads=["A%d" % (s % 2), "Sb%d%d" % (d, a)] + gn(20, 2) + gn(14 + d), writes=["P%d" % pk])
                        if (d == 0 and c in (7, 15, 19)) or (d == 1 and c in (16, 8, 0)):
                            t0, N, _ = TT[ti]
                            first = (d == 0) if ti == 0 else (d == 1)
                            if first:
                                tr.op('act', lambda e, pk=pk, t0=t0, N=N: e.copy(out=oF[:, t0:t0 + N], in_=P[pk][:, 0:N]),
                                      reads=["P%d" % pk], writes=["oF.%d" % ti])
                            else:
                                tr.op('dve', lambda e, pk=pk, t0=t0, N=N: e.tensor_tensor(
                                    out=oF[:, t0:t0 + N], in0=P[pk][:, 0:N], in1=oF[:, t0:t0 + N], op=ALU.add),
                                    reads=["P%d" % pk, "oF.%d" % ti], writes=["oF.%d" % ti])

                stage1(0)
                for s in range(NCH):
                    if s + 1 < NCH:
                        stage1(s + 1)
                    stage2(s)
                    stage3(s)
                    yield

            def group_norm(src, src_names, ti, t0, N, ones_mat, ones_name, inv_n, pbank):
                pbank = next_bank()
                tr.op('pool', lambda e: e.tensor_tensor(out=mtb[0][:, 0:N], in0=src[:, t0:t0 + N], in1=src[:, t0:t0 + N], op=ALU.mult),
                      reads=src_names, writes=["mtb0"])
                tr.op('pe', lambda e: e.matmul(P[pbank][:, 0:N], ones_mat, mtb[0][:, 0:N], start=True, stop=True),
                      reads=["mtb0", ones_name], writes=["P%d" % pbank])
                m1 = next_mt()
                if False and ones_name == "blk":
                    tr.op('dve', lambda e: e.tensor_scalar(out=mt[m1][:, 0:N], in0=P[pbank][:, 0:N], scalar1=inv_n, scalar2=EPS,
                                                           op0=ALU.mult, op1=ALU.add),
                          reads=["P%d" % pbank], writes=["mt%d" % m1])
                    tr.op('pool', lambda e: e.tensor_tensor(out=mt[m1][:, 0:N], in0=mt[m1][:, 0:N], in1=mhalf[:, 0:N], op=ALU.pow),
                          reads=["mt%d" % m1, "mhalf"], writes=["mt%d" % m1])
                    return m1
                tr.op('act', lambda e: e.activation(out=mt[m1][:, 0:N], in_=P[pbank][:, 0:N], func=AF.Ln, bias=epsc[:, 0:1], scale=inv_n),
                      reads=["P%d" % pbank, "epsc"], writes=["mt%d" % m1])
                tr.op('act', lambda e: e.activation(out=mt[m1][:, 0:N], in_=mt[m1][:, 0:N], func=AF.Exp, scale=-0.5),
                      reads=["mt%d" % m1], writes=["mt%d" % m1])
                return m1

            def group_norm_g(src, src_names, ti, t0, N, ones_mat, ones_name, inv_n, out):
                pbank = next_bank()
                tr.op('pool', lambda e: e.tensor_tensor(out=mtb[0][:, 0:N], in0=src[:, t0:t0 + N], in1=src[:, t0:t0 + N], op=ALU.mult),
                      reads=src_names, writes=["mtb0"])
                yield
                tr.op('pe', lambda e: e.matmul(P[pbank][:, 0:N], ones_mat, mtb[0][:, 0:N], start=True, stop=True),
                      reads=["mtb0", ones_name], writes=["P%d" % pbank])
                yield
                m1 = next_mt()
                tr.op('act', lambda e: e.activation(out=mt[m1][:, 0:N], in_=P[pbank][:, 0:N], func=AF.Ln, bias=epsc[:, 0:1], scale=inv_n),
                      reads=["P%d" % pbank, "epsc"], writes=["mt%d" % m1])
                tr.op('act', lambda e: e.activation(out=mt[m1][:, 0:N], in_=mt[m1][:, 0:N], func=AF.Exp, scale=-0.5),
                      reads=["mt%d" % m1], writes=["mt%d" % m1])
                out.append(m1)
                yield

            def orec_final(h):
                for ti, (t0, N, _) in enumerate(TT):
                    m1 = group_norm(oF, ["oF.%d" % ti], ti, t0, N, ones_bf[:], "ones", 1.0 / 128, 4)
                    m2 = next_mt()
                    tr.op('dve', lambda e: e.scalar_tensor_tensor(out=mt[m2][:, 0:N], in0=oF[:, t0:t0 + N], scalar=small[:, 0:1],
                                                                  in1=mt[m1][:, 0:N], op0=ALU.mult, op1=ALU.mult),
                          reads=["oF.%d" % ti, "small", "mt%d" % m1], writes=["mt%d" % m2])
                    tr.op('pool', lambda e: e.tensor_tensor(out=oT[:, h, t0:t0 + N], in0=mt[m2][:, 0:N], in1=GS[:, t0:t0 + N], op=ALU.mult),
                          reads=["mt%d" % m2, "GS"], writes=["g%d" % h])
                    yield

            def qk_norm_rope(X, xg, gcol, dst, dstn, is_k, h):
                for ti, (t0, N, _) in enumerate(TT):
                    res = []
                    yield from group_norm_g(X, gn(xg, 2), ti, t0, N, blk_bf[:], "blk", 1.0 / 64, res)
                    m1 = res[0]
                    tr.op('dve', lambda e: e.scalar_tensor_tensor(out=X[:, t0:t0 + N], in0=X[:, t0:t0 + N], scalar=small[:, gcol:gcol + 1],
                                                                  in1=mt[m1][:, 0:N], op0=ALU.mult, op1=ALU.mult),
                          reads=gn(xg, 2) + ["small", "mt%d" % m1], writes=gn(xg, 2))
                    yield
                    if is_k:
                        tr.dma('sp', kTo_d[:, h, t0:t0 + N], X[:, t0:t0 + N], "out_k%d" % ti, reads=gn(xg, 2))
                    if ti == 2:
                        tr.op('act', lambda e: e.copy(out=dst[:, t0:t0 + N], in_=X[:, t0:t0 + N]), reads=gn(xg, 2), writes=dstn)
                        yield
                        continue
                    tr.op('dve', lambda e: e.tensor_copy(out=mtb[1][:, 0:N], in_=X[:, t0:t0 + N]), reads=gn(xg, 2), writes=["mtb1"])
                    m3 = next_mt()
                    tr.op('pool', lambda e: e.tensor_tensor(out=mt[m3][:, 0:N], in0=X[:, t0:t0 + N], in1=ropeC[:, t0:t0 + N], op=ALU.mult),
                          reads=gn(xg, 2) + ["ropeC"], writes=["mt%d" % m3])
                    yield
                    kp = next_bank()
                    tr.op('pe', lambda e: e.matmul(P[kp][:, 0:N], perm_bf[:], mtb[1][:, 0:N], start=True, stop=True),
                          reads=["mtb1", "perm"], writes=["P%d" % kp])
                    yield
                    m2 = next_mt()
                    tr.op('dve', lambda e: e.tensor_tensor(out=mt[m2][:, 0:N], in0=P[kp][:, 0:N], in1=ropeS[:, t0:t0 + N], op=ALU.mult),
                          reads=["P%d" % kp, "ropeS"], writes=["mt%d" % m2])
                    yield
                    tr.op('pool', lambda e: e.tensor_tensor(out=dst[:, t0:t0 + N], in0=mt[m2][:, 0:N], in1=mt[m3][:, 0:N], op=ALU.add),
                          reads=["mt%d" % m2, "mt%d" % m3], writes=dstn)
                    yield

            def attention(h, st):
                qr, kr, qrn, krn = qrs[st], krs[st], qrns[st], krns[st]
                blocks = []
                for qb in range(4):
                    specs = [(kcT[:, h, kc * 128:(kc + 1) * 128], ["kcT"], kc) for kc in range(4)]
                    specs += [(kr[:, j * 128:(j + 1) * 128], krn, 4 + j) for j in range(8)]
                    blocks.append((qb * 256, specs, (lambda ki, qb=qb: ki * 4 + qb)))
                specs = [(kr[:, (8 + j) * 128:(9 + j) * 128], krn, 12 + j) for j in range(2)]
                blocks.append((1024, specs, None))
                units = [(bi, m) for bi in range(len(blocks)) for m in range(2)]
                vnames = ["Vaug.c", "Vaug.1"] + ["Vaug.%d" % i for i in range(10)]

                def qk(ui, ki):
                    bi, m = units[ui]
                    q0, specs, maskcol = blocks[bi]
                    kap, knames, vidx = specs[ki]
                    pr = slice(64 * m, 64 * m + 64)
                    k = next_bank()
                    tr.op('pe', lambda e: e.matmul(P[k][:, 0:256], kap[pr, :], qr[pr, q0:q0 + 256], start=True, stop=True),
                          reads=knames + qrn, writes=["P%d" % k])
                    bias = maskb[:, maskcol(ki):maskcol(ki) + 1] if maskcol is not None else 0.0
                    extra = gn(14, 6) if (ui < 2 and ki == 0) else []
                    tr.op('act', lambda e: e.activation(out=ET[ui % 2][:, ki, :], in_=P[k][:, 0:256], func=AF.Exp,
                                                        bias=bias, scale=0.125),
                          reads=["P%d" % k, "maskb"], writes=["ET%d.%d" % (ui % 2, ki)] + extra)

                def pv(ui, ki):
                    bi, m = units[ui]
                    q0, specs, maskcol = blocks[bi]
                    nk = len(specs)
                    vidx = specs[ki][2]
                    kb = 4 + 2 * (ui % 2)

                    def f(e):
                        e.matmul(P[kb][:, 0:130], ET[ui % 2][:, ki, 0:128], Vaug[:, vidx, h, :], start=(ki == 0), stop=(ki == nk - 1))
                        return e.matmul(P[kb + 1][:, 0:130], ET[ui % 2][:, ki, 128:256], Vaug[:, vidx, h, :],
                                        start=(ki == 0), stop=(ki == nk - 1))
                    extra = gn(14, 6) if (ui >= len(units) - 2 and ki == nk - 1) else []
                    tr.op('pe', f, reads=["ET%d.%d" % (ui % 2, ki)] + vnames + extra, writes=["P%d" % kb, "P%d" % (kb + 1)])

                def post_pv(ui):
                    bi, m = units[ui]
                    kb = 4 + 2 * (ui % 2)
                    for qs in range(2):
                        k = kb + qs
                        si = pc['a'] % 2
                        pc['a'] += 1
                        tr.op('dve', lambda e, k=k, si=si: e.reciprocal(out=sm[si][:, 0:1], in_=P[k][:, 128:129]),
                              reads=["P%d" % k], writes=["sm%d" % si])
                        tr.op('dve', lambda e, k=k, si=si, qs=qs: e.tensor_scalar(
                            out=om[m][qs][:], in0=P[k][:, 0:128], scalar1=sm[si][:, 0:1], scalar2=None, op0=ALU.mult),
                            reads=["P%d" % k, "sm%d" % si], writes=["om%d%d" % (m, qs)])

                def post_block(bi):
                    q0 = blocks[bi][0]
                    for qs in range(2):
                        tr.op('dve', lambda e, qs=qs: e.scalar_tensor_tensor(out=odt[qs][:], in0=om[1][qs][:], scalar=lamt[:, 2:3],
                                                                            in1=om[0][qs][:], op0=ALU.mult, op1=ALU.add),
                              reads=["om0%d" % qs, "om1%d" % qs, "lamt"], writes=["od%d" % qs])
                        tr.op('dve', lambda e, qs=qs: e.scalar_tensor_tensor(out=sqj[:], in0=odt[qs][:], scalar=1.0, in1=odt[qs][:],
                                                                            op0=ALU.mult, op1=ALU.mult, accum_out=sm[qs][:, 1:2]),
                              reads=["od%d" % qs], writes=["sqj", "smq%d" % qs])
                        tr.op('act', lambda e, qs=qs: e.activation(out=sm[qs][:, 2:3], in_=sm[qs][:, 1:2], func=AF.Ln, bias=epsc[:, 0:1], scale=1.0 / 128),
                              reads=["smq%d" % qs, "epsc"], writes=["smq%d" % qs])
                        tr.op('act', lambda e, qs=qs: e.activation(out=sm[qs][:, 3:4], in_=sm[qs][:, 2:3], func=AF.Exp, scale=-0.5),
                              reads=["smq%d" % qs], writes=["smq%d" % qs])
                        tr.op('dve', lambda e, qs=qs: e.scalar_tensor_tensor(out=onb[qs][:], in0=odt[qs][:], scalar=sm[qs][:, 3:4],
                                                                            in1=subln[:], op0=ALU.mult, op1=ALU.mult),
                              reads=["od%d" % qs, "smq%d" % qs, "subln"], writes=["on%d" % qs])

                def post_block_b(bi):
                    q0 = blocks[bi][0]
                    k = next_bank()
                    pt = P[k][:, :].bitcast(BF16)

                    def f(e):
                        e.transpose(out=pt[:, 0:128], in_=onb[0][:], identity=ident_bf[:])
                        return e.transpose(out=pt[:, 128:256], in_=onb[1][:], identity=ident_bf[:])
                    tr.op('pe', f, reads=["on0", "on1", "ident"], writes=["P%d" % k])
                    tr.op('dve', lambda e: e.tensor_copy(out=oT[:, 4 + h, q0:q0 + 256], in_=pt[:, 0:256]),
                          reads=["P%d" % k], writes=["g%d" % (4 + h)])

                def nkeys(ui):
                    return len(blocks[units[ui][0]][1])

                pending, pending_a = [], []
                for ui in range(len(units) + 1):
                    yield
                    nu = nkeys(ui) if ui < len(units) else 0
                    npv = nkeys(ui - 1) if ui > 0 else 0
                    for ki in range(max(nu, npv)):
                        if ki and ki % 2 == 0:
                            yield
                        if ki < nu:
                            qk(ui, ki)
                        if ki < npv:
                            pv(ui - 1, ki)
                    if ui > 0:
                        for pb in pending:
                            post_block_b(pb)
                        pending.clear()
                        for pa in pending_a:
                            post_block(pa)
                            pending.append(pa)
                        pending_a.clear()
                        post_pv(ui - 1)
                        if units[ui - 1][1] == 1:
                            pending_a.append(units[ui - 1][0])
                for pa in pending_a:
                    post_block(pa)
                    pending.append(pa)
                for pb in pending:
                    post_block_b(pb)

            def run(*gens, weights=None, set_ctx=True):
                gens = list(gens)
                w = dict(zip([id(g) for g in gens], weights or [1] * len(gens)))
                order = {id(g): i for i, g in enumerate(gens)}
                while gens:
                    for g in list(gens):
                        for _ in range(w[id(g)]):
                            try:
                                if set_ctx:
                                    gp['ctx'] = order[id(g)]
                                next(g)
                            except StopIteration:
                                gens.remove(g)
                                break

            def seq(*gens):
                for g in gens:
                    yield from g

            def evac_copy(dst, dstn, eng='act'):
                if eng == 'act':
                    return lambda ps, pn, t0, N, ti: tr.op(
                        'act', lambda e: e.copy(out=dst[:, t0:t0 + N], in_=ps[:, 0:N]), reads=[pn], writes=dstn)
                return lambda ps, pn, t0, N, ti: tr.op(
                    'dve', lambda e: e.tensor_copy(out=dst[:, t0:t0 + N], in_=ps[:, 0:N]), reads=[pn], writes=dstn)

            def hgrn_front(h):
                base = 7 * h
                yield from proj(base + 0, evac_copy(QT, ["QT"]))
                for d in range(2):
                    def evf(ps, pn, t0, N, ti, h=h, d=d):
                        m1 = next_mt()
                        Fx, Fn = (Fv, gn(8, 2)) if d == 0 else (F2v, ["F2"])
                        tr.op('act', lambda e: e.activation(out=mt[m1][:, 0:N], in_=ps[:, 0:N], func=AF.Sigmoid),
                              reads=[pn], writes=["mt%d" % m1])
                        tr.op('dve', lambda e: e.tensor_scalar(out=Fx[:, t0:t0 + N], in0=mt[m1][:, 0:N], scalar1=oml[:, h:h + 1],
                                                               scalar2=lb[:, h:h + 1], op0=ALU.mult, op1=ALU.add),
                              reads=["mt%d" % m1, "oml", "lb"], writes=Fn)
                    yield from proj(base + 2 + d, evf)

                def rest():
                    VT = gran(16)
                    yield from proj(base + 1, evac_copy(VT, gn(16)))
                    transposes(VT, gn(16), Vtok, gn(20, 2))
                    yield
                    yield from proj(base + 4, lambda ps, pn, t0, N, ti: tr.op(
                        'act', lambda e: e.activation(out=GS[:, t0:t0 + N], in_=ps[:, 0:N], func=AF.Silu), reads=[pn], writes=["GS"]))
                run(rest(), prep(0, h), prep(1, h), set_ctx=False)
                yield

            def attn_front(h, st):
                base = 7 * h
                yield from proj(base + 5, evac_copy(AQ, gn(8, 2), 'dve'))
                yield from proj(base + 6, evac_copy(AK, gn(10, 2), 'dve'))
                yield from qk_norm_rope(AQ, 8, 1, qrs[st], qrns[st], False, h)
                yield from qk_norm_rope(AK, 10, 2, krs[st], krns[st], True, h)

            for h in range(4):
                if h == 0:
                    run(hgrn_front(h))
                if h == 3:
                    gp['by_ctx'] = {0: [2, 3], 1: [2, 3]}
                    run(chains(h), attn_front(0, 0), weights=[1, 2])
                    gp['by_ctx'] = None
                else:
                    run(chains(h))
                if h < 3:
                    run(hgrn_front(h + 1), orec_final(h))
                else:
                    run(vproj_and_caches(), orec_final(h))
            gp['banks'] = [0, 1, 2, 3]
            for h in range(4):
                gens = [attention(h, h % 2)]
                if h < 3:
                    gens.append(attn_front(h + 1, (h + 1) % 2))
                gp['by_ctx'] = {0: [0, 1, 2], 1: [3]} if h < 3 else None
                run(*gens, weights=[1, 1])
                gp['by_ctx'] = None

            for o in range(8):
                bi = cnt['wo'] % 2
                cnt['wo'] += 1
                tr.dma('pool', wobuf[bi][:, 0:8, :], wmo_d[o], "wo%d" % bi, writes=["wo%d" % bi, "QT", "GS", "KK"])
                for ti, (t0, N, ci) in enumerate(TT):
                    k = 5 + cnt['oo'] % 2
                    cnt['oo'] += 1

                    def f(e, k=k, bi=bi):
                        ins = None
                        for kc in range(8):
                            ins = e.matmul(P[k][:, 0:N], wobuf[bi][:, kc, :], oT[:, kc, t0:t0 + N], start=(kc == 0), stop=(kc == 7))
                        return ins
                    tr.op('pe', f, reads=["wo%d" % bi] + gn(0, 8), writes=["P%d" % k])
                    tr.op('dve', lambda e, k=k, o=o: e.scalar_tensor_tensor(
                        out=xT[:, o, t0:t0 + N], in0=P[k][:, 0:N], scalar=Gmod[:, 1, ci, o:o + 1],
                        in1=xT[:, o, t0:t0 + N], op0=ALU.mult, op1=ALU.add),
                        reads=["P%d" % k, "Gmod", "xT.%d.%d" % (o, ti)], writes=["xT.%d.%d" % (o, ti)])
            barrier()

        mtbig = sb("mtbig", [128, 2560], BF16)
        load_mixer_consts()

        def ada_hook(j):
            for sbi in ([0, 1] if j == 0 else [j + 1]):
                if sbi < ADA_NSLAB:
                    ada_slab(sbi)
        ffn(0, w1in_d, w1out_d, hook=(ada_hook if STAGE >= 2 else None))
        if STAGE >= 2:
            mods_finalize(1)
            mods_finalize(2)
        if STAGE >= 2:
            mixer()
        if STAGE >= 3:
            ffn(2, w2in_d, w2out_d)

        if STAGE < 3:
            for c in range(8):
                tr.dma('sp', yT_d[:, c, :], xT[:, c, :], "out_y", reads=["xT.%d.%d" % (c, t) for t in range(3)])
        tr.finish()
    return nc


def _fm(x2d):
    t, f = x2d.shape
    return np.ascontiguousarray(x2d.T.reshape(f // 128, 128, t).transpose(1, 0, 2))


def _host_consts():
    c = {}
    perm = np.zeros((128, 128), np.float32)
    for m in range(128):
        r = m % 32
        partner = m + 16 if r < 16 else m - 16
        perm[partner, m] = 1.0
    c["perm"] = perm
    blk = np.zeros((128, 128), np.float32)
    blk[:64, :64] = 1.0
    blk[64:, 64:] = 1.0
    c["blk64"] = blk
    c["ident"] = np.eye(128, dtype=np.float32)
    tri = np.zeros((64, 2, 64), np.float32)
    s = np.arange(64)[:, None]
    t = np.arange(64)[None, :]
    tri[:, 0, :] = (s <= t)
    tri[:, 1, :] = (s >= t)
    c["tri"] = tri
    mres = np.ones((128, T), np.float32)
    mres[:, ::CH] = 0.0
    c["mres"] = mres
    return c


def _rope_tables(sample):
    C = np.ones((128, T), np.float32)
    S = np.zeros((128, T), np.float32)
    if sample:
        n = 1024
        half = 32
        inv = (10000.0 ** (-np.arange(0, half, 2, dtype=np.float32) / half)).astype(np.float32)
        pos_row = np.repeat(np.arange(n // 64, dtype=np.float32), 64)
        pos_col = np.tile(np.arange(64, dtype=np.float32), n // 64)
        ar = (pos_row[:, None] * inv).astype(np.float32)
        ac = (pos_col[:, None] * inv).astype(np.float32)
        cr, sr, cc, sc = [a.astype(np.float32) for a in (np.cos(ar), np.sin(ar), np.cos(ac), np.sin(ac))]
        for m in range(2):
            b = 64 * m
            C[b + 0:b + 16, :n] = cr.T
            C[b + 16:b + 32, :n] = cr.T
            C[b + 32:b + 48, :n] = cc.T
            C[b + 48:b + 64, :n] = cc.T
            S[b + 0:b + 16, :n] = -sr.T
            S[b + 16:b + 32, :n] = sr.T
            S[b + 32:b + 48, :n] = -sc.T
            S[b + 48:b + 64, :n] = sc.T
    return C, S


def kernel(x_prompt, x_sample, c, cache_attn_k, cache_attn_v, state_hgrn, c_ctx,
           w_ada, b_ada, norm_ffn1, w_ffn1_in, w_ffn1_out, norm_mix, w_mix_in, w_mix_out,
           hgrn_lb_logits, hgrn_out_norm, attn_q_norm, attn_k_norm, attn_lambda, attn_subln,
           norm_ffn2, w_ffn2_in, w_ffn2_out):
    f = lambda a: np.asarray(a, np.float32)
    x_prompt, x_sample, c, c_ctx = f(x_prompt), f(x_sample), f(c), f(c_ctx)
    cache_attn_k, cache_attn_v, state_hgrn = f(cache_attn_k), f(cache_attn_v), f(state_hgrn)

    def ffn_in_layout(w):
        w = f(w)[0]
        g = w[:, :DFF].reshape(8, 128, NJ, 128)
        u = w[:, DFF:].reshape(8, 128, NJ, 128)
        return np.ascontiguousarray(np.concatenate([g, u], axis=3).transpose(2, 1, 0, 3))

    def out_layout(w, nk):
        w = f(w)[0] if w.ndim == 3 else f(w)
        return np.ascontiguousarray(w.reshape(nk, 128, 8, 128).transpose(2, 1, 0, 3))

    wm = f(w_mix_in)[0]
    order = []
    for h in range(4):
        order += [h, 4 + h, 8 + h, 12 + h, 16 + h, 20 + h, 24 + h]
    wm_chunks = wm.reshape(8, 128, 32, 128)
    wmix = wm_chunks[:, :, order, :].reshape(8, 128, 14, 256).transpose(2, 1, 0, 3)
    wav = wm[:, 3584:].reshape(8, 128, 512).transpose(1, 0, 2)

    shared = {
        "badaT": np.ascontiguousarray(f(b_ada)[0].reshape(72, 128).T),
        "gT": np.ascontiguousarray(np.stack([f(norm_ffn1)[0], f(norm_mix)[0], f(norm_ffn2)[0]])
                                   .reshape(3, 8, 128).transpose(2, 0, 1)),
        "w_ada": np.ascontiguousarray(f(w_ada)[0]),
        "w1in": ffn_in_layout(w_ffn1_in), "w1out": out_layout(w_ffn1_out, NJ),
        "w2in": ffn_in_layout(w_ffn2_in), "w2out": out_layout(w_ffn2_out, NJ),
        "wmix": np.ascontiguousarray(wmix), "wav": np.ascontiguousarray(wav),
        "wmo": out_layout(w_mix_out, 8),
        "lbl": np.ascontiguousarray(f(hgrn_lb_logits).reshape(2, 4, 128).transpose(2, 0, 1)),
        "small": np.ascontiguousarray(np.stack([
            f(hgrn_out_norm)[0], np.tile(f(attn_q_norm)[0], 2), np.tile(f(attn_k_norm)[0], 2),
            np.zeros(128, np.float32)], axis=1)),
        "lamp": np.ascontiguousarray(np.broadcast_to(f(attn_lambda)[0][None], (128, 4, 64))),
        "subln": np.ascontiguousarray(np.broadcast_to(f(attn_subln)[0][None], (128, 128))),
    }
    shared.update(_host_consts())
    ropeP = _rope_tables(False)
    ropeS_ = _rope_tables(True)

    in_maps = []
    for core in range(NCORES):
        sample = core < 2
        if sample:
            xs = np.concatenate([x_sample[core], x_prompt[30 + core]], axis=0)
            cond = np.stack([c[core], c_ctx])
        else:
            p0 = 5 * (core - 2)
            xs = x_prompt[p0:p0 + 5].reshape(T, D)
            cond = np.stack([c_ctx, c_ctx])
        m = dict(shared)
        m["xT"] = _fm(xs)
        m["condT"] = np.ascontiguousarray(cond.reshape(2, 8, 128).transpose(2, 1, 0))
        C_, S_ = ropeS_ if sample else ropeP
        m["ropeC"], m["ropeS"] = C_, S_
        NEG = -30000.0
        maskb = np.full((12, 4), NEG, np.float32)
        flags = np.zeros((128, 10), np.float32)
        if sample:
            maskb[:, :] = 0.0
            flags[:, 1:4] = 1.0
            flags[:, 5:8] = 1.0
            kc_ = cache_attn_k[core, 0]
            m["kcT"] = np.ascontiguousarray(kc_.reshape(512, 4, 128).transpose(2, 1, 0))
            m["vc"] = np.ascontiguousarray(cache_attn_v[core, 0].reshape(4, 128, 4, 128).transpose(1, 0, 2, 3))
            m["s0"] = np.ascontiguousarray(state_hgrn[core, 0].transpose(2, 0, 1, 3))
        else:
            for qb in range(4):
                maskb[4 + 2 * qb: 6 + 2 * qb, qb] = 0.0
            m["kcT"] = np.zeros((128, 4, 512), np.float32)
            m["vc"] = np.zeros((128, 4, 4, 128), np.float32)
            m["s0"] = np.zeros((128, 2, 4, 128), np.float32)
        m["maskb"] = np.ascontiguousarray(np.broadcast_to(maskb.reshape(1, 48), (128, 48)))
        m["flags"] = flags
        in_maps.append(m)

    if _ONLY_PREPARE:
        return in_maps
    nc = build_program()
    if _DEBUG_CORES:
        res = run_bass_kernel_spmd(nc, [in_maps[i] for i in _DEBUG_CORES], core_ids=list(range(len(_DEBUG_CORES))))
        return [res.results[0]["yT"]]
    res = run_bass_kernel_spmd(nc, in_maps, core_ids=list(range(NCORES)))
    return _assemble(res.results)


_ONLY_PREPARE = False
_DEBUG_CORES = None


def _assemble(R):
    y_prompt = np.zeros((32, 256, D), np.float32)
    y_sample = np.zeros((2, 1024, D), np.float32)
    new_k = np.zeros((32, 1, 256, 4, 2, 64), np.float32)
    new_v = np.zeros((32, 1, 256, 4, 128), np.float32)
    new_s = np.zeros((32, 1, 2, 4, 128, 128), np.float32)
    for core in range(NCORES):
        r = R[core]
        y = r["yT"].transpose(2, 1, 0).reshape(T, D)
        kk = r["kTo"].transpose(2, 1, 0).reshape(T, 4, 2, 64)
        vv = r["vo"].transpose(1, 0, 2).reshape(T, 4, 128)
        ss = r["so"].transpose(1, 2, 3, 0, 4)
        if core < 2:
            y_sample[core] = y[:1024]
            blocks = [(4, 30 + core)]
        else:
            blocks = [(b, 5 * (core - 2) + b) for b in range(5)]
        for b, pi in blocks:
            sl = slice(256 * b, 256 * (b + 1))
            y_prompt[pi] = y[sl]
            new_k[pi, 0] = kk[sl]
            new_v[pi, 0] = vv[sl]
            new_s[pi, 0] = ss[b]
    return (y_prompt, y_sample, new_k, new_v, new_s)
```

```python
import contextlib
import numpy as np
import concourse.bass as bass
import concourse.mybir as mybir
from concourse.bass_utils import run_bass_kernel_spmd

F32 = mybir.dt.float32
BF16 = mybir.dt.bfloat16
AF = mybir.ActivationFunctionType
ALU = mybir.AluOpType

NCORES = 8
D = 1024
T = 1280
NBLK = 5
DFF = 2816
NJ = DFF // 128
EPS = 1e-6
TT = [(0, 512, 0), (512, 512, 0), (1024, 256, 1)]
CH = 64
NCH = T // CH
LAM_INIT = 0.2
STAGE = 99
MIXL = 8


class Tr:
    def __init__(self, nc, stack):
        self.nc = nc
        self.stack = stack
        self.E = {'pe': nc.tensor, 'act': nc.scalar, 'dve': nc.vector, 'pool': nc.gpsimd, 'sp': nc.sync}
        self.sem = {k: stack.enter_context(nc.semaphore("sem_" + k)) for k in self.E}
        self.cnt = {k: 0 for k in self.E}
        self.lastw = {}
        self.rd = {}
        self.waited = {k: {} for k in self.E}
        self.dsem = {}

    def _wait(self, eng, dep, kind):
        if dep is None:
            return
        sem, val, src, name = dep
        if src == eng:
            if eng in ('pe', 'sp'):
                return
        if self.waited[eng].get(name, 0) >= val:
            return
        self.E[eng].wait_ge(sem, val)
        self.waited[eng][name] = val

    def _pre(self, eng, reads, writes):
        for r in reads:
            self._wait(eng, self.lastw.get(r), 'raw')
            if len(r) == 2 and r[0] == 'P':
                for d in self.rd.get(r, ()):
                    if d[2] != eng:
                        self._wait(eng, d, 'war')
        for w in writes:
            self._wait(eng, self.lastw.get(w), 'waw')
            for d in self.rd.get(w, ()):
                self._wait(eng, d, 'war')

    def _post(self, dep, reads, writes):
        for r in reads:
            self.rd.setdefault(r, []).append(dep)
        for w in writes:
            self.lastw[w] = dep
            self.rd[w] = []

    def op(self, eng, fn, reads=(), writes=()):
        self._pre(eng, reads, writes)
        ins = fn(self.E[eng])
        self.cnt[eng] += 1
        ins.then_inc(self.sem[eng], 1)
        self._post((self.sem[eng], self.cnt[eng], eng, eng), reads, writes)

    def dma(self, q, out, in_, semname, reads=(), writes=()):
        if semname not in self.dsem:
            self.dsem[semname] = [self.stack.enter_context(self.nc.semaphore("d_" + semname)), 0]
        ds = self.dsem[semname]
        self._pre(q, reads, writes)
        self.E[q].dma_start(out=out, in_=in_).then_inc(ds[0], 16)
        ds[1] += 16
        self._post((ds[0], ds[1], 'dma', "d_" + semname), reads, writes)

    def finish(self):
        for name, (sem, val) in self.dsem.items():
            if val:
                self.nc.sync.wait_ge(sem, val)


def build_program():
    nc = bass.Bass("TRN2", target_bir_lowering=False)

    def din(name, shape, dt=F32):
        return nc.dram_tensor(name, list(shape), dt, kind="ExternalInput").ap()

    def dout(name, shape, dt=F32):
        return nc.dram_tensor(name, list(shape), dt, kind="ExternalOutput").ap()

    xT_d = din("xT", [128, 8, T])
    condT_d = din("condT", [128, 8, 2])
    badaT_d = din("badaT", [128, 72])
    gT_d = din("gT", [128, 3, 8])
    wada_d = din("w_ada", [D, 9 * D])
    w1in_d = din("w1in", [NJ, 128, 8, 256])
    w1out_d = din("w1out", [8, 128, NJ, 128])
    w2in_d = din("w2in", [NJ, 128, 8, 256])
    w2out_d = din("w2out", [8, 128, NJ, 128])
    wmix_d = din("wmix", [14, 128, 8, 256])
    wav_d = din("wav", [128, 8, 512])
    wmo_d = din("wmo", [8, 128, 8, 128])
    lbl_d = din("lbl", [128, 2, 4])
    small_d = din("small", [128, 4])
    lamp_d = din("lamp", [128, 4, 64])
    subln_d = din("subln", [128, 128])
    ropeC_d = din("ropeC", [128, T])
    ropeS_d = din("ropeS", [128, T])
    perm_d = din("perm", [128, 128])
    blk64_d = din("blk64", [128, 128])
    ident_d = din("ident", [128, 128])
    tri_d = din("tri", [64, 2, 64])
    mres_d = din("mres", [128, T])
    maskb_d = din("maskb", [128, 48])
    flags_d = din("flags", [128, 10])
    kcT_d = din("kcT", [128, 4, 512])
    vc_d = din("vc", [128, 4, 4, 128])
    s0_d = din("s0", [128, 2, 4, 128])

    yT_d = dout("yT", [128, 8, T])
    kTo_d = dout("kTo", [128, 4, T])
    vo_d = dout("vo", [128, 10, 512])
    so_d = dout("so", [128, NBLK, 2, 4, 128])

    with contextlib.ExitStack() as st:
        tr = Tr(nc, st)

        def sb(name, shape, dt=F32):
            return st.enter_context(nc.sbuf_tensor(name, list(shape), dt))

        P = [st.enter_context(nc.psum_tensor("ps%d" % i, [128, 512], F32)) for i in range(8)]

        xT = sb("xT_s", [128, 8, T])
        hT = sb("hT_s", [128, 8, T], BF16)
        arena = sb("arena", [128, NJ * T], BF16)
        actT = arena[:, :].rearrange("p (j t) -> p j t", j=NJ)
        NWB = 3
        wbuf = [sb("wbuf%d" % i, [128, 8, 256], BF16) for i in range(NWB)]
        wobuf = [sb("wobuf%d" % i, [128, NJ, 128], BF16) for i in range(2)]
        condT = sb("condT_s", [128, 8, 2])
        scond = sb("scond", [128, 8, 2], BF16)
        badaT = sb("badaT_s", [128, 72])
        gT = sb("gT_s", [128, 3, 8])
        modT = sb("modT", [128, 72, 2])
        Amod = sb("Amod", [128, 3, 2, 8])
        Gmod = sb("Gmod", [128, 3, 2, 8])
        ones_bf = sb("ones_bf", [128, 128], BF16)
        scratch = sb("scratch", [128, 10240], BF16)
        sq = scratch[:, 0:4096].rearrange("p (c n) -> p c n", c=8)
        sd = scratch[:, 4096:5120].bitcast(F32)
        rstd = scratch[:, 5120:6144].bitcast(F32)
        tmpA = [scratch[:, 6144 + 1024 * i:7168 + 1024 * i].bitcast(F32) for i in range(2)]
        sg = [scratch[:, 8192 + 1024 * i:9216 + 1024 * i].bitcast(F32) for i in range(2)]

        tr.dma('sp', condT[:], condT_d, "in0", writes=["condT"])
        tr.dma('sp', badaT[:], badaT_d, "in0b", writes=["badaT"])
        tr.dma('sp', gT[:], gT_d, "in0c", writes=["gT"])
        for c in range(8):
            tr.dma('sp', xT[:, c, :], xT_d[:, c, :], "x%d" % c, writes=["xT.%d.%d" % (c, t) for t in range(3)])
        tr.op('dve', lambda e: e.memset(ones_bf[:], 1.0), writes=["ones"])
        epsc0 = sb("epsc0", [128, 1])
        tr.op('dve', lambda e: e.memset(epsc0[:], EPS), writes=["epsc0"])
        tr.op('act', lambda e: e.activation(out=scond[:], in_=condT[:], func=AF.Silu),
              reads=["condT"], writes=["scond"])

        RS = [rstd, sg[0], sg[1]]
        RSN = ["rstd", "sg0", "sg1"]

        def norm_stats():
            for ti, (t0, N, ci) in enumerate(TT):
                xr = ["xT.%d.%d" % (c, ti) for c in range(8)]
                tr.op('dve', lambda e: e.tensor_tensor(out=sq[:, 0:4, 0:N], in0=xT[:, 0:4, t0:t0 + N],
                                                       in1=xT[:, 0:4, t0:t0 + N], op=ALU.mult),
                      reads=xr[0:4], writes=["sq.a"])
                tr.op('act', lambda e: e.activation(out=sq[:, 4:8, 0:N], in_=xT[:, 4:8, t0:t0 + N], func=AF.Square),
                      reads=xr[4:8], writes=["sq.b"])

                def f(e):
                    ins = None
                    for c in range(8):
                        ins = e.matmul(P[4][:, 0:N], ones_bf[:], sq[:, c, 0:N], start=(c == 0), stop=(c == 7))
                    return ins
                tr.op('pe', f, reads=["sq.a", "sq.b", "ones"], writes=["P4"])
                tr.op('act', lambda e: e.activation(out=sd[:, 0:N], in_=P[4][:, 0:N], func=AF.Ln,
                                                    bias=epsc0[:, 0:1], scale=1.0 / D), reads=["P4", "epsc0"], writes=["sd"])
                tr.op('act', lambda e: e.activation(out=RS[ti][:, 0:N], in_=sd[:, 0:N], func=AF.Exp, scale=-0.5),
                      reads=["sd"], writes=[RSN[ti]])

        norm_stats()

        wada_v = wada_d.rearrange("(kc p) n -> p kc n", p=128)
        wada_buf = [arena[:, i * 9216:(i + 1) * 9216].rearrange("p (k n) -> p k n", k=8) for i in range(2)]
        psm = P[7][:, 0:144].rearrange("p (c i) -> p c i", i=2)
        for s in range(3):
            bi = s % 2
            tr.dma('pool', wada_buf[bi], wada_v[:, :, s * 1152:(s + 1) * 1152], "wada%d" % bi,
                   writes=["wada%d" % bi])

            def f(e, s=s, bi=bi):
                ins = None
                for cc in range(9):
                    ch = s * 9 + cc
                    for kc in range(8):
                        ins = e.matmul(psm[:, ch, :], wada_buf[bi][:, kc, cc * 128:(cc + 1) * 128],
                                       scond[:, kc, :], start=(kc == 0), stop=(kc == 7))
                return ins
            tr.op('pe', f, reads=["wada%d" % bi, "scond"], writes=["P7"])
        def mods_finalize(n):
            lo, hi = 24 * n, 24 * (n + 1)
            for i in range(2):
                tr.op('dve', lambda e, i=i: e.tensor_tensor(out=modT[:, lo:hi, i], in0=psm[:, lo:hi, i], in1=badaT[:, lo:hi],
                                                           op=ALU.add),
                      reads=["P7", "badaT"], writes=["modT"])
            for i in range(2):
                tr.op('dve', lambda e, i=i: e.scalar_tensor_tensor(
                    out=Amod[:, n, i, :], in0=modT[:, (3 * n + 1) * 8:(3 * n + 2) * 8, i], scalar=1.0,
                    in1=gT[:, n, :], op0=ALU.add, op1=ALU.mult), reads=["modT", "gT"], writes=["Amod"])
                tr.op('dve', lambda e, i=i: e.tensor_scalar(
                    out=Gmod[:, n, i, :], in0=modT[:, (3 * n + 2) * 8:(3 * n + 3) * 8, i],
                    scalar1=(1.0 if n == 1 else 0.5), scalar2=None, op0=ALU.mult),
                    reads=["modT"], writes=["Gmod"])

        mods_finalize(0)

        ADA_B0 = 3456
        ADA_NSLAB = (9 * D - ADA_B0 + 255) // 256

        def ada_slab(sbi):
            c0 = ADA_B0 + 256 * sbi
            ncol = min(256, 9 * D - c0)
            bi = cnt['wo'] % 2
            cnt['wo'] += 1
            wob = wobuf[bi][:, :, :].rearrange("p a b -> p (a b)")[:, 0:2048].rearrange("p (k n) -> p k n", k=8)
            tr.dma('pool', wob[:, :, 0:ncol], wada_v[:, :, c0:c0 + ncol], "wo%d" % bi, writes=["wo%d" % bi])

            def f(e):
                ins = None
                for cc in range(ncol // 128):
                    ch = c0 // 128 + cc
                    for kc in range(8):
                        ins = e.matmul(psm[:, ch, :], wob[:, kc, cc * 128:(cc + 1) * 128], scond[:, kc, :],
                                       start=(kc == 0), stop=(kc == 7))
                return ins
            tr.op('pe', f, reads=["wo%d" % bi, "scond"], writes=["P7"])

        def Bmod(n, i, c):
            return modT[:, 3 * n * 8 + c, i:i + 1]

        cnt = {'io': 0, 'oo': 0, 'tmp': 0, 'wb': 0, 'wo': 0}

        def norm_apply(n):
            for ti, (t0, N, ci) in enumerate(TT):
                for c in range(8):
                    k = cnt['tmp'] % 2
                    cnt['tmp'] += 1
                    tr.op('dve', lambda e, c=c, k=k: e.scalar_tensor_tensor(
                        out=tmpA[k][:, 0:N], in0=xT[:, c, t0:t0 + N], scalar=Amod[:, n, ci, c:c + 1],
                        in1=RS[ti][:, 0:N], op0=ALU.mult, op1=ALU.mult),
                        reads=["xT.%d.%d" % (c, ti), "Amod", RSN[ti]], writes=["tmpA%d" % k])
                    tr.op('act', lambda e, c=c, k=k: e.activation(
                        out=hT[:, c, t0:t0 + N], in_=tmpA[k][:, 0:N], func=AF.Identity, bias=Bmod(n, ci, c)),
                        reads=["tmpA%d" % k, "modT"], writes=["hT.%d.%d" % (ti, c)])

        def norm_mod(n):
            if n > 0:
                norm_stats()
            norm_apply(n)

        def ffn(n, win_d, wout_d, hook=None):
            pre = {}

            def issue(j):
                bi = cnt['wb'] % NWB
                cnt['wb'] += 1
                tr.dma('pool', wbuf[bi][:], win_d[j], "wb%d" % bi, writes=["wb%d" % bi])
                return bi
            for j in range(NWB - 1):
                pre[j] = issue(j)
            norm_mod(n)
            for j in range(NJ):
                bi = pre[j] if j in pre else issue(j)
                for ti, (t0, N, ci) in enumerate(TT):
                    k = cnt['io'] % 2
                    cnt['io'] += 1
                    pg, pu = P[2 * k], P[2 * k + 1]

                    def f(e, pg=pg, pu=pu, bi=bi):
                        ins = None
                        for kc in range(8):
                            e.matmul(pg[:, 0:N], wbuf[bi][:, kc, 0:128], hT[:, kc, t0:t0 + N],
                                     start=(kc == 0), stop=(kc == 7))
                        for kc in range(8):
                            ins = e.matmul(pu[:, 0:N], wbuf[bi][:, kc, 128:256], hT[:, kc, t0:t0 + N],
                                           start=(kc == 0), stop=(kc == 7))
                        return ins
                    tr.op('pe', f, reads=["wb%d" % bi] + ["hT.%d.%d" % (ti, c_) for c_ in range(8)], writes=["P%d" % (2 * k), "P%d" % (2 * k + 1)])
                    tr.op('act', lambda e, pg=pg, k=k: e.activation(out=sg[k][:, 0:N], in_=pg[:, 0:N], func=AF.Silu),
                          reads=["P%d" % (2 * k)], writes=["sg%d" % k])
                    tr.op('dve', lambda e, pu=pu, k=k, j=j: e.tensor_tensor(
                        out=actT[:, j, t0:t0 + N], in0=sg[k][:, 0:N], in1=pu[:, 0:N], op=ALU.mult),
                        reads=["sg%d" % k, "P%d" % (2 * k + 1)], writes=["actT.%d" % ti])
                if hook is not None:
                    hook(j)
            for o in range(8):
                bi = cnt['wo'] % 2
                cnt['wo'] += 1
                tr.dma('pool', wobuf[bi][:], wout_d[o], "wo%d" % bi, writes=["wo%d" % bi])
                for ti, (t0, N, ci) in enumerate(TT):
                    k = 5 + cnt['oo'] % 2
                    cnt['oo'] += 1

                    def f(e, k=k, bi=bi, o=o):
                        ins = None
                        for kc in range(NJ):
                            ins = e.matmul(P[k][:, 0:N], wobuf[bi][:, kc, :], actT[:, kc, t0:t0 + N],
                                           start=(kc == 0), stop=(kc == NJ - 1))
                        return ins
                    tr.op('pe', f, reads=["wo%d" % bi, "actT.%d" % ti], writes=["P%d" % k])
                    tr.op('dve', lambda e, k=k, o=o: e.scalar_tensor_tensor(
                        out=xT[:, o, t0:t0 + N], in0=P[k][:, 0:N], scalar=Gmod[:, n, ci, o:o + 1],
                        in1=xT[:, o, t0:t0 + N], op0=ALU.mult, op1=ALU.add),
                        reads=["P%d" % k, "Gmod", "xT.%d.%d" % (o, ti)], writes=["xT.%d.%d" % (o, ti)])
                if n == 2:
                    tr.dma('sp', yT_d[:, o, :], xT[:, o, :], "out_y", reads=["xT.%d.%d" % (o, t) for t in range(3)])


        def barrier():
            for e in ('pe', 'act', 'dve'):
                for o in ('pe', 'act', 'dve'):
                    if o != e and tr.cnt[o] > tr.waited[e].get(o, 0):
                        tr.E[e].wait_ge(tr.sem[o], tr.cnt[o])
                        tr.waited[e][o] = tr.cnt[o]
                for name, (sem, val) in tr.dsem.items():
                    if name.startswith("out_") and val > tr.waited[e].get("d_" + name, 0):
                        tr.E[e].wait_ge(sem, val)
                        tr.waited[e]["d_" + name] = val

        ropeC = sb("ropeC_s", [128, 1024])
        ropeS = sb("ropeS_s", [128, 1024])
        mres = sb("mres_s", [128, T], BF16)
        perm_bf = sb("perm_bf", [128, 128], BF16)
        blk_bf = sb("blk_bf", [128, 128], BF16)
        ident_bf = sb("ident_bf", [128, 128], BF16)
        tri = sb("tri_s", [64, 2, 64])
        maskb = sb("maskb_s", [128, 48])
        flags = sb("flags_s", [128, 10])
        s0in = sb("s0in", [128, 2, 4, 128])
        lbl = sb("lbl_s", [128, 2, 4])
        small = sb("small_s", [128, 4])
        lamp = sb("lamp_s", [128, 4, 64])
        subln = sb("subln_s", [128, 128])
        lb = sb("lb", [128, 4])
        oml = sb("oml", [128, 4])
        lamt = sb("lamt", [128, 8])
        lamj = sb("lamj", [128, 64])
        oF = sb("oF", [128, T])
        mt = [sb("mt%d" % i, [128, 512]) for i in range(4)]
        mtb = [sb("mtb%d" % i, [128, 512], BF16) for i in range(2)]
        Sfm = sb("Sfm", [128, 2, 2, 128])
        Sbm = sb("Sbm", [128, 2, 2, 128], BF16)
        Asm = sb("Asm", [64, 2, 2, 64], BF16)
        dec = [sb("dec%d" % d, [128, NCH]) for d in range(2)]
        bend = sb("bend", [128, NCH])
        om = [[sb("om%d%d" % (m, q), [128, 128]) for q in range(2)] for m in range(2)]
        odt = [sb("od%d" % q, [128, 128]) for q in range(2)]
        onb = [sb("on%d" % q, [128, 128], BF16) for q in range(2)]
        sm = [sb("sm%d" % i, [128, 4]) for i in range(2)]
        sqj = sb("sqj", [128, 128])
        epsc = sb("epsc", [128, 1])
        mhalf = sb("mhalf", [128, 512])

        def load_mixer_consts():
            tr.op('dve', lambda e: e.memset(epsc[:], EPS), writes=["epsc"])
            tr.op('dve', lambda e: e.memset(mhalf[:], -0.5), writes=["mhalf"])
            for (dst, src_, nm) in [(ropeC[:], ropeC_d[:, 0:1024], "ropeC"), (ropeS[:], ropeS_d[:, 0:1024], "ropeS"),
                                    (tri[:], tri_d, "tri"), (maskb[:], maskb_d, "maskb"), (flags[:], flags_d, "flags"),
                                    (s0in[:], s0_d, "s0in"), (lbl[:], lbl_d, "lbl"), (small[:], small_d, "small"),
                                    (lamp[:], lamp_d, "lamp"), (subln[:], subln_d, "subln")]:
                tr.dma('sp', dst, src_, "c_" + nm, writes=[nm])
            for (dst, src_, nm) in [(mres[:], mres_d, "mres"), (perm_bf[:], perm_d, "perm"),
                                    (blk_bf[:], blk64_d, "blk"), (ident_bf[:], ident_d, "ident")]:
                tr.dma('pool', dst, src_, "c_" + nm, writes=[nm])
            tr.op('dve', lambda e: e.tensor_tensor(out=lb[:], in0=lbl[:, 0, :], in1=lbl[:, 1, :], op=ALU.subtract),
                  reads=["lbl"], writes=["lb"])
            tr.op('act', lambda e: e.activation(out=lb[:], in_=lb[:], func=AF.Sigmoid), reads=["lb"], writes=["lb"])
            tr.op('dve', lambda e: e.tensor_scalar(out=oml[:], in0=lb[:], scalar1=-1.0, scalar2=1.0,
                                                   op0=ALU.mult, op1=ALU.add), reads=["lb"], writes=["oml"])
            for i in range(2):
                tr.op('dve', lambda e, i=i: e.tensor_tensor(out=lamj[:], in0=lamp[:, 2 * i, :], in1=lamp[:, 2 * i + 1, :],
                                                           op=ALU.mult), reads=["lamp"], writes=["lamj"])
                tr.op('act', lambda e, i=i: e.activation(out=sqj[:, 0:64], in_=lamj[:], func=AF.Identity,
                                                        accum_out=lamt[:, i:i + 1]), reads=["lamj"], writes=["lamt", "sqj"])
            tr.op('act', lambda e: e.activation(out=lamt[:, 4:6], in_=lamt[:, 0:2], func=AF.Exp), reads=["lamt"], writes=["lamt"])
            tr.op('dve', lambda e: e.tensor_tensor(out=lamt[:, 6:7], in0=lamt[:, 5:6], in1=lamt[:, 4:5], op=ALU.subtract),
                  reads=["lamt"], writes=["lamt"])
            tr.op('dve', lambda e: e.tensor_scalar(out=lamt[:, 2:3], in0=lamt[:, 6:7], scalar1=-LAM_INIT, scalar2=None,
                                                   op0=ALU.add), reads=["lamt"], writes=["lamt"])
            tr.op('dve', lambda e: e.tensor_scalar(out=subln[:], in0=subln[:], scalar1=1.0 - LAM_INIT, scalar2=None,
                                                   op0=ALU.mult), reads=["subln"], writes=["subln"])

        def mixer():
            barrier()
            norm_mod(1)
            barrier()
            if MIXL < 1:
                return
            FFN_SCR = ["sq.a", "sq.b", "sd", "rstd", "tmpA0", "tmpA1", "sg0", "sg1", "F2", "BB2", "EE2", "KK2"]

            def gran(g, n=1):
                return arena[:, g * 1280:(g + n) * 1280]

            def gn(g, n=1):
                return ["g%d" % i for i in range(g, g + n)]
            oT = arena[:, 0:8 * 1280].rearrange("p (c t) -> p c t", c=8)
            Fv, BBv, EEv = gran(8, 2).bitcast(F32), gran(10, 2).bitcast(F32), gran(12, 2).bitcast(F32)
            Qd = [gran(14), gran(15)]
            F2v, BB2v, EE2v = scratch[:, 0:2560].bitcast(F32), scratch[:, 2560:5120].bitcast(F32), scratch[:, 5120:7680].bitcast(F32)
            KK2 = scratch[:, 7680:8960]
            Ktok = [arena[0:64, (16 + 2 * d) * 1280:(18 + 2 * d) * 1280].rearrange("p (a b) -> p a b", a=NCH)
                    for d in range(2)]
            Vtok = arena[0:64, 20 * 1280:22 * 1280].rearrange("p (a b) -> p a b", a=NCH)
            wo0 = wobuf[0][:, :, :].rearrange("p a b -> p (a b)")
            wo1 = wobuf[1][:, :, :].rearrange("p a b -> p (a b)")
            QT = wo0[:, 0:2560].bitcast(F32)
            GS = wo1[:, 0:1280]
            KK = wo1[:, 1280:2560]
            Vaug = scratch[:, 0:7280].rearrange("p (k h e) -> p k h e", k=14, h=4)
            kcT = scratch[:, 7280:7280 + 2048].rearrange("p (h k) -> p h k", h=4)
            AQ, AK = Fv, BBv
            qrs, krs = [gran(12), gran(20)], [gran(13), gran(21)]
            qrns, krns = [gn(12), gn(20)], [gn(13), gn(21)]
            ET = [arena[:, 14 * 1280 + i * 3072: 14 * 1280 + (i + 1) * 3072].rearrange("p (k q) -> p k q", k=12)
                  for i in range(2)]
            ETn = [["ET0"], ["ET1"]]
            ptb = P[7][:, :].bitcast(BF16)
            pc = {'p': 0, 'mt': 0, 'a': 0, 'sp': 0, 'po': 0, 'wo': 0}
            gp = {'banks': [0, 1, 2, 3]}

            def next_bank():
                banks = gp['banks']
                if gp.get('by_ctx') is not None:
                    banks = gp['by_ctx'][gp['ctx']]
                k = banks[pc['p'] % len(banks)]
                pc['p'] += 1
                return k

            def next_mt():
                k = pc['mt'] % 4
                pc['mt'] += 1
                return k

            def load_slab(src_ap):
                bi = cnt['wb'] % NWB
                cnt['wb'] += 1
                tr.dma('pool', wbuf[bi][:], src_ap, "wb%d" % bi, writes=["wb%d" % bi])
                return bi

            def vproj_and_caches():
                import os
                DBG = os.environ.get("KDBG", "")
                if "a" not in DBG:
                    tr.dma('pool', kcT, kcT_d, "kc", writes=["kcT"] + FFN_SCR)
                if "b" not in DBG:
                    tr.dma('pool', Vaug[:, 0:4, :, 0:128], vc_d, "vc", writes=["Vaug.c"] + FFN_SCR)
                if "c" not in DBG:
                    tr.op('dve', lambda e: e.memset(Vaug[:, :, :, 128:130], 1.0), writes=["Vaug.1"] + FFN_SCR)

                wav_v = wav_d.rearrange("p k (s n) -> s p k n", s=2)
                for hh in range(0 if "d" in DBG else 2):
                    bi = load_slab(wav_v[hh])
                    for ti in range(10):
                        k = 5 + pc['po'] % 2
                        pc['po'] += 1

                        def f(e, k=k, bi=bi, ti=ti):
                            ins = None
                            for kc in range(8):
                                ins = e.matmul(P[k][:, 0:256], hT[:, kc, ti * 128:(ti + 1) * 128], wbuf[bi][:, kc, :],
                                               start=(kc == 0), stop=(kc == 7))
                            return ins
                        tr.op('pe', f, reads=["wb%d" % bi] + ["hT.%d.%d" % (min(ti // 4, 2), c_) for c_ in range(8)], writes=["P%d" % k])
                        m_ = next_mt()
                        tr.op('act', lambda e, k=k, m_=m_: e.copy(out=mt[m_][:, 0:256], in_=P[k][:, 0:256]),
                              reads=["P%d" % k], writes=["mt%d" % m_])
                        tr.dma('sp', vo_d[:, ti, hh * 256:(hh + 1) * 256], mt[m_][:, 0:256], "out_v%d" % m_, reads=["mt%d" % m_])
                        tr.op('dve', lambda e, k=k, ti=ti, hh=hh: e.tensor_copy(
                            out=Vaug[:, 4 + ti, 2 * hh:2 * hh + 2, 0:128],
                            in_=P[k][:, 0:256].rearrange("p (h e) -> p h e", h=2)),
                            reads=["P%d" % k], writes=["Vaug.%d" % ti] + FFN_SCR)
                    yield


            def proj(ci, evac):
                if getattr(proj, "cur", None) != ci // 2:
                    proj.bi = load_slab(wmix_d[ci // 2])
                    proj.cur = ci // 2
                bi, half = proj.bi, ci % 2
                for ti, (t0, N, _) in enumerate(TT):
                    k = next_bank()

                    def f(e, k=k, bi=bi):
                        ins = None
                        for kc in range(8):
                            ins = e.matmul(P[k][:, 0:N], wbuf[bi][:, kc, half * 128:(half + 1) * 128],
                                           hT[:, kc, t0:t0 + N], start=(kc == 0), stop=(kc == 7))
                        return ins
                    tr.op('pe', f, reads=["wb%d" % bi] + ["hT.%d.%d" % (ti, c_) for c_ in range(8)], writes=["P%d" % k])
                    evac(P[k], "P%d" % k, t0, N, ti)
                    yield

            def transposes(srcT, src_names, dst, dst_names):
                for g0 in range(0, NCH, 8):
                    n = min(8, NCH - g0)
                    kt = next_bank()
                    ptk = P[kt][:, :].bitcast(BF16)

                    def f(e, g0=g0, n=n, ptk=ptk):
                        ins = None
                        for c in range(n):
                            ins = e.transpose(out=ptk[0:64, c * 128:(c + 1) * 128],
                                              in_=srcT[:, (g0 + c) * CH:(g0 + c + 1) * CH], identity=ident_bf[:])
                        return ins
                    tr.op('pe', f, reads=src_names + ["ident"], writes=["P%d" % kt])
                    tr.op('act', lambda e, g0=g0, n=n, ptk=ptk: e.copy(
                        out=dst[:, g0:g0 + n, :], in_=ptk[0:64, 0:n * 128].rearrange("p (a b) -> p a b", a=n)),
                        reads=["P%d" % kt], writes=dst_names)

            def prep(d, h):
                Fx, BBx, EEx, KKx = (Fv, BBv, EEv, KK) if d == 0 else (F2v, BB2v, EE2v, KK2)
                Fn, BBn, EEn, KKn = (gn(8, 2), gn(10, 2), gn(12, 2), ["KK"]) if d == 0 else (["F2"], ["BB2"], ["EE2"], ["KK2"])
                KH = gran(12) if d == 0 else scratch[:, 5120:6400]
                tr.op('dve', lambda e: e.tensor_scalar(out=KKx, in0=Fx, scalar1=-1.0, scalar2=1.0, op0=ALU.mult, op1=ALU.add),
                      reads=Fn, writes=KKn)
                yield
                tr.op('act', lambda e: e.activation(out=Fx, in_=Fx, func=AF.Ln), reads=Fn, writes=Fn)
                yield
                tr.op('dve', lambda e: e.tensor_tensor_scan(out=BBx, data0=mres[:], data1=Fx, initial=0.0,
                                                            op0=ALU.mult, op1=ALU.add),
                      reads=Fn + ["mres"], writes=BBn)
                yield
                if d == 1:
                    tr.op('dve', lambda e: e.tensor_copy(out=bend[:], in_=BBx[:, CH - 1::CH]), reads=BBn, writes=["bend"])
                    tr.op('dve', lambda e: e.tensor_tensor(out=Fx, in0=Fx, in1=BBx, op=ALU.subtract),
                          reads=Fn + BBn, writes=Fn)
                    yield
                    tr.op('dve', lambda e: e.tensor_tensor(
                        out=BBx.rearrange("p (a b) -> p a b", a=NCH), in0=Fx.rearrange("p (a b) -> p a b", a=NCH),
                        in1=bend[:].unsqueeze(2).to_broadcast([128, NCH, CH]), op=ALU.add),
                        reads=Fn + ["bend"], writes=BBn)
                    yield
                tr.op('act', lambda e: e.activation(out=EEx, in_=BBx, func=AF.Exp), reads=BBn, writes=EEn)
                yield
                col = (CH - 1) if d == 0 else 0
                tr.op('dve', lambda e: e.tensor_copy(out=dec[d][:], in_=EEx[:, col::CH]), reads=EEn, writes=["dec%d" % d])
                tr.op('pool', lambda e: e.tensor_tensor(out=Qd[d], in0=QT, in1=EEx, op=ALU.mult),
                      reads=EEn + ["QT"], writes=gn(14 + d))
                yield
                tr.op('act', lambda e: e.activation(out=EEx, in_=BBx, func=AF.Exp, scale=-1.0), reads=BBn, writes=EEn)
                yield
                tr.op('dve', lambda e: e.tensor_tensor(out=KTf[d], in0=KKx, in1=EEx, op=ALU.mult),
                      reads=EEn + KKn, writes=["KTf%d" % d])
                yield
                tr.op('pool', lambda e: e.tensor_tensor(
                    out=KH.rearrange("p (a b) -> p a b", a=NCH), in0=KTf[d].rearrange("p (a b) -> p a b", a=NCH),
                    in1=dec[d][:].unsqueeze(2).to_broadcast([128, NCH, CH]), op=ALU.mult),
                    reads=["KTf%d" % d, "dec%d" % d], writes=EEn)
                yield
                transposes(KH, EEn, Ktok[d], gn(16 + 2 * d, 2))

            KTf = [mtbig[:, 0:1280], mtbig[:, 1280:2560]]

            def chains(h):
                SfT = lambda buf, d: Sfm[:, buf, d, :]
                SbT = lambda buf, d: Sbm[:, buf, d, :]
                tr.op('dve', lambda e: e.tensor_copy(out=SfT(0, 0), in_=s0in[:, 0, h, :]), reads=["s0in"], writes=["Sf00"])
                tr.op('dve', lambda e: e.memset(SfT(0, 1), 0.0), writes=["Sf10"])
                tr.op('act', lambda e: e.copy(out=Sbm[:, 0, :, :], in_=Sfm[:, 0, :, :]), reads=["Sf00", "Sf10"], writes=["Sb00", "Sb10"])

                def cidx(s, d):
                    return s if d == 0 else NCH - 1 - s

                def stage1(s):
                    ks, kS = 4 + s % 2, 6 + s % 2

                    def f(e):
                        ins = None
                        for d in range(2):
                            c = cidx(s, d)
                            cs = slice(c * CH, (c + 1) * CH)
                            e.matmul(P[ks][0:64, d * 64:(d + 1) * 64], KTf[d][:, cs], Qd[d][:, cs], start=True, stop=True)
                            ins = e.matmul(P[kS][:, d * 128:(d + 1) * 128], Ktok[d][:, c, :], Vtok[:, c, :], start=True, stop=True)
                        return ins
                    tr.op('pe', f, reads=["KTf0", "KTf1"] + gn(14, 8), writes=["P%d" % ks, "P%d" % kS])

                def stage2(s):
                    ks, kS = 4 + s % 2, 6 + s % 2
                    a, b = s % 2, 1 - s % 2
                    tr.op('dve', lambda e: e.tensor_tensor(
                        out=Asm[:, s % 2, :, :], in0=P[ks][0:64, 0:128].rearrange("p (d t) -> p d t", d=2), in1=tri[:, :, :], op=ALU.mult),
                        reads=["P%d" % ks, "tri"], writes=["A%d" % (s % 2)])
                    for d in range(2):
                        c = cidx(s, d)
                        blk = c // 4
                        sfn, sbn = "Sf%d%d" % (d, a), "Sb%d%d" % (d, a)
                        enter = (d == 0 and c % 4 == 0 and c > 0) or (d == 1 and c % 4 == 3 and c < NCH - 1)
                        if enter:
                            if d == 0 and blk == 4:
                                tr.op('dve', lambda e, d=d: e.memset(SfT(a, d), 0.0), writes=[sfn])
                                tr.op('dve', lambda e, d=d: e.memset(SbT(a, d), 0.0), writes=[sbn])
                            elif d == 1 and blk == 3:
                                tr.op('dve', lambda e, d=d: e.tensor_copy(out=SfT(a, d), in_=s0in[:, 1, h, :]), reads=["s0in"], writes=[sfn])
                                tr.op('act', lambda e, d=d: e.copy(out=SbT(a, d), in_=s0in[:, 1, h, :]), reads=["s0in"], writes=[sbn])
                            else:
                                fl = flags[:, blk:blk + 1] if d == 0 else flags[:, 5 + blk:6 + blk]
                                tr.op('dve', lambda e, d=d, fl=fl: e.tensor_scalar(out=SfT(a, d), in0=SfT(a, d), scalar1=fl, scalar2=None, op0=ALU.mult),
                                      reads=[sfn, "flags"], writes=[sfn])
                                tr.op('dve', lambda e, d=d, fl=fl: e.tensor_scalar(out=SbT(a, d), in0=SbT(a, d), scalar1=fl, scalar2=None, op0=ALU.mult),
                                      reads=[sbn, "flags"], writes=[sbn])
                        tr.op('dve', lambda e, d=d, c=c: e.scalar_tensor_tensor(
                            out=SfT(b, d), in0=SfT(a, d), scalar=dec[d][:, c:c + 1], in1=P[kS][:, d * 128:(d + 1) * 128],
                            op0=ALU.mult, op1=ALU.add),
                            reads=["P%d" % kS, sfn, "dec%d" % d], writes=["Sf%d%d" % (d, b)])
                    tr.op('act', lambda e: e.copy(out=Sbm[:, b, :, :], in_=Sfm[:, b, :, :]),
                          reads=["Sf0%d" % b, "Sf1%d" % b], writes=["Sb0%d" % b, "Sb1%d" % b])
                    for d in range(2):
                        c = cidx(s, d)
                        if (d == 0 and c % 4 == 3) or (d == 1 and c % 4 == 0):
                            tr.dma('sp', so_d[:, c // 4, d, h, :], SfT(b, d), "out_s%d%d" % (d, b), reads=["Sf%d%d" % (d, b)])

                def stage3(s):
                    a = s % 2
                    for d in range(2):
                        c = cidx(s, d)
                        cs = slice(c * CH, (c + 1) * CH)
                        ti = min(c // 8, 2)
                        pk = d
                        po = P[pk][:, (c % 8) * CH:(c % 8 + 1) * CH]

                        def fo(e, d=d, c=c, po=po, cs=cs):
                            e.matmul(po, Vtok[:, c, :], Asm[:, s % 2, d, :], start=True, stop=False)
                            return e.matmul(po, SbT(a, d), Qd[d][:, cs], start=False, stop=True)
                        tr.op('pe', fo, reads=["A%d" % (s % 2), "Sb%d%d" % (d, a)] + gn(20, 2) + gn(14 + d), writes=["P%d" % pk])
                        if (d == 0 and c in (7, 15, 19)) or (d == 1 and c in (16, 8, 0)):
                            t0, N, _ = TT[ti]
                            first = (d == 0) if ti == 0 else (d == 1)
                            if first:
                                tr.op('act', lambda e, pk=pk, t0=t0, N=N: e.copy(out=oF[:, t0:t0 + N], in_=P[pk][:, 0:N]),
                                      reads=["P%d" % pk], writes=["oF.%d" % ti])
                            else:
                                tr.op('dve', lambda e, pk=pk, t0=t0, N=N: e.tensor_tensor(
                                    out=oF[:, t0:t0 + N], in0=P[pk][:, 0:N], in1=oF[:, t0:t0 + N], op=ALU.add),
                                    reads=["P%d" % pk, "oF.%d" % ti], writes=["oF.%d" % ti])

                stage1(0)
                for s in range(NCH):
                    if s + 1 < NCH:
                        stage1(s + 1)
                    stage2(s)
                    stage3(s)
                    yield

            def group_norm(src, src_names, ti, t0, N, ones_mat, ones_name, inv_n, pbank):
                pbank = next_bank()
                tr.op('dve', lambda e: e.tensor_tensor(out=mtb[0][:, 0:N], in0=src[:, t0:t0 + N], in1=src[:, t0:t0 + N], op=ALU.mult),
                      reads=src_names, writes=["mtb0"])
                tr.op('pe', lambda e: e.matmul(P[pbank][:, 0:N], ones_mat, mtb[0][:, 0:N], start=True, stop=True),
                      reads=["mtb0", ones_name], writes=["P%d" % pbank])
                m1 = next_mt()
                if False and ones_name == "blk":
                    tr.op('dve', lambda e: e.tensor_scalar(out=mt[m1][:, 0:N], in0=P[pbank][:, 0:N], scalar1=inv_n, scalar2=EPS,
                                                           op0=ALU.mult, op1=ALU.add),
                          reads=["P%d" % pbank], writes=["mt%d" % m1])
                    tr.op('pool', lambda e: e.tensor_tensor(out=mt[m1][:, 0:N], in0=mt[m1][:, 0:N], in1=mhalf[:, 0:N], op=ALU.pow),
                          reads=["mt%d" % m1, "mhalf"], writes=["mt%d" % m1])
                    return m1
                tr.op('act', lambda e: e.activation(out=mt[m1][:, 0:N], in_=P[pbank][:, 0:N], func=AF.Ln, bias=epsc[:, 0:1], scale=inv_n),
                      reads=["P%d" % pbank, "epsc"], writes=["mt%d" % m1])
                tr.op('act', lambda e: e.activation(out=mt[m1][:, 0:N], in_=mt[m1][:, 0:N], func=AF.Exp, scale=-0.5),
                      reads=["mt%d" % m1], writes=["mt%d" % m1])
                return m1

            def group_norm_g(src, src_names, ti, t0, N, ones_mat, ones_name, inv_n, out):
                pbank = next_bank()
                tr.op('pool', lambda e: e.tensor_tensor(out=mtb[0][:, 0:N], in0=src[:, t0:t0 + N], in1=src[:, t0:t0 + N], op=ALU.mult),
                      reads=src_names, writes=["mtb0"])
                yield
                tr.op('pe', lambda e: e.matmul(P[pbank][:, 0:N], ones_mat, mtb[0][:, 0:N], start=True, stop=True),
                      reads=["mtb0", ones_name], writes=["P%d" % pbank])
                yield
                m1 = next_mt()
                tr.op('act', lambda e: e.activation(out=mt[m1][:, 0:N], in_=P[pbank][:, 0:N], func=AF.Ln, bias=epsc[:, 0:1], scale=inv_n),
                      reads=["P%d" % pbank, "epsc"], writes=["mt%d" % m1])
                tr.op('act', lambda e: e.activation(out=mt[m1][:, 0:N], in_=mt[m1][:, 0:N], func=AF.Exp, scale=-0.5),
                      reads=["mt%d" % m1], writes=["mt%d" % m1])
                out.append(m1)
                yield

            def orec_final(h):
                for ti, (t0, N, _) in enumerate(TT):
                    m1 = group_norm(oF, ["oF.%d" % ti], ti, t0, N, ones_bf[:], "ones", 1.0 / 128, 4)
                    m2 = next_mt()
                    tr.op('dve', lambda e: e.scalar_tensor_tensor(out=mt[m2][:, 0:N], in0=oF[:, t0:t0 + N], scalar=small[:, 0:1],
                                                                  in1=mt[m1][:, 0:N], op0=ALU.mult, op1=ALU.mult),
                          reads=["oF.%d" % ti, "small", "mt%d" % m1], writes=["mt%d" % m2])
                    tr.op('dve', lambda e: e.tensor_tensor(out=oT[:, h, t0:t0 + N], in0=mt[m2][:, 0:N], in1=GS[:, t0:t0 + N], op=ALU.mult),
                          reads=["mt%d" % m2, "GS"], writes=["g%d" % h])
                    yield

            def qk_norm_rope(X, xg, gcol, dst, dstn, is_k, h):
                for ti, (t0, N, _) in enumerate(TT):
                    res = []
                    yield from group_norm_g(X, gn(xg, 2), ti, t0, N, blk_bf[:], "blk", 1.0 / 64, res)
                    m1 = res[0]
                    tr.op('dve', lambda e: e.scalar_tensor_tensor(out=X[:, t0:t0 + N], in0=X[:, t0:t0 + N], scalar=small[:, gcol:gcol + 1],
                                                                  in1=mt[m1][:, 0:N], op0=ALU.mult, op1=ALU.mult),
                          reads=gn(xg, 2) + ["small", "mt%d" % m1], writes=gn(xg, 2))
                    yield
                    if is_k:
                        tr.dma('sp', kTo_d[:, h, t0:t0 + N], X[:, t0:t0 + N], "out_k%d" % ti, reads=gn(xg, 2))
                    if ti == 2:
                        tr.op('act', lambda e: e.copy(out=dst[:, t0:t0 + N], in_=X[:, t0:t0 + N]), reads=gn(xg, 2), writes=dstn)
                        yield
                        continue
                    tr.op('dve', lambda e: e.tensor_copy(out=mtb[1][:, 0:N], in_=X[:, t0:t0 + N]), reads=gn(xg, 2), writes=["mtb1"])
                    m3 = next_mt()
                    tr.op('pool', lambda e: e.tensor_tensor(out=mt[m3][:, 0:N], in0=X[:, t0:t0 + N], in1=ropeC[:, t0:t0 + N], op=ALU.mult),
                          reads=gn(xg, 2) + ["ropeC"], writes=["mt%d" % m3])
                    yield
                    kp = next_bank()
                    tr.op('pe', lambda e: e.matmul(P[kp][:, 0:N], perm_bf[:], mtb[1][:, 0:N], start=True, stop=True),
                          reads=["mtb1", "perm"], writes=["P%d" % kp])
                    yield
                    m2 = next_mt()
                    tr.op('dve', lambda e: e.tensor_tensor(out=mt[m2][:, 0:N], in0=P[kp][:, 0:N], in1=ropeS[:, t0:t0 + N], op=ALU.mult),
                          reads=["P%d" % kp, "ropeS"], writes=["mt%d" % m2])
                    yield
                    tr.op('pool', lambda e: e.tensor_tensor(out=dst[:, t0:t0 + N], in0=mt[m2][:, 0:N], in1=mt[m3][:, 0:N], op=ALU.add),
                          reads=["mt%d" % m2, "mt%d" % m3], writes=dstn)
                    yield

            def attention(h, st):
                qr, kr, qrn, krn = qrs[st], krs[st], qrns[st], krns[st]
                blocks = []
                for qb in range(4):
                    specs = [(kcT[:, h, kc * 128:(kc + 1) * 128], ["kcT"], kc) for kc in range(4)]
                    specs += [(kr[:, j * 128:(j + 1) * 128], krn, 4 + j) for j in range(8)]
                    blocks.append((qb * 256, specs, (lambda ki, qb=qb: ki * 4 + qb)))
                specs = [(kr[:, (8 + j) * 128:(9 + j) * 128], krn, 12 + j) for j in range(2)]
                blocks.append((1024, specs, None))
                units = [(bi, m) for bi in range(len(blocks)) for m in range(2)]
                vnames = ["Vaug.c", "Vaug.1"] + ["Vaug.%d" % i for i in range(10)]

                def qk(ui, ki):
                    bi, m = units[ui]
                    q0, specs, maskcol = blocks[bi]
                    kap, knames, vidx = specs[ki]
                    pr = slice(64 * m, 64 * m + 64)
                    k = next_bank()
                    tr.op('pe', lambda e: e.matmul(P[k][:, 0:256], kap[pr, :], qr[pr, q0:q0 + 256], start=True, stop=True),
                          reads=knames + qrn, writes=["P%d" % k])
                    bias = maskb[:, maskcol(ki):maskcol(ki) + 1] if maskcol is not None else 0.0
                    extra = gn(14, 6) if (ui < 2 and ki == 0) else []
                    tr.op('act', lambda e: e.activation(out=ET[ui % 2][:, ki, :], in_=P[k][:, 0:256], func=AF.Exp,
                                                        bias=bias, scale=0.125),
                          reads=["P%d" % k, "maskb"], writes=["ET%d.%d" % (ui % 2, ki)] + extra)

                def pv(ui, ki):
                    bi, m = units[ui]
                    q0, specs, maskcol = blocks[bi]
                    nk = len(specs)
                    vidx = specs[ki][2]
                    kb = 4 + 2 * (ui % 2)

                    def f(e):
                        e.matmul(P[kb][:, 0:130], ET[ui % 2][:, ki, 0:128], Vaug[:, vidx, h, :], start=(ki == 0), stop=(ki == nk - 1))
                        return e.matmul(P[kb + 1][:, 0:130], ET[ui % 2][:, ki, 128:256], Vaug[:, vidx, h, :],
                                        start=(ki == 0), stop=(ki == nk - 1))
                    extra = gn(14, 6) if (ui >= len(units) - 2 and ki == nk - 1) else []
                    tr.op('pe', f, reads=["ET%d.%d" % (ui % 2, ki)] + vnames + extra, writes=["P%d" % kb, "P%d" % (kb + 1)])

                def post_pv(ui):
                    bi, m = units[ui]
                    kb = 4 + 2 * (ui % 2)
                    for qs in range(2):
                        k = kb + qs
                        si = pc['a'] % 2
                        pc['a'] += 1
                        tr.op('dve', lambda e, k=k, si=si: e.reciprocal(out=sm[si][:, 0:1], in_=P[k][:, 128:129]),
                              reads=["P%d" % k], writes=["sm%d" % si])
                        tr.op('dve', lambda e, k=k, si=si, qs=qs: e.tensor_scalar(
                            out=om[m][qs][:], in0=P[k][:, 0:128], scalar1=sm[si][:, 0:1], scalar2=None, op0=ALU.mult),
                            reads=["P%d" % k, "sm%d" % si], writes=["om%d%d" % (m, qs)])

                def post_block(bi):
                    q0 = blocks[bi][0]
                    for qs in range(2):
                        tr.op('dve', lambda e, qs=qs: e.scalar_tensor_tensor(out=odt[qs][:], in0=om[1][qs][:], scalar=lamt[:, 2:3],
                                                                            in1=om[0][qs][:], op0=ALU.mult, op1=ALU.add),
                              reads=["om0%d" % qs, "om1%d" % qs, "lamt"], writes=["od%d" % qs])
                        tr.op('dve', lambda e, qs=qs: e.scalar_tensor_tensor(out=sqj[:], in0=odt[qs][:], scalar=1.0, in1=odt[qs][:],
                                                                            op0=ALU.mult, op1=ALU.mult, accum_out=sm[qs][:, 1:2]),
                              reads=["od%d" % qs], writes=["sqj", "smq%d" % qs])
                        tr.op('act', lambda e, qs=qs: e.activation(out=sm[qs][:, 2:3], in_=sm[qs][:, 1:2], func=AF.Ln, bias=epsc[:, 0:1], scale=1.0 / 128),
                              reads=["smq%d" % qs, "epsc"], writes=["smq%d" % qs])
                        tr.op('act', lambda e, qs=qs: e.activation(out=sm[qs][:, 3:4], in_=sm[qs][:, 2:3], func=AF.Exp, scale=-0.5),
                              reads=["smq%d" % qs], writes=["smq%d" % qs])
                        tr.op('dve', lambda e, qs=qs: e.scalar_tensor_tensor(out=onb[qs][:], in0=odt[qs][:], scalar=sm[qs][:, 3:4],
                                                                            in1=subln[:], op0=ALU.mult, op1=ALU.mult),
                              reads=["od%d" % qs, "smq%d" % qs, "subln"], writes=["on%d" % qs])

                def post_block_b(bi):
                    q0 = blocks[bi][0]
                    k = next_bank()
                    pt = P[k][:, :].bitcast(BF16)

                    def f(e):
                        e.transpose(out=pt[:, 0:128], in_=onb[0][:], identity=ident_bf[:])
                        return e.transpose(out=pt[:, 128:256], in_=onb[1][:], identity=ident_bf[:])
                    tr.op('pe', f, reads=["on0", "on1", "ident"], writes=["P%d" % k])
                    tr.op('dve', lambda e: e.tensor_copy(out=oT[:, 4 + h, q0:q0 + 256], in_=pt[:, 0:256]),
                          reads=["P%d" % k], writes=["g%d" % (4 + h)])

                def nkeys(ui):
                    return len(blocks[units[ui][0]][1])

                pending, pending_a = [], []
                for ui in range(len(units) + 1):
                    yield
                    nu = nkeys(ui) if ui < len(units) else 0
                    npv = nkeys(ui - 1) if ui > 0 else 0
                    for ki in range(max(nu, npv)):
                        if ki and ki % 2 == 0:
                            yield
                        if ki < nu:
                            qk(ui, ki)
                        if ki < npv:
                            pv(ui - 1, ki)
                    if ui > 0:
                        for pb in pending:
                            post_block_b(pb)
                        pending.clear()
                        for pa in pending_a:
                            post_block(pa)
                            pending.append(pa)
                        pending_a.clear()
                        post_pv(ui - 1)
                        if units[ui - 1][1] == 1:
                            pending_a.append(units[ui - 1][0])
                for pa in pending_a:
                    post_block(pa)
                    pending.append(pa)
                for pb in pending:
                    post_block_b(pb)

            def run(*gens, weights=None, set_ctx=True):
                gens = list(gens)
                w = dict(zip([id(g) for g in gens], weights or [1] * len(gens)))
                order = {id(g): i for i, g in enumerate(gens)}
                while gens:
                    for g in list(gens):
                        for _ in range(w[id(g)]):
                            try:
                                if set_ctx:
                                    gp['ctx'] = order[id(g)]
                                next(g)
                            except StopIteration:
                                gens.remove(g)
                                break

            def seq(*gens):
                for g in gens:
                    yield from g

            def evac_copy(dst, dstn, eng='act'):
                if eng == 'act':
                    return lambda ps, pn, t0, N, ti: tr.op(
                        'act', lambda e: e.copy(out=dst[:, t0:t0 + N], in_=ps[:, 0:N]), reads=[pn], writes=dstn)
                return lambda ps, pn, t0, N, ti: tr.op(
                    'dve', lambda e: e.tensor_copy(out=dst[:, t0:t0 + N], in_=ps[:, 0:N]), reads=[pn], writes=dstn)

            def hgrn_front(h):
                base = 7 * h
                yield from proj(base + 0, evac_copy(QT, ["QT"]))
                for d in range(2):
                    def evf(ps, pn, t0, N, ti, h=h, d=d):
                        m1 = next_mt()
                        Fx, Fn = (Fv, gn(8, 2)) if d == 0 else (F2v, ["F2"])
                        tr.op('act', lambda e: e.activation(out=mt[m1][:, 0:N], in_=ps[:, 0:N], func=AF.Sigmoid),
                              reads=[pn], writes=["mt%d" % m1])
                        tr.op('dve', lambda e: e.tensor_scalar(out=Fx[:, t0:t0 + N], in0=mt[m1][:, 0:N], scalar1=oml[:, h:h + 1],
                                                               scalar2=lb[:, h:h + 1], op0=ALU.mult, op1=ALU.add),
                              reads=["mt%d" % m1, "oml", "lb"], writes=Fn)
                    yield from proj(base + 2 + d, evf)

                def rest():
                    VT = gran(16)
                    yield from proj(base + 1, evac_copy(VT, gn(16)))
                    transposes(VT, gn(16), Vtok, gn(20, 2))
                    yield
                    yield from proj(base + 4, lambda ps, pn, t0, N, ti: tr.op(
                        'act', lambda e: e.activation(out=GS[:, t0:t0 + N], in_=ps[:, 0:N], func=AF.Silu), reads=[pn], writes=["GS"]))
                run(rest(), prep(0, h), prep(1, h), set_ctx=False)
                yield

            def attn_front(h, st):
                base = 7 * h
                yield from proj(base + 5, evac_copy(AQ, gn(8, 2), 'dve'))
                yield from proj(base + 6, evac_copy(AK, gn(10, 2), 'dve'))
                yield from qk_norm_rope(AQ, 8, 1, qrs[st], qrns[st], False, h)
                yield from qk_norm_rope(AK, 10, 2, krs[st], krns[st], True, h)

            for h in range(4):
                if h == 0:
                    run(hgrn_front(h))
                if h == 3:
                    gp['by_ctx'] = {0: [2, 3], 1: [2, 3]}
                    run(chains(h), attn_front(0, 0), weights=[1, 2])
                    gp['by_ctx'] = None
                else:
                    run(chains(h))
                if h < 3:
                    run(hgrn_front(h + 1), orec_final(h))
                else:
                    run(orec_final(h))
            barrier()
            run(vproj_and_caches())
            gp['banks'] = [0, 1, 2, 3]
            for h in range(4):
                gens = [attention(h, h % 2)]
                if h < 3:
                    gens.append(attn_front(h + 1, (h + 1) % 2))
                gp['by_ctx'] = {0: [0, 1, 2], 1: [3]} if h < 3 else None
                run(*gens, weights=[1, 1])
                gp['by_ctx'] = None

            for o in range(8):
                bi = cnt['wo'] % 2
                cnt['wo'] += 1
                tr.dma('pool', wobuf[bi][:, 0:8, :], wmo_d[o], "wo%d" % bi, writes=["wo%d" % bi, "QT", "GS", "KK"])
                for ti, (t0, N, ci) in enumerate(TT):
                    k = 5 + cnt['oo'] % 2
                    cnt['oo'] += 1

                    def f(e, k=k, bi=bi):
                        ins = None
                        for kc in range(8):
                            ins = e.matmul(P[k][:, 0:N], wobuf[bi][:, kc, :], oT[:, kc, t0:t0 + N], start=(kc == 0), stop=(kc == 7))
                        return ins
                    tr.op('pe', f, reads=["wo%d" % bi] + gn(0, 8), writes=["P%d" % k])
                    tr.op('dve', lambda e, k=k, o=o: e.scalar_tensor_tensor(
                        out=xT[:, o, t0:t0 + N], in0=P[k][:, 0:N], scalar=Gmod[:, 1, ci, o:o + 1],
                        in1=xT[:, o, t0:t0 + N], op0=ALU.mult, op1=ALU.add),
                        reads=["P%d" % k, "Gmod", "xT.%d.%d" % (o, ti)], writes=["xT.%d.%d" % (o, ti)])
            barrier()

        mtbig = sb("mtbig", [128, 2560], BF16)
        load_mixer_consts()

        def ada_hook(j):
            for sbi in ([0, 1] if j == 0 else [j + 1]):
                if sbi < ADA_NSLAB:
                    ada_slab(sbi)
        ffn(0, w1in_d, w1out_d, hook=(ada_hook if STAGE >= 2 else None))
        if STAGE >= 2:
            mods_finalize(1)
            mods_finalize(2)
        if STAGE >= 2:
            mixer()
        if STAGE >= 3:
            ffn(2, w2in_d, w2out_d)

        if STAGE < 3:
            for c in range(8):
                tr.dma('sp', yT_d[:, c, :], xT[:, c, :], "out_y", reads=["xT.%d.%d" % (c, t) for t in range(3)])
        tr.finish()
    return nc


def _fm(x2d):
    t, f = x2d.shape
    return np.ascontiguousarray(x2d.T.reshape(f // 128, 128, t).transpose(1, 0, 2))


def _host_consts():
    c = {}
    perm = np.zeros((128, 128), np.float32)
    for m in range(128):
        r = m % 32
        partner = m + 16 if r < 16 else m - 16
        perm[partner, m] = 1.0
    c["perm"] = perm
    blk = np.zeros((128, 128), np.float32)
    blk[:64, :64] = 1.0
    blk[64:, 64:] = 1.0
    c["blk64"] = blk
    c["ident"] = np.eye(128, dtype=np.float32)
    tri = np.zeros((64, 2, 64), np.float32)
    s = np.arange(64)[:, None]
    t = np.arange(64)[None, :]
    tri[:, 0, :] = (s <= t)
    tri[:, 1, :] = (s >= t)
    c["tri"] = tri
    mres = np.ones((128, T), np.float32)
    mres[:, ::CH] = 0.0
    c["mres"] = mres
    return c


def _rope_tables(sample):
    C = np.ones((128, T), np.float32)
    S = np.zeros((128, T), np.float32)
    if sample:
        n = 1024
        half = 32
        inv = (10000.0 ** (-np.arange(0, half, 2, dtype=np.float32) / half)).astype(np.float32)
        pos_row = np.repeat(np.arange(n // 64, dtype=np.float32), 64)
        pos_col = np.tile(np.arange(64, dtype=np.float32), n // 64)
        ar = (pos_row[:, None] * inv).astype(np.float32)
        ac = (pos_col[:, None] * inv).astype(np.float32)
        cr, sr, cc, sc = [a.astype(np.float32) for a in (np.cos(ar), np.sin(ar), np.cos(ac), np.sin(ac))]
        for m in range(2):
            b = 64 * m
            C[b + 0:b + 16, :n] = cr.T
            C[b + 16:b + 32, :n] = cr.T
            C[b + 32:b + 48, :n] = cc.T
            C[b + 48:b + 64, :n] = cc.T
            S[b + 0:b + 16, :n] = -sr.T
            S[b + 16:b + 32, :n] = sr.T
            S[b + 32:b + 48, :n] = -sc.T
            S[b + 48:b + 64, :n] = sc.T
    return C, S


def kernel(x_prompt, x_sample, c, cache_attn_k, cache_attn_v, state_hgrn, c_ctx,
           w_ada, b_ada, norm_ffn1, w_ffn1_in, w_ffn1_out, norm_mix, w_mix_in, w_mix_out,
           hgrn_lb_logits, hgrn_out_norm, attn_q_norm, attn_k_norm, attn_lambda, attn_subln,
           norm_ffn2, w_ffn2_in, w_ffn2_out):
    f = lambda a: np.asarray(a, np.float32)
    x_prompt, x_sample, c, c_ctx = f(x_prompt), f(x_sample), f(c), f(c_ctx)
    cache_attn_k, cache_attn_v, state_hgrn = f(cache_attn_k), f(cache_attn_v), f(state_hgrn)

    def ffn_in_layout(w):
        w = f(w)[0]
        g = w[:, :DFF].reshape(8, 128, NJ, 128)
        u = w[:, DFF:].reshape(8, 128, NJ, 128)
        return np.ascontiguousarray(np.concatenate([g, u], axis=3).transpose(2, 1, 0, 3))

    def out_layout(w, nk):
        w = f(w)[0] if w.ndim == 3 else f(w)
        return np.ascontiguousarray(w.reshape(nk, 128, 8, 128).transpose(2, 1, 0, 3))

    wm = f(w_mix_in)[0]
    order = []
    for h in range(4):
        order += [h, 4 + h, 8 + h, 12 + h, 16 + h, 20 + h, 24 + h]
    wm_chunks = wm.reshape(8, 128, 32, 128)
    wmix = wm_chunks[:, :, order, :].reshape(8, 128, 14, 256).transpose(2, 1, 0, 3)
    wav = wm[:, 3584:].reshape(8, 128, 512).transpose(1, 0, 2)

    shared = {
        "badaT": np.ascontiguousarray(f(b_ada)[0].reshape(72, 128).T),
        "gT": np.ascontiguousarray(np.stack([f(norm_ffn1)[0], f(norm_mix)[0], f(norm_ffn2)[0]])
                                   .reshape(3, 8, 128).transpose(2, 0, 1)),
        "w_ada": np.ascontiguousarray(f(w_ada)[0]),
        "w1in": ffn_in_layout(w_ffn1_in), "w1out": out_layout(w_ffn1_out, NJ),
        "w2in": ffn_in_layout(w_ffn2_in), "w2out": out_layout(w_ffn2_out, NJ),
        "wmix": np.ascontiguousarray(wmix), "wav": np.ascontiguousarray(wav),
        "wmo": out_layout(w_mix_out, 8),
        "lbl": np.ascontiguousarray(f(hgrn_lb_logits).reshape(2, 4, 128).transpose(2, 0, 1)),
        "small": np.ascontiguousarray(np.stack([
            f(hgrn_out_norm)[0], np.tile(f(attn_q_norm)[0], 2), np.tile(f(attn_k_norm)[0], 2),
            np.zeros(128, np.float32)], axis=1)),
        "lamp": np.ascontiguousarray(np.broadcast_to(f(attn_lambda)[0][None], (128, 4, 64))),
        "subln": np.ascontiguousarray(np.broadcast_to(f(attn_subln)[0][None], (128, 128))),
    }
    shared.update(_host_consts())
    ropeP = _rope_tables(False)
    ropeS_ = _rope_tables(True)

    in_maps = []
    for core in range(NCORES):
        sample = core < 2
        if sample:
            xs = np.concatenate([x_sample[core], x_prompt[30 + core]], axis=0)
            cond = np.stack([c[core], c_ctx])
        else:
            p0 = 5 * (core - 2)
            xs = x_prompt[p0:p0 + 5].reshape(T, D)
            cond = np.stack([c_ctx, c_ctx])
        m = dict(shared)
        m["xT"] = _fm(xs)
        m["condT"] = np.ascontiguousarray(cond.reshape(2, 8, 128).transpose(2, 1, 0))
        C_, S_ = ropeS_ if sample else ropeP
        m["ropeC"], m["ropeS"] = C_, S_
        NEG = -30000.0
        maskb = np.full((12, 4), NEG, np.float32)
        flags = np.zeros((128, 10), np.float32)
        if sample:
            maskb[:, :] = 0.0
            flags[:, 1:4] = 1.0
            flags[:, 5:8] = 1.0
            kc_ = cache_attn_k[core, 0]
            m["kcT"] = np.ascontiguousarray(kc_.reshape(512, 4, 128).transpose(2, 1, 0))
            m["vc"] = np.ascontiguousarray(cache_attn_v[core, 0].reshape(4, 128, 4, 128).transpose(1, 0, 2, 3))
            m["s0"] = np.ascontiguousarray(state_hgrn[core, 0].transpose(2, 0, 1, 3))
        else:
            for qb in range(4):
                maskb[4 + 2 * qb: 6 + 2 * qb, qb] = 0.0
            m["kcT"] = np.zeros((128, 4, 512), np.float32)
            m["vc"] = np.zeros((128, 4, 4, 128), np.float32)
            m["s0"] = np.zeros((128, 2, 4, 128), np.float32)
        m["maskb"] = np.ascontiguousarray(np.broadcast_to(maskb.reshape(1, 48), (128, 48)))
        m["flags"] = flags
        in_maps.append(m)

    if _ONLY_PREPARE:
        return in_maps
    nc = build_program()
    if _DEBUG_CORES:
        res = run_bass_kernel_spmd(nc, [in_maps[i] for i in _DEBUG_CORES], core_ids=list(range(len(_DEBUG_CORES))))
        return [res.results[0]["yT"]]
    res = run_bass_kernel_spmd(nc, in_maps, core_ids=list(range(NCORES)))
    return _assemble(res.results)


_ONLY_PREPARE = False
_DEBUG_CORES = None


def _assemble(R):
    y_prompt = np.zeros((32, 256, D), np.float32)
    y_sample = np.zeros((2, 1024, D), np.float32)
    new_k = np.zeros((32, 1, 256, 4, 2, 64), np.float32)
    new_v = np.zeros((32, 1, 256, 4, 128), np.float32)
    new_s = np.zeros((32, 1, 2, 4, 128, 128), np.float32)
    for core in range(NCORES):
        r = R[core]
        y = r["yT"].transpose(2, 1, 0).reshape(T, D)
        kk = r["kTo"].transpose(2, 1, 0).reshape(T, 4, 2, 64)
        vv = r["vo"].transpose(1, 0, 2).reshape(T, 4, 128)
        ss = r["so"].transpose(1, 2, 3, 0, 4)
        if core < 2:
            y_sample[core] = y[:1024]
            blocks = [(4, 30 + core)]
        else:
            blocks = [(b, 5 * (core - 2) + b) for b in range(5)]
        for b, pi in blocks:
            sl = slice(256 * b, 256 * (b + 1))
            y_prompt[pi] = y[sl]
            new_k[pi, 0] = kk[sl]
            new_v[pi, 0] = vv[sl]
            new_s[pi, 0] = ss[b]
    return (y_prompt, y_sample, new_k, new_v, new_s)
```

```python
import contextlib
import numpy as np
import concourse.bass as bass
import concourse.mybir as mybir
from concourse.bass_utils import run_bass_kernel_spmd

F32 = mybir.dt.float32
BF16 = mybir.dt.bfloat16
AF = mybir.ActivationFunctionType
ALU = mybir.AluOpType

NCORES = 8
D = 1024
T = 1280
NBLK = 5
DFF = 2816
NJ = DFF // 128
EPS = 1e-6
TT = [(0, 512, 0), (512, 512, 0), (1024, 256, 1)]
CH = 64
NCH = T // CH
LAM_INIT = 0.2
STAGE = 99
MIXL = 8


class Tr:
    def __init__(self, nc, stack):
        self.nc = nc
        self.stack = stack
        self.E = {'pe': nc.tensor, 'act': nc.scalar, 'dve': nc.vector, 'pool': nc.gpsimd, 'sp': nc.sync}
        self.sem = {k: stack.enter_context(nc.semaphore("sem_" + k)) for k in self.E}
        self.cnt = {k: 0 for k in self.E}
        self.lastw = {}
        self.rd = {}
        self.waited = {k: {} for k in self.E}
        self.dsem = {}

    def _wait(self, eng, dep, kind):
        if dep is None:
            return
        sem, val, src, name = dep
        if src == eng:
            if eng in ('pe', 'sp'):
                return
        if self.waited[eng].get(name, 0) >= val:
            return
        self.E[eng].wait_ge(sem, val)
        self.waited[eng][name] = val

    def _pre(self, eng, reads, writes):
        for r in reads:
            self._wait(eng, self.lastw.get(r), 'raw')
            if len(r) == 2 and r[0] == 'P':
                for d in self.rd.get(r, ()):
                    if d[2] != eng:
                        self._wait(eng, d, 'war')
        for w in writes:
            self._wait(eng, self.lastw.get(w), 'waw')
            for d in self.rd.get(w, ()):
                self._wait(eng, d, 'war')

    def _post(self, dep, reads, writes):
        for r in reads:
            self.rd.setdefault(r, []).append(dep)
        for w in writes:
            self.lastw[w] = dep
            self.rd[w] = []

    def op(self, eng, fn, reads=(), writes=()):
        self._pre(eng, reads, writes)
        ins = fn(self.E[eng])
        self.cnt[eng] += 1
        ins.then_inc(self.sem[eng], 1)
        self._post((self.sem[eng], self.cnt[eng], eng, eng), reads, writes)

    def dma(self, q, out, in_, semname, reads=(), writes=()):
        if semname not in self.dsem:
            self.dsem[semname] = [self.stack.enter_context(self.nc.semaphore("d_" + semname)), 0]
        ds = self.dsem[semname]
        self._pre(q, reads, writes)
        self.E[q].dma_start(out=out, in_=in_).then_inc(ds[0], 16)
        ds[1] += 16
        self._post((ds[0], ds[1], 'dma', "d_" + semname), reads, writes)

    def finish(self):
        for name, (sem, val) in self.dsem.items():
            if val:
                self.nc.sync.wait_ge(sem, val)


def build_program():
    nc = bass.Bass("TRN2", target_bir_lowering=False)

    def din(name, shape, dt=F32):
        return nc.dram_tensor(name, list(shape), dt, kind="ExternalInput").ap()

    def dout(name, shape, dt=F32):
        return nc.dram_tensor(name, list(shape), dt, kind="ExternalOutput").ap()

    xT_d = din("xT", [128, 8, T])
    condT_d = din("condT", [128, 8, 2])
    badaT_d = din("badaT", [128, 72])
    gT_d = din("gT", [128, 3, 8])
    wada_d = din("w_ada", [D, 9 * D])
    w1in_d = din("w1in", [NJ, 128, 8, 256])
    w1out_d = din("w1out", [8, 128, NJ, 128])
    w2in_d = din("w2in", [NJ, 128, 8, 256])
    w2out_d = din("w2out", [8, 128, NJ, 128])
    wmix_d = din("wmix", [14, 128, 8, 256])
    wav_d = din("wav", [128, 8, 512])
    wmo_d = din("wmo", [8, 128, 8, 128])
    lbl_d = din("lbl", [128, 2, 4])
    small_d = din("small", [128, 4])
    lamp_d = din("lamp", [128, 4, 64])
    subln_d = din("subln", [128, 128])
    ropeC_d = din("ropeC", [128, T])
    ropeS_d = din("ropeS", [128, T])
    perm_d = din("perm", [128, 128])
    blk64_d = din("blk64", [128, 128])
    ident_d = din("ident", [128, 128])
    tri_d = din("tri", [64, 2, 64])
    mres_d = din("mres", [128, T])
    maskb_d = din("maskb", [128, 48])
    flags_d = din("flags", [128, 10])
    kcT_d = din("kcT", [128, 4, 512])
    vc_d = din("vc", [128, 4, 4, 128])
    s0_d = din("s0", [128, 2, 4, 128])

    yT_d = dout("yT", [128, 8, T])
    kTo_d = dout("kTo", [128, 4, T])
    vo_d = dout("vo", [128, 10, 512])
    so_d = dout("so", [128, NBLK, 2, 4, 128])

    with contextlib.ExitStack() as st:
        tr = Tr(nc, st)

        def sb(name, shape, dt=F32):
            return st.enter_context(nc.sbuf_tensor(name, list(shape), dt))

        P = [st.enter_context(nc.psum_tensor("ps%d" % i, [128, 512], F32)) for i in range(8)]

        xT = sb("xT_s", [128, 8, T])
        hT = sb("hT_s", [128, 8, T], BF16)
        arena = sb("arena", [128, NJ * T], BF16)
        actT = arena[:, :].rearrange("p (j t) -> p j t", j=NJ)
        NWB = 3
        wbuf = [sb("wbuf%d" % i, [128, 8, 256], BF16) for i in range(NWB)]
        wobuf = [sb("wobuf%d" % i, [128, NJ, 128], BF16) for i in range(2)]
        condT = sb("condT_s", [128, 8, 2])
        scond = sb("scond", [128, 8, 2], BF16)
        badaT = sb("badaT_s", [128, 72])
        gT = sb("gT_s", [128, 3, 8])
        modT = sb("modT", [128, 72, 2])
        Amod = sb("Amod", [128, 3, 2, 8])
        Gmod = sb("Gmod", [128, 3, 2, 8])
        ones_bf = sb("ones_bf", [128, 128], BF16)
        scratch = sb("scratch", [128, 10240], BF16)
        sq = scratch[:, 0:4096].rearrange("p (c n) -> p c n", c=8)
        sd = scratch[:, 4096:5120].bitcast(F32)
        rstd = scratch[:, 5120:6144].bitcast(F32)
        tmpA = [scratch[:, 6144 + 1024 * i:7168 + 1024 * i].bitcast(F32) for i in range(2)]
        sg = [scratch[:, 8192 + 1024 * i:9216 + 1024 * i].bitcast(F32) for i in range(2)]

        tr.dma('sp', condT[:], condT_d, "in0", writes=["condT"])
        tr.dma('sp', badaT[:], badaT_d, "in0b", writes=["badaT"])
        tr.dma('sp', gT[:], gT_d, "in0c", writes=["gT"])
        for c in range(8):
            tr.dma('sp', xT[:, c, :], xT_d[:, c, :], "x%d" % c, writes=["xT.%d.%d" % (c, t) for t in range(3)])
        tr.op('dve', lambda e: e.memset(ones_bf[:], 1.0), writes=["ones"])
        epsc0 = sb("epsc0", [128, 1])
        tr.op('dve', lambda e: e.memset(epsc0[:], EPS), writes=["epsc0"])
        tr.op('act', lambda e: e.activation(out=scond[:], in_=condT[:], func=AF.Silu),
              reads=["condT"], writes=["scond"])

        RS = [rstd, sg[0], sg[1]]
        RSN = ["rstd", "sg0", "sg1"]

        def norm_stats():
            for ti, (t0, N, ci) in enumerate(TT):
                xr = ["xT.%d.%d" % (c, ti) for c in range(8)]
                tr.op('dve', lambda e: e.tensor_tensor(out=sq[:, 0:4, 0:N], in0=xT[:, 0:4, t0:t0 + N],
                                                       in1=xT[:, 0:4, t0:t0 + N], op=ALU.mult),
                      reads=xr[0:4], writes=["sq.a"])
                tr.op('act', lambda e: e.activation(out=sq[:, 4:8, 0:N], in_=xT[:, 4:8, t0:t0 + N], func=AF.Square),
                      reads=xr[4:8], writes=["sq.b"])

                def f(e):
                    ins = None
                    for c in range(8):
                        ins = e.matmul(P[4][:, 0:N], ones_bf[:], sq[:, c, 0:N], start=(c == 0), stop=(c == 7))
                    return ins
                tr.op('pe', f, reads=["sq.a", "sq.b", "ones"], writes=["P4"])
                tr.op('act', lambda e: e.activation(out=sd[:, 0:N], in_=P[4][:, 0:N], func=AF.Ln,
                                                    bias=epsc0[:, 0:1], scale=1.0 / D), reads=["P4", "epsc0"], writes=["sd"])
                tr.op('act', lambda e: e.activation(out=RS[ti][:, 0:N], in_=sd[:, 0:N], func=AF.Exp, scale=-0.5),
                      reads=["sd"], writes=[RSN[ti]])

        norm_stats()

        wada_v = wada_d.rearrange("(kc p) n -> p kc n", p=128)
        wada_buf = [arena[:, i * 9216:(i + 1) * 9216].rearrange("p (k n) -> p k n", k=8) for i in range(2)]
        psm = P[7][:, 0:144].rearrange("p (c i) -> p c i", i=2)
        for s in range(3):
            bi = s % 2
            tr.dma('pool', wada_buf[bi], wada_v[:, :, s * 1152:(s + 1) * 1152], "wada%d" % bi,
                   writes=["wada%d" % bi])

            def f(e, s=s, bi=bi):
                ins = None
                for cc in range(9):
                    ch = s * 9 + cc
                    for kc in range(8):
                        ins = e.matmul(psm[:, ch, :], wada_buf[bi][:, kc, cc * 128:(cc + 1) * 128],
                                       scond[:, kc, :], start=(kc == 0), stop=(kc == 7))
                return ins
            tr.op('pe', f, reads=["wada%d" % bi, "scond"], writes=["P7"])
        def mods_finalize(n):
            lo, hi = 24 * n, 24 * (n + 1)
            for i in range(2):
                tr.op('dve', lambda e, i=i: e.tensor_tensor(out=modT[:, lo:hi, i], in0=psm[:, lo:hi, i], in1=badaT[:, lo:hi],
                                                           op=ALU.add),
                      reads=["P7", "badaT"], writes=["modT"])
            for i in range(2):
                tr.op('dve', lambda e, i=i: e.scalar_tensor_tensor(
                    out=Amod[:, n, i, :], in0=modT[:, (3 * n + 1) * 8:(3 * n + 2) * 8, i], scalar=1.0,
                    in1=gT[:, n, :], op0=ALU.add, op1=ALU.mult), reads=["modT", "gT"], writes=["Amod"])
                tr.op('dve', lambda e, i=i: e.tensor_scalar(
                    out=Gmod[:, n, i, :], in0=modT[:, (3 * n + 2) * 8:(3 * n + 3) * 8, i],
                    scalar1=(1.0 if n == 1 else 0.5), scalar2=None, op0=ALU.mult),
                    reads=["modT"], writes=["Gmod"])

        mods_finalize(0)

        ADA_B0 = 3456
        ADA_NSLAB = (9 * D - ADA_B0 + 255) // 256

        def ada_slab(sbi):
            c0 = ADA_B0 + 256 * sbi
            ncol = min(256, 9 * D - c0)
            bi = cnt['wo'] % 2
            cnt['wo'] += 1
            wob = wobuf[bi][:, :, :].rearrange("p a b -> p (a b)")[:, 0:2048].rearrange("p (k n) -> p k n", k=8)
            tr.dma('pool', wob[:, :, 0:ncol], wada_v[:, :, c0:c0 + ncol], "wo%d" % bi, writes=["wo%d" % bi])

            def f(e):
                ins = None
                for cc in range(ncol // 128):
                    ch = c0 // 128 + cc
                    for kc in range(8):
                        ins = e.matmul(psm[:, ch, :], wob[:, kc, cc * 128:(cc + 1) * 128], scond[:, kc, :],
                                       start=(kc == 0), stop=(kc == 7))
                return ins
            tr.op('pe', f, reads=["wo%d" % bi, "scond"], writes=["P7"])

        def Bmod(n, i, c):
            return modT[:, 3 * n * 8 + c, i:i + 1]

        cnt = {'io': 0, 'oo': 0, 'tmp': 0, 'wb': 0, 'wo': 0}

        def norm_apply(n):
            for ti, (t0, N, ci) in enumerate(TT):
                for c in range(8):
                    k = cnt['tmp'] % 2
                    cnt['tmp'] += 1
                    tr.op('dve', lambda e, c=c, k=k: e.scalar_tensor_tensor(
                        out=tmpA[k][:, 0:N], in0=xT[:, c, t0:t0 + N], scalar=Amod[:, n, ci, c:c + 1],
                        in1=RS[ti][:, 0:N], op0=ALU.mult, op1=ALU.mult),
                        reads=["xT.%d.%d" % (c, ti), "Amod", RSN[ti]], writes=["tmpA%d" % k])
                    tr.op('act', lambda e, c=c, k=k: e.activation(
                        out=hT[:, c, t0:t0 + N], in_=tmpA[k][:, 0:N], func=AF.Identity, bias=Bmod(n, ci, c)),
                        reads=["tmpA%d" % k, "modT"], writes=["hT.%d.%d" % (ti, c)])

        def norm_mod(n):
            if n > 0:
                norm_stats()
            norm_apply(n)

        def ffn(n, win_d, wout_d, hook=None):
            pre = {}

            def issue(j):
                bi = cnt['wb'] % NWB
                cnt['wb'] += 1
                tr.dma('pool', wbuf[bi][:], win_d[j], "wb%d" % bi, writes=["wb%d" % bi])
                return bi
            for j in range(NWB - 1):
                pre[j] = issue(j)
            norm_mod(n)
            for j in range(NJ):
                bi = pre[j] if j in pre else issue(j)
                for ti, (t0, N, ci) in enumerate(TT):
                    k = cnt['io'] % 2
                    cnt['io'] += 1
                    pg, pu = P[2 * k], P[2 * k + 1]

                    def f(e, pg=pg, pu=pu, bi=bi):
                        ins = None
                        for kc in range(8):
                            e.matmul(pg[:, 0:N], wbuf[bi][:, kc, 0:128], hT[:, kc, t0:t0 + N],
                                     start=(kc == 0), stop=(kc == 7))
                        for kc in range(8):
                            ins = e.matmul(pu[:, 0:N], wbuf[bi][:, kc, 128:256], hT[:, kc, t0:t0 + N],
                                           start=(kc == 0), stop=(kc == 7))
                        return ins
                    tr.op('pe', f, reads=["wb%d" % bi] + ["hT.%d.%d" % (ti, c_) for c_ in range(8)], writes=["P%d" % (2 * k), "P%d" % (2 * k + 1)])
                    tr.op('act', lambda e, pg=pg, k=k: e.activation(out=sg[k][:, 0:N], in_=pg[:, 0:N], func=AF.Silu),
                          reads=["P%d" % (2 * k)], writes=["sg%d" % k])
                    tr.op('dve', lambda e, pu=pu, k=k, j=j: e.tensor_tensor(
                        out=actT[:, j, t0:t0 + N], in0=sg[k][:, 0:N], in1=pu[:, 0:N], op=ALU.mult),
                        reads=["sg%d" % k, "P%d" % (2 * k + 1)], writes=["actT.%d" % ti])
                if hook is not None:
                    hook(j)
            for o in range(8):
                bi = cnt['wo'] % 2
                cnt['wo'] += 1
                tr.dma('pool', wobuf[bi][:], wout_d[o], "wo%d" % bi, writes=["wo%d" % bi])
                for ti, (t0, N, ci) in enumerate(TT):
                    k = 5 + cnt['oo'] % 2
                    cnt['oo'] += 1

                    def f(e, k=k, bi=bi, o=o):
                        ins = None
                        for kc in range(NJ):
                            ins = e.matmul(P[k][:, 0:N], wobuf[bi][:, kc, :], actT[:, kc, t0:t0 + N],
                                           start=(kc == 0), stop=(kc == NJ - 1))
                        return ins
                    tr.op('pe', f, reads=["wo%d" % bi, "actT.%d" % ti], writes=["P%d" % k])
                    tr.op('dve', lambda e, k=k, o=o: e.scalar_tensor_tensor(
                        out=xT[:, o, t0:t0 + N], in0=P[k][:, 0:N], scalar=Gmod[:, n, ci, o:o + 1],
                        in1=xT[:, o, t0:t0 + N], op0=ALU.mult, op1=ALU.add),
                        reads=["P%d" % k, "Gmod", "xT.%d.%d" % (o, ti)], writes=["xT.%d.%d" % (o, ti)])
                if n == 2:
                    tr.dma('sp', yT_d[:, o, :], xT[:, o, :], "out_y", reads=["xT.%d.%d" % (o, t) for t in range(3)])


        def barrier():
            for e in ('pe', 'act', 'dve'):
                for o in ('pe', 'act', 'dve'):
                    if o != e and tr.cnt[o] > tr.waited[e].get(o, 0):
                        tr.E[e].wait_ge(tr.sem[o], tr.cnt[o])
                        tr.waited[e][o] = tr.cnt[o]
                for name, (sem, val) in tr.dsem.items():
                    if name.startswith("out_") and val > tr.waited[e].get("d_" + name, 0):
                        tr.E[e].wait_ge(sem, val)
                        tr.waited[e]["d_" + name] = val

        ropeC = sb("ropeC_s", [128, 1024])
        ropeS = sb("ropeS_s", [128, 1024])
        mres = sb("mres_s", [128, T], BF16)
        perm_bf = sb("perm_bf", [128, 128], BF16)
        blk_bf = sb("blk_bf", [128, 128], BF16)
        ident_bf = sb("ident_bf", [128, 128], BF16)
        tri = sb("tri_s", [64, 2, 64])
        maskb = sb("maskb_s", [128, 48])
        flags = sb("flags_s", [128, 10])
        s0in = sb("s0in", [128, 2, 4, 128])
        lbl = sb("lbl_s", [128, 2, 4])
        small = sb("small_s", [128, 4])
        lamp = sb("lamp_s", [128, 4, 64])
        subln = sb("subln_s", [128, 128])
        lb = sb("lb", [128, 4])
        oml = sb("oml", [128, 4])
        lamt = sb("lamt", [128, 8])
        lamj = sb("lamj", [128, 64])
        oF = sb("oF", [128, T])
        mt = [sb("mt%d" % i, [128, 512]) for i in range(4)]
        mtb = [sb("mtb%d" % i, [128, 512], BF16) for i in range(2)]
        Sfm = sb("Sfm", [128, 2, 2, 128])
        Sbm = sb("Sbm", [128, 2, 2, 128], BF16)
        Asm = sb("Asm", [64, 2, 2, 64], BF16)
        dec = [sb("dec%d" % d, [128, NCH]) for d in range(2)]
        bend = sb("bend", [128, NCH])
        om = [[sb("om%d%d" % (m, q), [128, 128]) for q in range(2)] for m in range(2)]
        odt = [sb("od%d" % q, [128, 128]) for q in range(2)]
        onb = [sb("on%d" % q, [128, 128], BF16) for q in range(2)]
        sm = [sb("sm%d" % i, [128, 4]) for i in range(2)]
        sqj = sb("sqj", [128, 128])
        epsc = sb("epsc", [128, 1])
        mhalf = sb("mhalf", [128, 512])

        def load_mixer_consts():
            tr.op('dve', lambda e: e.memset(epsc[:], EPS), writes=["epsc"])
            tr.op('dve', lambda e: e.memset(mhalf[:], -0.5), writes=["mhalf"])
            for (dst, src_, nm) in [(ropeC[:], ropeC_d[:, 0:1024], "ropeC"), (ropeS[:], ropeS_d[:, 0:1024], "ropeS"),
                                    (tri[:], tri_d, "tri"), (maskb[:], maskb_d, "maskb"), (flags[:], flags_d, "flags"),
                                    (s0in[:], s0_d, "s0in"), (lbl[:], lbl_d, "lbl"), (small[:], small_d, "small"),
                                    (lamp[:], lamp_d, "lamp"), (subln[:], subln_d, "subln")]:
                tr.dma('sp', dst, src_, "c_" + nm, writes=[nm])
            for (dst, src_, nm) in [(mres[:], mres_d, "mres"), (perm_bf[:], perm_d, "perm"),
                                    (blk_bf[:], blk64_d, "blk"), (ident_bf[:], ident_d, "ident")]:
                tr.dma('pool', dst, src_, "c_" + nm, writes=[nm])
            tr.op('dve', lambda e: e.tensor_tensor(out=lb[:], in0=lbl[:, 0, :], in1=lbl[:, 1, :], op=ALU.subtract),
                  reads=["lbl"], writes=["lb"])
            tr.op('act', lambda e: e.activation(out=lb[:], in_=lb[:], func=AF.Sigmoid), reads=["lb"], writes=["lb"])
            tr.op('dve', lambda e: e.tensor_scalar(out=oml[:], in0=lb[:], scalar1=-1.0, scalar2=1.0,
                                                   op0=ALU.mult, op1=ALU.add), reads=["lb"], writes=["oml"])
            for i in range(2):
                tr.op('dve', lambda e, i=i: e.tensor_tensor(out=lamj[:], in0=lamp[:, 2 * i, :], in1=lamp[:, 2 * i + 1, :],
                                                           op=ALU.mult), reads=["lamp"], writes=["lamj"])
                tr.op('act', lambda e, i=i: e.activation(out=sqj[:, 0:64], in_=lamj[:], func=AF.Identity,
                                                        accum_out=lamt[:, i:i + 1]), reads=["lamj"], writes=["lamt", "sqj"])
            tr.op('act', lambda e: e.activation(out=lamt[:, 4:6], in_=lamt[:, 0:2], func=AF.Exp), reads=["lamt"], writes=["lamt"])
            tr.op('dve', lambda e: e.tensor_tensor(out=lamt[:, 6:7], in0=lamt[:, 5:6], in1=lamt[:, 4:5], op=ALU.subtract),
                  reads=["lamt"], writes=["lamt"])
            tr.op('dve', lambda e: e.tensor_scalar(out=lamt[:, 2:3], in0=lamt[:, 6:7], scalar1=-LAM_INIT, scalar2=None,
                                                   op0=ALU.add), reads=["lamt"], writes=["lamt"])
            tr.op('dve', lambda e: e.tensor_scalar(out=subln[:], in0=subln[:], scalar1=1.0 - LAM_INIT, scalar2=None,
                                                   op0=ALU.mult), reads=["subln"], writes=["subln"])

        def mixer():
            barrier()
            norm_mod(1)
            barrier()
            if MIXL < 1:
                return
            FFN_SCR = ["sq.a", "sq.b", "sd", "rstd", "tmpA0", "tmpA1", "sg0", "sg1", "F2", "BB2", "EE2", "KK2"]

            def gran(g, n=1):
                return arena[:, g * 1280:(g + n) * 1280]

            def gn(g, n=1):
                return ["g%d" % i for i in range(g, g + n)]
            oT = arena[:, 0:8 * 1280].rearrange("p (c t) -> p c t", c=8)
            Fv, BBv, EEv = gran(8, 2).bitcast(F32), gran(10, 2).bitcast(F32), gran(12, 2).bitcast(F32)
            Qd = [gran(14), gran(15)]
            F2v, BB2v, EE2v = scratch[:, 0:2560].bitcast(F32), scratch[:, 2560:5120].bitcast(F32), scratch[:, 5120:7680].bitcast(F32)
            KK2 = scratch[:, 7680:8960]
            Ktok = [arena[0:64, (16 + 2 * d) * 1280:(18 + 2 * d) * 1280].rearrange("p (a b) -> p a b", a=NCH)
                    for d in range(2)]
            Vtok = arena[0:64, 20 * 1280:22 * 1280].rearrange("p (a b) -> p a b", a=NCH)
            wo0 = wobuf[0][:, :, :].rearrange("p a b -> p (a b)")
            wo1 = wobuf[1][:, :, :].rearrange("p a b -> p (a b)")
            QT = wo0[:, 0:2560].bitcast(F32)
            GS = wo1[:, 0:1280]
            KK = wo1[:, 1280:2560]
            Vaug = scratch[:, 0:7280].rearrange("p (k h e) -> p k h e", k=14, h=4)
            kcT = scratch[:, 7280:7280 + 2048].rearrange("p (h k) -> p h k", h=4)
            AQ, AK = Fv, BBv
            qrs, krs = [gran(12), gran(20)], [gran(13), gran(21)]
            qrns, krns = [gn(12), gn(20)], [gn(13), gn(21)]
            ET = [arena[:, 14 * 1280 + i * 3072: 14 * 1280 + (i + 1) * 3072].rearrange("p (k q) -> p k q", k=12)
                  for i in range(2)]
            ETn = [["ET0"], ["ET1"]]
            ptb = P[7][:, :].bitcast(BF16)
            pc = {'p': 0, 'mt': 0, 'a': 0, 'sp': 0, 'po': 0, 'wo': 0}
            gp = {'banks': [0, 1, 2, 3]}

            def next_bank():
                banks = gp['banks']
                if gp.get('by_ctx') is not None:
                    banks = gp['by_ctx'][gp['ctx']]
                k = banks[pc['p'] % len(banks)]
                pc['p'] += 1
                return k

            def next_mt():
                k = pc['mt'] % 4
                pc['mt'] += 1
                return k

            def load_slab(src_ap):
                bi = cnt['wb'] % NWB
                cnt['wb'] += 1
                tr.dma('pool', wbuf[bi][:], src_ap, "wb%d" % bi, writes=["wb%d" % bi])
                return bi

            def vproj_and_caches():
                import os
                DBG = os.environ.get("KDBG", "")
                if "a" not in DBG:
                    tr.dma('pool', kcT, kcT_d, "kc", writes=["kcT"] + FFN_SCR)
                if "b" not in DBG:
                    tr.dma('pool', Vaug[:, 0:4, :, 0:128], vc_d, "vc", writes=["Vaug.c"] + FFN_SCR)
                if "c" not in DBG:
                    tr.op('dve', lambda e: e.memset(Vaug[:, :, :, 128:130], 1.0), writes=["Vaug.1"] + FFN_SCR)

                wav_v = wav_d.rearrange("p k (s n) -> s p k n", s=2)
                for hh in range(0 if "d" in DBG else 2):
                    bi = load_slab(wav_v[hh])
                    for ti in range(10):
                        k = 5 + pc['po'] % 2
                        pc['po'] += 1

                        def f(e, k=k, bi=bi, ti=ti):
                            ins = None
                            for kc in range(8):
                                ins = e.matmul(P[k][:, 0:256], hT[:, kc, ti * 128:(ti + 1) * 128], wbuf[bi][:, kc, :],
                                               start=(kc == 0), stop=(kc == 7))
                            return ins
                        tr.op('pe', f, reads=["wb%d" % bi] + ["hT.%d.%d" % (min(ti // 4, 2), c_) for c_ in range(8)], writes=["P%d" % k])
                        m_ = next_mt()
                        tr.op('act', lambda e, k=k, m_=m_: e.copy(out=mt[m_][:, 0:256], in_=P[k][:, 0:256]),
                              reads=["P%d" % k], writes=["mt%d" % m_])
                        tr.dma('sp', vo_d[:, ti, hh * 256:(hh + 1) * 256], mt[m_][:, 0:256], "out_v%d" % m_, reads=["mt%d" % m_])
                        tr.op('dve', lambda e, k=k, ti=ti, hh=hh: e.tensor_copy(
                            out=Vaug[:, 4 + ti, 2 * hh:2 * hh + 2, 0:128],
                            in_=P[k][:, 0:256].rearrange("p (h e) -> p h e", h=2)),
                            reads=["P%d" % k], writes=["Vaug.%d" % ti] + FFN_SCR)
                    yield


            def proj(ci, evac):
                if getattr(proj, "cur", None) != ci // 2:
                    proj.bi = load_slab(wmix_d[ci // 2])
                    proj.cur = ci // 2
                bi, half = proj.bi, ci % 2
                for ti, (t0, N, _) in enumerate(TT):
                    k = next_bank()

                    def f(e, k=k, bi=bi):
                        ins = None
                        for kc in range(8):
                            ins = e.matmul(P[k][:, 0:N], wbuf[bi][:, kc, half * 128:(half + 1) * 128],
                                           hT[:, kc, t0:t0 + N], start=(kc == 0), stop=(kc == 7))
                        return ins
                    tr.op('pe', f, reads=["wb%d" % bi] + ["hT.%d.%d" % (ti, c_) for c_ in range(8)], writes=["P%d" % k])
                    evac(P[k], "P%d" % k, t0, N, ti)
                    yield

            def transposes(srcT, src_names, dst, dst_names):
                for g0 in range(0, NCH, 8):
                    n = min(8, NCH - g0)
                    kt = next_bank()
                    ptk = P[kt][:, :].bitcast(BF16)

                    def f(e, g0=g0, n=n, ptk=ptk):
                        ins = None
                        for c in range(n):
                            ins = e.transpose(out=ptk[0:64, c * 128:(c + 1) * 128],
                                              in_=srcT[:, (g0 + c) * CH:(g0 + c + 1) * CH], identity=ident_bf[:])
                        return ins
                    tr.op('pe', f, reads=src_names + ["ident"], writes=["P%d" % kt])
                    tr.op('act', lambda e, g0=g0, n=n, ptk=ptk: e.copy(
                        out=dst[:, g0:g0 + n, :], in_=ptk[0:64, 0:n * 128].rearrange("p (a b) -> p a b", a=n)),
                        reads=["P%d" % kt], writes=dst_names)

            def prep(d, h):
                Fx, BBx, EEx, KKx = (Fv, BBv, EEv, KK) if d == 0 else (F2v, BB2v, EE2v, KK2)
                Fn, BBn, EEn, KKn = (gn(8, 2), gn(10, 2), gn(12, 2), ["KK"]) if d == 0 else (["F2"], ["BB2"], ["EE2"], ["KK2"])
                KH = gran(12) if d == 0 else scratch[:, 5120:6400]
                tr.op('dve', lambda e: e.tensor_scalar(out=KKx, in0=Fx, scalar1=-1.0, scalar2=1.0, op0=ALU.mult, op1=ALU.add),
                      reads=Fn, writes=KKn)
                yield
                tr.op('act', lambda e: e.activation(out=Fx, in_=Fx, func=AF.Ln), reads=Fn, writes=Fn)
                yield
                tr.op('dve', lambda e: e.tensor_tensor_scan(out=BBx, data0=mres[:], data1=Fx, initial=0.0,
                                                            op0=ALU.mult, op1=ALU.add),
                      reads=Fn + ["mres"], writes=BBn)
                yield
                if d == 1:
                    tr.op('dve', lambda e: e.tensor_copy(out=bend[:], in_=BBx[:, CH - 1::CH]), reads=BBn, writes=["bend"])
                    tr.op('dve', lambda e: e.tensor_tensor(out=Fx, in0=Fx, in1=BBx, op=ALU.subtract),
                          reads=Fn + BBn, writes=Fn)
                    yield
                    tr.op('dve', lambda e: e.tensor_tensor(
                        out=BBx.rearrange("p (a b) -> p a b", a=NCH), in0=Fx.rearrange("p (a b) -> p a b", a=NCH),
                        in1=bend[:].unsqueeze(2).to_broadcast([128, NCH, CH]), op=ALU.add),
                        reads=Fn + ["bend"], writes=BBn)
                    yield
                tr.op('act', lambda e: e.activation(out=EEx, in_=BBx, func=AF.Exp), reads=BBn, writes=EEn)
                yield
                col = (CH - 1) if d == 0 else 0
                tr.op('dve', lambda e: e.tensor_copy(out=dec[d][:], in_=EEx[:, col::CH]), reads=EEn, writes=["dec%d" % d])
                tr.op('dve', lambda e: e.tensor_tensor(out=Qd[d], in0=QT, in1=EEx, op=ALU.mult),
                      reads=EEn + ["QT"], writes=gn(14 + d))
                yield
                tr.op('act', lambda e: e.activation(out=EEx, in_=BBx, func=AF.Exp, scale=-1.0), reads=BBn, writes=EEn)
                yield
                tr.op('dve', lambda e: e.tensor_tensor(out=KTf[d], in0=KKx, in1=EEx, op=ALU.mult),
                      reads=EEn + KKn, writes=["KTf%d" % d])
                yield
                tr.op('dve', lambda e: e.tensor_tensor(
                    out=KH.rearrange("p (a b) -> p a b", a=NCH), in0=KTf[d].rearrange("p (a b) -> p a b", a=NCH),
                    in1=dec[d][:].unsqueeze(2).to_broadcast([128, NCH, CH]), op=ALU.mult),
                    reads=["KTf%d" % d, "dec%d" % d], writes=EEn)
                yield
                transposes(KH, EEn, Ktok[d], gn(16 + 2 * d, 2))

            KTf = [mtbig[:, 0:1280], mtbig[:, 1280:2560]]

            def chains(h):
                SfT = lambda buf, d: Sfm[:, buf, d, :]
                SbT = lambda buf, d: Sbm[:, buf, d, :]
                tr.op('dve', lambda e: e.tensor_copy(out=SfT(0, 0), in_=s0in[:, 0, h, :]), reads=["s0in"], writes=["Sf00"])
                tr.op('dve', lambda e: e.memset(SfT(0, 1), 0.0), writes=["Sf10"])
                tr.op('act', lambda e: e.copy(out=Sbm[:, 0, :, :], in_=Sfm[:, 0, :, :]), reads=["Sf00", "Sf10"], writes=["Sb00", "Sb10"])

                def cidx(s, d):
                    return s if d == 0 else NCH - 1 - s

                def stage1(s):
                    ks, kS = 4 + s % 2, 6 + s % 2

                    def f(e):
                        ins = None
                        for d in range(2):
                            c = cidx(s, d)
                            cs = slice(c * CH, (c + 1) * CH)
                            e.matmul(P[ks][0:64, d * 64:(d + 1) * 64], KTf[d][:, cs], Qd[d][:, cs], start=True, stop=True)
                            ins = e.matmul(P[kS][:, d * 128:(d + 1) * 128], Ktok[d][:, c, :], Vtok[:, c, :], start=True, stop=True)
                        return ins
                    tr.op('pe', f, reads=["KTf0", "KTf1"] + gn(14, 8), writes=["P%d" % ks, "P%d" % kS])

                def stage2(s):
                    ks, kS = 4 + s % 2, 6 + s % 2
                    a, b = s % 2, 1 - s % 2
                    tr.op('dve', lambda e: e.tensor_tensor(
                        out=Asm[:, s % 2, :, :], in0=P[ks][0:64, 0:128].rearrange("p (d t) -> p d t", d=2), in1=tri[:, :, :], op=ALU.mult),
                        reads=["P%d" % ks, "tri"], writes=["A%d" % (s % 2)])
                    for d in range(2):
                        c = cidx(s, d)
                        blk = c // 4
                        sfn, sbn = "Sf%d%d" % (d, a), "Sb%d%d" % (d, a)
                        enter = (d == 0 and c % 4 == 0 and c > 0) or (d == 1 and c % 4 == 3 and c < NCH - 1)
                        if enter:
                            if d == 0 and blk == 4:
                                tr.op('dve', lambda e, d=d: e.memset(SfT(a, d), 0.0), writes=[sfn])
                                tr.op('dve', lambda e, d=d: e.memset(SbT(a, d), 0.0), writes=[sbn])
                            elif d == 1 and blk == 3:
                                tr.op('dve', lambda e, d=d: e.tensor_copy(out=SfT(a, d), in_=s0in[:, 1, h, :]), reads=["s0in"], writes=[sfn])
                                tr.op('act', lambda e, d=d: e.copy(out=SbT(a, d), in_=s0in[:, 1, h, :]), reads=["s0in"], writes=[sbn])
                            else:
                                fl = flags[:, blk:blk + 1] if d == 0 else flags[:, 5 + blk:6 + blk]
                                tr.op('dve', lambda e, d=d, fl=fl: e.tensor_scalar(out=SfT(a, d), in0=SfT(a, d), scalar1=fl, scalar2=None, op0=ALU.mult),
                                      reads=[sfn, "flags"], writes=[sfn])
                                tr.op('dve', lambda e, d=d, fl=fl: e.tensor_scalar(out=SbT(a, d), in0=SbT(a, d), scalar1=fl, scalar2=None, op0=ALU.mult),
                                      reads=[sbn, "flags"], writes=[sbn])
                        tr.op('dve', lambda e, d=d, c=c: e.scalar_tensor_tensor(
                            out=SfT(b, d), in0=SfT(a, d), scalar=dec[d][:, c:c + 1], in1=P[kS][:, d * 128:(d + 1) * 128],
                            op0=ALU.mult, op1=ALU.add),
                            reads=["P%d" % kS, sfn, "dec%d" % d], writes=["Sf%d%d" % (d, b)])
                    tr.op('act', lambda e: e.copy(out=Sbm[:, b, :, :], in_=Sfm[:, b, :, :]),
                          reads=["Sf0%d" % b, "Sf1%d" % b], writes=["Sb0%d" % b, "Sb1%d" % b])
                    for d in range(2):
                        c = cidx(s, d)
                        if (d == 0 and c % 4 == 3) or (d == 1 and c % 4 == 0):
                            tr.dma('sp', so_d[:, c // 4, d, h, :], SfT(b, d), "out_s%d%d" % (d, b), reads=["Sf%d%d" % (d, b)])

                def stage3(s):
                    a = s % 2
                    for d in range(2):
                        c = cidx(s, d)
                        cs = slice(c * CH, (c + 1) * CH)
                        ti = min(c // 8, 2)
                        pk = d
                        po = P[pk][:, (c % 8) * CH:(c % 8 + 1) * CH]

                        def fo(e, d=d, c=c, po=po, cs=cs):
                            e.matmul(po, Vtok[:, c, :], Asm[:, s % 2, d, :], start=True, stop=False)
                            return e.matmul(po, SbT(a, d), Qd[d][:, cs], start=False, stop=True)
                        tr.op('pe', fo, reads=["A%d" % (s % 2), "Sb%d%d" % (d, a)] + gn(20, 2) + gn(14 + d), writes=["P%d" % pk])
                        if (d == 0 and c in (7, 15, 19)) or (d == 1 and c in (16, 8, 0)):
                            t0, N, _ = TT[ti]
                            first = (d == 0) if ti == 0 else (d == 1)
                            if first:
                                tr.op('act', lambda e, pk=pk, t0=t0, N=N: e.copy(out=oF[:, t0:t0 + N], in_=P[pk][:, 0:N]),
                                      reads=["P%d" % pk], writes=["oF.%d" % ti])
                            else:
                                tr.op('dve', lambda e, pk=pk, t0=t0, N=N: e.tensor_tensor(
                                    out=oF[:, t0:t0 + N], in0=P[pk][:, 0:N], in1=oF[:, t0:t0 + N], op=ALU.add),
                                    reads=["P%d" % pk, "oF.%d" % ti], writes=["oF.%d" % ti])

                stage1(0)
                for s in range(NCH):
                    if s + 1 < NCH:
                        stage1(s + 1)
                    stage2(s)
                    stage3(s)
                    yield

            def group_norm(src, src_names, ti, t0, N, ones_mat, ones_name, inv_n, pbank):
                pbank = next_bank()
                tr.op('dve', lambda e: e.tensor_tensor(out=mtb[0][:, 0:N], in0=src[:, t0:t0 + N], in1=src[:, t0:t0 + N], op=ALU.mult),
                      reads=src_names, writes=["mtb0"])
                tr.op('pe', lambda e: e.matmul(P[pbank][:, 0:N], ones_mat, mtb[0][:, 0:N], start=True, stop=True),
                      reads=["mtb0", ones_name], writes=["P%d" % pbank])
                m1 = next_mt()
                if False and ones_name == "blk":
                    tr.op('dve', lambda e: e.tensor_scalar(out=mt[m1][:, 0:N], in0=P[pbank][:, 0:N], scalar1=inv_n, scalar2=EPS,
                                                           op0=ALU.mult, op1=ALU.add),
                          reads=["P%d" % pbank], writes=["mt%d" % m1])
                    tr.op('dve', lambda e: e.tensor_tensor(out=mt[m1][:, 0:N], in0=mt[m1][:, 0:N], in1=mhalf[:, 0:N], op=ALU.pow),
                          reads=["mt%d" % m1, "mhalf"], writes=["mt%d" % m1])
                    return m1
                tr.op('act', lambda e: e.activation(out=mt[m1][:, 0:N], in_=P[pbank][:, 0:N], func=AF.Ln, bias=epsc[:, 0:1], scale=inv_n),
                      reads=["P%d" % pbank, "epsc"], writes=["mt%d" % m1])
                tr.op('act', lambda e: e.activation(out=mt[m1][:, 0:N], in_=mt[m1][:, 0:N], func=AF.Exp, scale=-0.5),
                      reads=["mt%d" % m1], writes=["mt%d" % m1])
                return m1

            def group_norm_g(src, src_names, ti, t0, N, ones_mat, ones_name, inv_n, out):
                pbank = next_bank()
                tr.op('dve', lambda e: e.tensor_tensor(out=mtb[0][:, 0:N], in0=src[:, t0:t0 + N], in1=src[:, t0:t0 + N], op=ALU.mult),
                      reads=src_names, writes=["mtb0"])
                yield
                tr.op('pe', lambda e: e.matmul(P[pbank][:, 0:N], ones_mat, mtb[0][:, 0:N], start=True, stop=True),
                      reads=["mtb0", ones_name], writes=["P%d" % pbank])
                yield
                m1 = next_mt()
                tr.op('act', lambda e: e.activation(out=mt[m1][:, 0:N], in_=P[pbank][:, 0:N], func=AF.Ln, bias=epsc[:, 0:1], scale=inv_n),
                      reads=["P%d" % pbank, "epsc"], writes=["mt%d" % m1])
                tr.op('act', lambda e: e.activation(out=mt[m1][:, 0:N], in_=mt[m1][:, 0:N], func=AF.Exp, scale=-0.5),
                      reads=["mt%d" % m1], writes=["mt%d" % m1])
                out.append(m1)
                yield

            def orec_final(h):
                for ti, (t0, N, _) in enumerate(TT):
                    m1 = group_norm(oF, ["oF.%d" % ti], ti, t0, N, ones_bf[:], "ones", 1.0 / 128, 4)
                    m2 = next_mt()
                    tr.op('dve', lambda e: e.scalar_tensor_tensor(out=mt[m2][:, 0:N], in0=oF[:, t0:t0 + N], scalar=small[:, 0:1],
                                                                  in1=mt[m1][:, 0:N], op0=ALU.mult, op1=ALU.mult),
                          reads=["oF.%d" % ti, "small", "mt%d" % m1], writes=["mt%d" % m2])
                    tr.op('dve', lambda e: e.tensor_tensor(out=oT[:, h, t0:t0 + N], in0=mt[m2][:, 0:N], in1=GS[:, t0:t0 + N], op=ALU.mult),
                          reads=["mt%d" % m2, "GS"], writes=["g%d" % h])
                    yield

            def qk_norm_rope(X, xg, gcol, dst, dstn, is_k, h):
                for ti, (t0, N, _) in enumerate(TT):
                    res = []
                    yield from group_norm_g(X, gn(xg, 2), ti, t0, N, blk_bf[:], "blk", 1.0 / 64, res)
                    m1 = res[0]
                    tr.op('dve', lambda e: e.scalar_tensor_tensor(out=X[:, t0:t0 + N], in0=X[:, t0:t0 + N], scalar=small[:, gcol:gcol + 1],
                                                                  in1=mt[m1][:, 0:N], op0=ALU.mult, op1=ALU.mult),
                          reads=gn(xg, 2) + ["small", "mt%d" % m1], writes=gn(xg, 2))
                    yield
                    if is_k:
                        tr.dma('sp', kTo_d[:, h, t0:t0 + N], X[:, t0:t0 + N], "out_k%d" % ti, reads=gn(xg, 2))
                    if ti == 2:
                        tr.op('act', lambda e: e.copy(out=dst[:, t0:t0 + N], in_=X[:, t0:t0 + N]), reads=gn(xg, 2), writes=dstn)
                        yield
                        continue
                    tr.op('dve', lambda e: e.tensor_copy(out=mtb[1][:, 0:N], in_=X[:, t0:t0 + N]), reads=gn(xg, 2), writes=["mtb1"])
                    m3 = next_mt()
                    tr.op('dve', lambda e: e.tensor_tensor(out=mt[m3][:, 0:N], in0=X[:, t0:t0 + N], in1=ropeC[:, t0:t0 + N], op=ALU.mult),
                          reads=gn(xg, 2) + ["ropeC"], writes=["mt%d" % m3])
                    yield
                    kp = next_bank()
                    tr.op('pe', lambda e: e.matmul(P[kp][:, 0:N], perm_bf[:], mtb[1][:, 0:N], start=True, stop=True),
                          reads=["mtb1", "perm"], writes=["P%d" % kp])
                    yield
                    m2 = next_mt()
                    tr.op('dve', lambda e: e.tensor_tensor(out=mt[m2][:, 0:N], in0=P[kp][:, 0:N], in1=ropeS[:, t0:t0 + N], op=ALU.mult),
                          reads=["P%d" % kp, "ropeS"], writes=["mt%d" % m2])
                    yield
                    tr.op('dve', lambda e: e.tensor_tensor(out=dst[:, t0:t0 + N], in0=mt[m2][:, 0:N], in1=mt[m3][:, 0:N], op=ALU.add),
                          reads=["mt%d" % m2, "mt%d" % m3], writes=dstn)
                    yield

            def attention(h, st):
                qr, kr, qrn, krn = qrs[st], krs[st], qrns[st], krns[st]
                blocks = []
                for qb in range(4):
                    specs = [(kcT[:, h, kc * 128:(kc + 1) * 128], ["kcT"], kc) for kc in range(4)]
                    specs += [(kr[:, j * 128:(j + 1) * 128], krn, 4 + j) for j in range(8)]
                    blocks.append((qb * 256, specs, (lambda ki, qb=qb: ki * 4 + qb)))
                specs = [(kr[:, (8 + j) * 128:(9 + j) * 128], krn, 12 + j) for j in range(2)]
                blocks.append((1024, specs, None))
                units = [(bi, m) for bi in range(len(blocks)) for m in range(2)]
                vnames = ["Vaug.c", "Vaug.1"] + ["Vaug.%d" % i for i in range(10)]

                def qk(ui, ki):
                    bi, m = units[ui]
                    q0, specs, maskcol = blocks[bi]
                    kap, knames, vidx = specs[ki]
                    pr = slice(64 * m, 64 * m + 64)
                    k = next_bank()
                    tr.op('pe', lambda e: e.matmul(P[k][:, 0:256], kap[pr, :], qr[pr, q0:q0 + 256], start=True, stop=True),
                          reads=knames + qrn, writes=["P%d" % k])
                    bias = maskb[:, maskcol(ki):maskcol(ki) + 1] if maskcol is not None else 0.0
                    extra = gn(14, 6) if (ui < 2 and ki == 0) else []
                    tr.op('act', lambda e: e.activation(out=ET[ui % 2][:, ki, :], in_=P[k][:, 0:256], func=AF.Exp,
                                                        bias=bias, scale=0.125),
                          reads=["P%d" % k, "maskb"], writes=["ET%d.%d" % (ui % 2, ki)] + extra)

                def pv(ui, ki):
                    bi, m = units[ui]
                    q0, specs, maskcol = blocks[bi]
                    nk = len(specs)
                    vidx = specs[ki][2]
                    kb = 4 + 2 * (ui % 2)

                    def f(e):
                        e.matmul(P[kb][:, 0:130], ET[ui % 2][:, ki, 0:128], Vaug[:, vidx, h, :], start=(ki == 0), stop=(ki == nk - 1))
                        return e.matmul(P[kb + 1][:, 0:130], ET[ui % 2][:, ki, 128:256], Vaug[:, vidx, h, :],
                                        start=(ki == 0), stop=(ki == nk - 1))
                    extra = gn(14, 6) if (ui >= len(units) - 2 and ki == nk - 1) else []
                    tr.op('pe', f, reads=["ET%d.%d" % (ui % 2, ki)] + vnames + extra, writes=["P%d" % kb, "P%d" % (kb + 1)])

                def post_pv(ui):
                    bi, m = units[ui]
                    kb = 4 + 2 * (ui % 2)
                    for qs in range(2):
                        k = kb + qs
                        si = pc['a'] % 2
                        pc['a'] += 1
                        tr.op('dve', lambda e, k=k, si=si: e.reciprocal(out=sm[si][:, 0:1], in_=P[k][:, 128:129]),
                              reads=["P%d" % k], writes=["sm%d" % si])
                        tr.op('dve', lambda e, k=k, si=si, qs=qs: e.tensor_scalar(
                            out=om[m][qs][:], in0=P[k][:, 0:128], scalar1=sm[si][:, 0:1], scalar2=None, op0=ALU.mult),
                            reads=["P%d" % k, "sm%d" % si], writes=["om%d%d" % (m, qs)])

                def post_block(bi):
                    q0 = blocks[bi][0]
                    for qs in range(2):
                        tr.op('dve', lambda e, qs=qs: e.scalar_tensor_tensor(out=odt[qs][:], in0=om[1][qs][:], scalar=lamt[:, 2:3],
                                                                            in1=om[0][qs][:], op0=ALU.mult, op1=ALU.add),
                              reads=["om0%d" % qs, "om1%d" % qs, "lamt"], writes=["od%d" % qs])
                        tr.op('dve', lambda e, qs=qs: e.scalar_tensor_tensor(out=sqj[:], in0=odt[qs][:], scalar=1.0, in1=odt[qs][:],
                                                                            op0=ALU.mult, op1=ALU.mult, accum_out=sm[qs][:, 1:2]),
                              reads=["od%d" % qs], writes=["sqj", "smq%d" % qs])
                        tr.op('act', lambda e, qs=qs: e.activation(out=sm[qs][:, 2:3], in_=sm[qs][:, 1:2], func=AF.Ln, bias=epsc[:, 0:1], scale=1.0 / 128),
                              reads=["smq%d" % qs, "epsc"], writes=["smq%d" % qs])
                        tr.op('act', lambda e, qs=qs: e.activation(out=sm[qs][:, 3:4], in_=sm[qs][:, 2:3], func=AF.Exp, scale=-0.5),
                              reads=["smq%d" % qs], writes=["smq%d" % qs])
                        tr.op('dve', lambda e, qs=qs: e.scalar_tensor_tensor(out=onb[qs][:], in0=odt[qs][:], scalar=sm[qs][:, 3:4],
                                                                            in1=subln[:], op0=ALU.mult, op1=ALU.mult),
                              reads=["od%d" % qs, "smq%d" % qs, "subln"], writes=["on%d" % qs])

                def post_block_b(bi):
                    q0 = blocks[bi][0]
                    k = next_bank()
                    pt = P[k][:, :].bitcast(BF16)

                    def f(e):
                        e.transpose(out=pt[:, 0:128], in_=onb[0][:], identity=ident_bf[:])
                        return e.transpose(out=pt[:, 128:256], in_=onb[1][:], identity=ident_bf[:])
                    tr.op('pe', f, reads=["on0", "on1", "ident"], writes=["P%d" % k])
                    tr.op('dve', lambda e: e.tensor_copy(out=oT[:, 4 + h, q0:q0 + 256], in_=pt[:, 0:256]),
                          reads=["P%d" % k], writes=["g%d" % (4 + h)])

                def nkeys(ui):
                    return len(blocks[units[ui][0]][1])

                pending, pending_a = [], []
                for ui in range(len(units) + 1):
                    yield
                    nu = nkeys(ui) if ui < len(units) else 0
                    npv = nkeys(ui - 1) if ui > 0 else 0
                    for ki in range(max(nu, npv)):
                        if ki and ki % 2 == 0:
                            yield
                        if ki < nu:
                            qk(ui, ki)
                        if ki < npv:
                            pv(ui - 1, ki)
                    if ui > 0:
                        for pb in pending:
                            post_block_b(pb)
                        pending.clear()
                        for pa in pending_a:
                            post_block(pa)
                            pending.append(pa)
                        pending_a.clear()
                        post_pv(ui - 1)
                        if units[ui - 1][1] == 1:
                            pending_a.append(units[ui - 1][0])
                for pa in pending_a:
                    post_block(pa)
                    pending.append(pa)
                for pb in pending:
                    post_block_b(pb)

            def run(*gens, weights=None, set_ctx=True):
                gens = list(gens)
                w = dict(zip([id(g) for g in gens], weights or [1] * len(gens)))
                order = {id(g): i for i, g in enumerate(gens)}
                while gens:
                    for g in list(gens):
                        for _ in range(w[id(g)]):
                            try:
                                if set_ctx:
                                    gp['ctx'] = order[id(g)]
                                next(g)
                            except StopIteration:
                                gens.remove(g)
                                break

            def seq(*gens):
                for g in gens:
                    yield from g

            def evac_copy(dst, dstn, eng='act'):
                if eng == 'act':
                    return lambda ps, pn, t0, N, ti: tr.op(
                        'act', lambda e: e.copy(out=dst[:, t0:t0 + N], in_=ps[:, 0:N]), reads=[pn], writes=dstn)
                return lambda ps, pn, t0, N, ti: tr.op(
                    'dve', lambda e: e.tensor_copy(out=dst[:, t0:t0 + N], in_=ps[:, 0:N]), reads=[pn], writes=dstn)

            def hgrn_front(h):
                base = 7 * h
                yield from proj(base + 0, evac_copy(QT, ["QT"]))
                for d in range(2):
                    def evf(ps, pn, t0, N, ti, h=h, d=d):
                        m1 = next_mt()
                        Fx, Fn = (Fv, gn(8, 2)) if d == 0 else (F2v, ["F2"])
                        tr.op('act', lambda e: e.activation(out=mt[m1][:, 0:N], in_=ps[:, 0:N], func=AF.Sigmoid),
                              reads=[pn], writes=["mt%d" % m1])
                        tr.op('dve', lambda e: e.tensor_scalar(out=Fx[:, t0:t0 + N], in0=mt[m1][:, 0:N], scalar1=oml[:, h:h + 1],
                                                               scalar2=lb[:, h:h + 1], op0=ALU.mult, op1=ALU.add),
                              reads=["mt%d" % m1, "oml", "lb"], writes=Fn)
                    yield from proj(base + 2 + d, evf)

                def rest():
                    VT = gran(16)
                    yield from proj(base + 1, evac_copy(VT, gn(16)))
                    transposes(VT, gn(16), Vtok, gn(20, 2))
                    yield
                    yield from proj(base + 4, lambda ps, pn, t0, N, ti: tr.op(
                        'act', lambda e: e.activation(out=GS[:, t0:t0 + N], in_=ps[:, 0:N], func=AF.Silu), reads=[pn], writes=["GS"]))
                run(rest(), prep(0, h), prep(1, h), set_ctx=False)
                yield

            def attn_front(h, st):
                base = 7 * h
                yield from proj(base + 5, evac_copy(AQ, gn(8, 2), 'dve'))
                yield from proj(base + 6, evac_copy(AK, gn(10, 2), 'dve'))
                yield from qk_norm_rope(AQ, 8, 1, qrs[st], qrns[st], False, h)
                yield from qk_norm_rope(AK, 10, 2, krs[st], krns[st], True, h)

            for h in range(4):
                if h == 0:
                    run(hgrn_front(h))
                if h == 3:
                    gp['by_ctx'] = {0: [2, 3], 1: [2, 3]}
                    run(chains(h), attn_front(0, 0), weights=[1, 2])
                    gp['by_ctx'] = None
                else:
                    run(chains(h))
                if h < 3:
                    run(hgrn_front(h + 1), orec_final(h))
                else:
                    run(orec_final(h))
            barrier()
            run(vproj_and_caches())
            gp['banks'] = [0, 1, 2, 3]
            for h in range(4):
                gens = [attention(h, h % 2)]
                if h < 3:
                    gens.append(attn_front(h + 1, (h + 1) % 2))
                gp['by_ctx'] = {0: [0, 1, 2], 1: [3]} if h < 3 else None
                run(*gens, weights=[1, 1])
                gp['by_ctx'] = None

            for o in range(8):
                bi = cnt['wo'] % 2
                cnt['wo'] += 1
                tr.dma('pool', wobuf[bi][:, 0:8, :], wmo_d[o], "wo%d" % bi, writes=["wo%d" % bi, "QT", "GS", "KK"])
                for ti, (t0, N, ci) in enumerate(TT):
                    k = 5 + cnt['oo'] % 2
                    cnt['oo'] += 1

                    def f(e, k=k, bi=bi):
                        ins = None
                        for kc in range(8):
                            ins = e.matmul(P[k][:, 0:N], wobuf[bi][:, kc, :], oT[:, kc, t0:t0 + N], start=(kc == 0), stop=(kc == 7))
                        return ins
                    tr.op('pe', f, reads=["wo%d" % bi] + gn(0, 8), writes=["P%d" % k])
                    tr.op('dve', lambda e, k=k, o=o: e.scalar_tensor_tensor(
                        out=xT[:, o, t0:t0 + N], in0=P[k][:, 0:N], scalar=Gmod[:, 1, ci, o:o + 1],
                        in1=xT[:, o, t0:t0 + N], op0=ALU.mult, op1=ALU.add),
                        reads=["P%d" % k, "Gmod", "xT.%d.%d" % (o, ti)], writes=["xT.%d.%d" % (o, ti)])
            barrier()

        mtbig = sb("mtbig", [128, 2560], BF16)
        load_mixer_consts()

        def ada_hook(j):
            for sbi in ([0, 1] if j == 0 else [j + 1]):
                if sbi < ADA_NSLAB:
                    ada_slab(sbi)
        ffn(0, w1in_d, w1out_d, hook=(ada_hook if STAGE >= 2 else None))
        if STAGE >= 2:
            mods_finalize(1)
            mods_finalize(2)
        if STAGE >= 2:
            mixer()
        if STAGE >= 3:
            ffn(2, w2in_d, w2out_d)

        if STAGE < 3:
            for c in range(8):
                tr.dma('sp', yT_d[:, c, :], xT[:, c, :], "out_y", reads=["xT.%d.%d" % (c, t) for t in range(3)])
        tr.finish()
    return nc


def _fm(x2d):
    t, f = x2d.shape
    return np.ascontiguousarray(x2d.T.reshape(f // 128, 128, t).transpose(1, 0, 2))


def _host_consts():
    c = {}
    perm = np.zeros((128, 128), np.float32)
    for m in range(128):
        r = m % 32
        partner = m + 16 if r < 16 else m - 16
        perm[partner, m] = 1.0
    c["perm"] = perm
    blk = np.zeros((128, 128), np.float32)
    blk[:64, :64] = 1.0
    blk[64:, 64:] = 1.0
    c["blk64"] = blk
    c["ident"] = np.eye(128, dtype=np.float32)
    tri = np.zeros((64, 2, 64), np.float32)
    s = np.arange(64)[:, None]
    t = np.arange(64)[None, :]
    tri[:, 0, :] = (s <= t)
    tri[:, 1, :] = (s >= t)
    c["tri"] = tri
    mres = np.ones((128, T), np.float32)
    mres[:, ::CH] = 0.0
    c["mres"] = mres
    return c


def _rope_tables(sample):
    C = np.ones((128, T), np.float32)
    S = np.zeros((128, T), np.float32)
    if sample:
        n = 1024
        half = 32
        inv = (10000.0 ** (-np.arange(0, half, 2, dtype=np.float32) / half)).astype(np.float32)
        pos_row = np.repeat(np.arange(n // 64, dtype=np.float32), 64)
        pos_col = np.tile(np.arange(64, dtype=np.float32), n // 64)
        ar = (pos_row[:, None] * inv).astype(np.float32)
        ac = (pos_col[:, None] * inv).astype(np.float32)
        cr, sr, cc, sc = [a.astype(np.float32) for a in (np.cos(ar), np.sin(ar), np.cos(ac), np.sin(ac))]
        for m in range(2):
            b = 64 * m
            C[b + 0:b + 16, :n] = cr.T
            C[b + 16:b + 32, :n] = cr.T
            C[b + 32:b + 48, :n] = cc.T
            C[b + 48:b + 64, :n] = cc.T
            S[b + 0:b + 16, :n] = -sr.T
            S[b + 16:b + 32, :n] = sr.T
            S[b + 32:b + 48, :n] = -sc.T
            S[b + 48:b + 64, :n] = sc.T
    return C, S


def kernel(x_prompt, x_sample, c, cache_attn_k, cache_attn_v, state_hgrn, c_ctx,
           w_ada, b_ada, norm_ffn1, w_ffn1_in, w_ffn1_out, norm_mix, w_mix_in, w_mix_out,
           hgrn_lb_logits, hgrn_out_norm, attn_q_norm, attn_k_norm, attn_lambda, attn_subln,
           norm_ffn2, w_ffn2_in, w_ffn2_out):
    f = lambda a: np.asarray(a, np.float32)
    x_prompt, x_sample, c, c_ctx = f(x_prompt), f(x_sample), f(c), f(c_ctx)
    cache_attn_k, cache_attn_v, state_hgrn = f(cache_attn_k), f(cache_attn_v), f(state_hgrn)

    def ffn_in_layout(w):
        w = f(w)[0]
        g = w[:, :DFF].reshape(8, 128, NJ, 128)
        u = w[:, DFF:].reshape(8, 128, NJ, 128)
        return np.ascontiguousarray(np.concatenate([g, u], axis=3).transpose(2, 1, 0, 3))

    def out_layout(w, nk):
        w = f(w)[0] if w.ndim == 3 else f(w)
        return np.ascontiguousarray(w.reshape(nk, 128, 8, 128).transpose(2, 1, 0, 3))

    wm = f(w_mix_in)[0]
    order = []
    for h in range(4):
        order += [h, 4 + h, 8 + h, 12 + h, 16 + h, 20 + h, 24 + h]
    wm_chunks = wm.reshape(8, 128, 32, 128)
    wmix = wm_chunks[:, :, order, :].reshape(8, 128, 14, 256).transpose(2, 1, 0, 3)
    wav = wm[:, 3584:].reshape(8, 128, 512).transpose(1, 0, 2)

    shared = {
        "badaT": np.ascontiguousarray(f(b_ada)[0].reshape(72, 128).T),
        "gT": np.ascontiguousarray(np.stack([f(norm_ffn1)[0], f(norm_mix)[0], f(norm_ffn2)[0]])
                                   .reshape(3, 8, 128).transpose(2, 0, 1)),
        "w_ada": np.ascontiguousarray(f(w_ada)[0]),
        "w1in": ffn_in_layout(w_ffn1_in), "w1out": out_layout(w_ffn1_out, NJ),
        "w2in": ffn_in_layout(w_ffn2_in), "w2out": out_layout(w_ffn2_out, NJ),
        "wmix": np.ascontiguousarray(wmix), "wav": np.ascontiguousarray(wav),
        "wmo": out_layout(w_mix_out, 8),
        "lbl": np.ascontiguousarray(f(hgrn_lb_logits).reshape(2, 4, 128).transpose(2, 0, 1)),
        "small": np.ascontiguousarray(np.stack([
            f(hgrn_out_norm)[0], np.tile(f(attn_q_norm)[0], 2), np.tile(f(attn_k_norm)[0], 2),
            np.zeros(128, np.float32)], axis=1)),
        "lamp": np.ascontiguousarray(np.broadcast_to(f(attn_lambda)[0][None], (128, 4, 64))),
        "subln": np.ascontiguousarray(np.broadcast_to(f(attn_subln)[0][None], (128, 128))),
    }
    shared.update(_host_consts())
    ropeP = _rope_tables(False)
    ropeS_ = _rope_tables(True)

    in_maps = []
    for core in range(NCORES):
        sample = core < 2
        if sample:
            xs = np.concatenate([x_sample[core], x_prompt[30 + core]], axis=0)
            cond = np.stack([c[core], c_ctx])
        else:
            p0 = 5 * (core - 2)
            xs = x_prompt[p0:p0 + 5].reshape(T, D)
            cond = np.stack([c_ctx, c_ctx])
        m = dict(shared)
        m["xT"] = _fm(xs)
        m["condT"] = np.ascontiguousarray(cond.reshape(2, 8, 128).transpose(2, 1, 0))
        C_, S_ = ropeS_ if sample else ropeP
        m["ropeC"], m["ropeS"] = C_, S_
        NEG = -30000.0
        maskb = np.full((12, 4), NEG, np.float32)
        flags = np.zeros((128, 10), np.float32)
        if sample:
            maskb[:, :] = 0.0
            flags[:, 1:4] = 1.0
            flags[:, 5:8] = 1.0
            kc_ = cache_attn_k[core, 0]
            m["kcT"] = np.ascontiguousarray(kc_.reshape(512, 4, 128).transpose(2, 1, 0))
            m["vc"] = np.ascontiguousarray(cache_attn_v[core, 0].reshape(4, 128, 4, 128).transpose(1, 0, 2, 3))
            m["s0"] = np.ascontiguousarray(state_hgrn[core, 0].transpose(2, 0, 1, 3))
        else:
            for qb in range(4):
                maskb[4 + 2 * qb: 6 + 2 * qb, qb] = 0.0
            m["kcT"] = np.zeros((128, 4, 512), np.float32)
            m["vc"] = np.zeros((128, 4, 4, 128), np.float32)
            m["s0"] = np.zeros((128, 2, 4, 128), np.float32)
        m["maskb"] = np.ascontiguousarray(np.broadcast_to(maskb.reshape(1, 48), (128, 48)))
        m["flags"] = flags
        in_maps.append(m)

    if _ONLY_PREPARE:
        return in_maps
    nc = build_program()
    if _DEBUG_CORES:
        res = run_bass_kernel_spmd(nc, [in_maps[i] for i in _DEBUG_CORES], core_ids=list(range(len(_DEBUG_CORES))))
        return [res.results[0]["yT"]]
    res = run_bass_kernel_spmd(nc, in_maps, core_ids=list(range(NCORES)))
    return _assemble(res.results)


_ONLY_PREPARE = False
_DEBUG_CORES = None


def _assemble(R):
    y_prompt = np.zeros((32, 256, D), np.float32)
    y_sample = np.zeros((2, 1024, D), np.float32)
    new_k = np.zeros((32, 1, 256, 4, 2, 64), np.float32)
    new_v = np.zeros((32, 1, 256, 4, 128), np.float32)
    new_s = np.zeros((32, 1, 2, 4, 128, 128), np.float32)
    for core in range(NCORES):
        r = R[core]
        y = r["yT"].transpose(2, 1, 0).reshape(T, D)
        kk = r["kTo"].transpose(2, 1, 0).reshape(T, 4, 2, 64)
        vv = r["vo"].transpose(1, 0, 2).reshape(T, 4, 128)
        ss = r["so"].transpose(1, 2, 3, 0, 4)
        if core < 2:
            y_sample[core] = y[:1024]
            blocks = [(4, 30 + core)]
        else:
            blocks = [(b, 5 * (core - 2) + b) for b in range(5)]
        for b, pi in blocks:
            sl = slice(256 * b, 256 * (b + 1))
            y_prompt[pi] = y[sl]
            new_k[pi, 0] = kk[sl]
            new_v[pi, 0] = vv[sl]
            new_s[pi, 0] = ss[b]
    return (y_prompt, y_sample, new_k, new_v, new_s)
```

```python
import contextlib
import numpy as np
import concourse.bass as bass
import concourse.mybir as mybir
from concourse.bass_utils import run_bass_kernel_spmd

F32 = mybir.dt.float32
BF16 = mybir.dt.bfloat16
AF = mybir.ActivationFunctionType
ALU = mybir.AluOpType

NCORES = 8
D = 1024
T = 1280
NBLK = 5
DFF = 2816
NJ = DFF // 128
EPS = 1e-6
TT = [(0, 512, 0), (512, 512, 0), (1024, 256, 1)]
CH = 64
NCH = T // CH
LAM_INIT = 0.2
STAGE = 99
MIXL = 8


class Tr:
    def __init__(self, nc, stack):
        self.nc = nc
        self.stack = stack
        self.E = {'pe': nc.tensor, 'act': nc.scalar, 'dve': nc.vector, 'pool': nc.gpsimd, 'sp': nc.sync}
        self.sem = {k: stack.enter_context(nc.semaphore("sem_" + k)) for k in self.E}
        self.cnt = {k: 0 for k in self.E}
        self.lastw = {}
        self.rd = {}
        self.waited = {k: {} for k in self.E}
        self.dsem = {}

    def _wait(self, eng, dep, kind):
        if dep is None:
            return
        sem, val, src, name = dep
        if src == eng:
            if eng in ('pe', 'sp'):
                return
        if self.waited[eng].get(name, 0) >= val:
            return
        self.E[eng].wait_ge(sem, val)
        self.waited[eng][name] = val

    def _pre(self, eng, reads, writes):
        for r in reads:
            self._wait(eng, self.lastw.get(r), 'raw')
            if len(r) == 2 and r[0] == 'P':
                for d in self.rd.get(r, ()):
                    if d[2] != eng:
                        self._wait(eng, d, 'war')
        for w in writes:
            self._wait(eng, self.lastw.get(w), 'waw')
            for d in self.rd.get(w, ()):
                self._wait(eng, d, 'war')

    def _post(self, dep, reads, writes):
        for r in reads:
            self.rd.setdefault(r, []).append(dep)
        for w in writes:
            self.lastw[w] = dep
            self.rd[w] = []

    def op(self, eng, fn, reads=(), writes=()):
        self._pre(eng, reads, writes)
        ins = fn(self.E[eng])
        self.cnt[eng] += 1
        ins.then_inc(self.sem[eng], 1)
        self._post((self.sem[eng], self.cnt[eng], eng, eng), reads, writes)

    def dma(self, q, out, in_, semname, reads=(), writes=()):
        if semname not in self.dsem:
            self.dsem[semname] = [self.stack.enter_context(self.nc.semaphore("d_" + semname)), 0]
        ds = self.dsem[semname]
        self._pre(q, reads, writes)
        self.E[q].dma_start(out=out, in_=in_).then_inc(ds[0], 16)
        ds[1] += 16
        self._post((ds[0], ds[1], 'dma', "d_" + semname), reads, writes)

    def finish(self):
        for name, (sem, val) in self.dsem.items():
            if val:
                self.nc.sync.wait_ge(sem, val)


def build_program():
    nc = bass.Bass("TRN2", target_bir_lowering=False)

    def din(name, shape, dt=F32):
        return nc.dram_tensor(name, list(shape), dt, kind="ExternalInput").ap()

    def dout(name, shape, dt=F32):
        return nc.dram_tensor(name, list(shape), dt, kind="ExternalOutput").ap()

    xT_d = din("xT", [128, 8, T])
    condT_d = din("condT", [128, 8, 2])
    badaT_d = din("badaT", [128, 72])
    gT_d = din("gT", [128, 3, 8])
    wada_d = din("w_ada", [D, 9 * D])
    w1in_d = din("w1in", [NJ, 128, 8, 256])
    w1out_d = din("w1out", [8, 128, NJ, 128])
    w2in_d = din("w2in", [NJ, 128, 8, 256])
    w2out_d = din("w2out", [8, 128, NJ, 128])
    wmix_d = din("wmix", [14, 128, 8, 256])
    wav_d = din("wav", [128, 8, 512])
    wmo_d = din("wmo", [8, 128, 8, 128])
    lbl_d = din("lbl", [128, 2, 4])
    small_d = din("small", [128, 4])
    lamp_d = din("lamp", [128, 4, 64])
    subln_d = din("subln", [128, 128])
    ropeC_d = din("ropeC", [128, T])
    ropeS_d = din("ropeS", [128, T])
    perm_d = din("perm", [128, 128])
    blk64_d = din("blk64", [128, 128])
    ident_d = din("ident", [128, 128])
    tri_d = din("tri", [64, 2, 64])
    mres_d = din("mres", [128, T])
    maskb_d = din("maskb", [128, 48])
    flags_d = din("flags", [128, 10])
    kcT_d = din("kcT", [128, 4, 512])
    vc_d = din("vc", [128, 4, 4, 128])
    s0_d = din("s0", [128, 2, 4, 128])

    yT_d = dout("yT", [128, 8, T])
    kTo_d = dout("kTo", [128, 4, T])
    vo_d = dout("vo", [128, 10, 512])
    so_d = dout("so", [128, NBLK, 2, 4, 128])

    with contextlib.ExitStack() as st:
        tr = Tr(nc, st)

        def sb(name, shape, dt=F32):
            return st.enter_context(nc.sbuf_tensor(name, list(shape), dt))

        P = [st.enter_context(nc.psum_tensor("ps%d" % i, [128, 512], F32)) for i in range(8)]

        xT = sb("xT_s", [128, 8, T])
        hT = sb("hT_s", [128, 8, T], BF16)
        arena = sb("arena", [128, NJ * T], BF16)
        actT = arena[:, :].rearrange("p (j t) -> p j t", j=NJ)
        NWB = 3
        wbuf = [sb("wbuf%d" % i, [128, 8, 256], BF16) for i in range(NWB)]
        wobuf = [sb("wobuf%d" % i, [128, NJ, 128], BF16) for i in range(2)]
        condT = sb("condT_s", [128, 8, 2])
        scond = sb("scond", [128, 8, 2], BF16)
        badaT = sb("badaT_s", [128, 72])
        gT = sb("gT_s", [128, 3, 8])
        modT = sb("modT", [128, 72, 2])
        Amod = sb("Amod", [128, 3, 2, 8])
        Gmod = sb("Gmod", [128, 3, 2, 8])
        ones_bf = sb("ones_bf", [128, 128], BF16)
        scratch = sb("scratch", [128, 10240], BF16)
        sq = scratch[:, 0:4096].rearrange("p (c n) -> p c n", c=8)
        sd = scratch[:, 4096:5120].bitcast(F32)
        rstd = scratch[:, 5120:6144].bitcast(F32)
        tmpA = [scratch[:, 6144 + 1024 * i:7168 + 1024 * i].bitcast(F32) for i in range(2)]
        sg = [scratch[:, 8192 + 1024 * i:9216 + 1024 * i].bitcast(F32) for i in range(2)]

        tr.dma('sp', condT[:], condT_d, "in0", writes=["condT"])
        tr.dma('sp', badaT[:], badaT_d, "in0b", writes=["badaT"])
        tr.dma('sp', gT[:], gT_d, "in0c", writes=["gT"])
        for c in range(8):
            tr.dma('sp', xT[:, c, :], xT_d[:, c, :], "x%d" % c, writes=["xT.%d.%d" % (c, t) for t in range(3)])
        tr.op('dve', lambda e: e.memset(ones_bf[:], 1.0), writes=["ones"])
        epsc0 = sb("epsc0", [128, 1])
        tr.op('dve', lambda e: e.memset(epsc0[:], EPS), writes=["epsc0"])
        tr.op('act', lambda e: e.activation(out=scond[:], in_=condT[:], func=AF.Silu),
              reads=["condT"], writes=["scond"])

        RS = [rstd, sg[0], sg[1]]
        RSN = ["rstd", "sg0", "sg1"]

        def norm_stats():
            for ti, (t0, N, ci) in enumerate(TT):
                xr = ["xT.%d.%d" % (c, ti) for c in range(8)]
                tr.op('dve', lambda e: e.tensor_tensor(out=sq[:, 0:4, 0:N], in0=xT[:, 0:4, t0:t0 + N],
                                                       in1=xT[:, 0:4, t0:t0 + N], op=ALU.mult),
                      reads=xr[0:4], writes=["sq.a"])
                tr.op('act', lambda e: e.activation(out=sq[:, 4:8, 0:N], in_=xT[:, 4:8, t0:t0 + N], func=AF.Square),
                      reads=xr[4:8], writes=["sq.b"])

                def f(e):
                    ins = None
                    for c in range(8):
                        ins = e.matmul(P[4][:, 0:N], ones_bf[:], sq[:, c, 0:N], start=(c == 0), stop=(c == 7))
                    return ins
                tr.op('pe', f, reads=["sq.a", "sq.b", "ones"], writes=["P4"])
                tr.op('act', lambda e: e.activation(out=sd[:, 0:N], in_=P[4][:, 0:N], func=AF.Ln,
                                                    bias=epsc0[:, 0:1], scale=1.0 / D), reads=["P4", "epsc0"], writes=["sd"])
                tr.op('act', lambda e: e.activation(out=RS[ti][:, 0:N], in_=sd[:, 0:N], func=AF.Exp, scale=-0.5),
                      reads=["sd"], writes=[RSN[ti]])

        norm_stats()

        wada_v = wada_d.rearrange("(kc p) n -> p kc n", p=128)
        wada_buf = [arena[:, i * 9216:(i + 1) * 9216].rearrange("p (k n) -> p k n", k=8) for i in range(2)]
        psm = P[7][:, 0:144].rearrange("p (c i) -> p c i", i=2)
        for s in range(3):
            bi = s % 2
            tr.dma('pool', wada_buf[bi], wada_v[:, :, s * 1152:(s + 1) * 1152], "wada%d" % bi,
                   writes=["wada%d" % bi])

            def f(e, s=s, bi=bi):
                ins = None
                for cc in range(9):
                    ch = s * 9 + cc
                    for kc in range(8):
                        ins = e.matmul(psm[:, ch, :], wada_buf[bi][:, kc, cc * 128:(cc + 1) * 128],
                                       scond[:, kc, :], start=(kc == 0), stop=(kc == 7))
                return ins
            tr.op('pe', f, reads=["wada%d" % bi, "scond"], writes=["P7"])
        def mods_finalize(n):
            lo, hi = 24 * n, 24 * (n + 1)
            for i in range(2):
                tr.op('dve', lambda e, i=i: e.tensor_tensor(out=modT[:, lo:hi, i], in0=psm[:, lo:hi, i], in1=badaT[:, lo:hi],
                                                           op=ALU.add),
                      reads=["P7", "badaT"], writes=["modT"])
            for i in range(2):
                tr.op('dve', lambda e, i=i: e.scalar_tensor_tensor(
                    out=Amod[:, n, i, :], in0=modT[:, (3 * n + 1) * 8:(3 * n + 2) * 8, i], scalar=1.0,
                    in1=gT[:, n, :], op0=ALU.add, op1=ALU.mult), reads=["modT", "gT"], writes=["Amod"])
                tr.op('dve', lambda e, i=i: e.tensor_scalar(
                    out=Gmod[:, n, i, :], in0=modT[:, (3 * n + 2) * 8:(3 * n + 3) * 8, i],
                    scalar1=(1.0 if n == 1 else 0.5), scalar2=None, op0=ALU.mult),
                    reads=["modT"], writes=["Gmod"])

        mods_finalize(0)

        ADA_B0 = 3456
        ADA_NSLAB = (9 * D - ADA_B0 + 255) // 256

        def ada_slab(sbi):
            c0 = ADA_B0 + 256 * sbi
            ncol = min(256, 9 * D - c0)
            bi = cnt['wo'] % 2
            cnt['wo'] += 1
            wob = wobuf[bi][:, :, :].rearrange("p a b -> p (a b)")[:, 0:2048].rearrange("p (k n) -> p k n", k=8)
            tr.dma('pool', wob[:, :, 0:ncol], wada_v[:, :, c0:c0 + ncol], "wo%d" % bi, writes=["wo%d" % bi])

            def f(e):
                ins = None
                for cc in range(ncol // 128):
                    ch = c0 // 128 + cc
                    for kc in range(8):
                        ins = e.matmul(psm[:, ch, :], wob[:, kc, cc * 128:(cc + 1) * 128], scond[:, kc, :],
                                       start=(kc == 0), stop=(kc == 7))
                return ins
            tr.op('pe', f, reads=["wo%d" % bi, "scond"], writes=["P7"])

        def Bmod(n, i, c):
            return modT[:, 3 * n * 8 + c, i:i + 1]

        cnt = {'io': 0, 'oo': 0, 'tmp': 0, 'wb': 0, 'wo': 0}

        def norm_apply(n):
            for ti, (t0, N, ci) in enumerate(TT):
                for c in range(8):
                    k = cnt['tmp'] % 2
                    cnt['tmp'] += 1
                    tr.op('dve', lambda e, c=c, k=k: e.scalar_tensor_tensor(
                        out=tmpA[k][:, 0:N], in0=xT[:, c, t0:t0 + N], scalar=Amod[:, n, ci, c:c + 1],
                        in1=RS[ti][:, 0:N], op0=ALU.mult, op1=ALU.mult),
                        reads=["xT.%d.%d" % (c, ti), "Amod", RSN[ti]], writes=["tmpA%d" % k])
                    tr.op('act', lambda e, c=c, k=k: e.activation(
                        out=hT[:, c, t0:t0 + N], in_=tmpA[k][:, 0:N], func=AF.Identity, bias=Bmod(n, ci, c)),
                        reads=["tmpA%d" % k, "modT"], writes=["hT.%d.%d" % (ti, c)])

        def norm_mod(n):
            if n > 0:
                norm_stats()
            norm_apply(n)

        def ffn(n, win_d, wout_d, hook=None):
            pre = {}

            def issue(j):
                bi = cnt['wb'] % NWB
                cnt['wb'] += 1
                tr.dma('pool', wbuf[bi][:], win_d[j], "wb%d" % bi, writes=["wb%d" % bi])
                return bi
            for j in range(NWB - 1):
                pre[j] = issue(j)
            norm_mod(n)
            for j in range(NJ):
                bi = pre[j] if j in pre else issue(j)
                for ti, (t0, N, ci) in enumerate(TT):
                    k = cnt['io'] % 2
                    cnt['io'] += 1
                    pg, pu = P[2 * k], P[2 * k + 1]

                    def f(e, pg=pg, pu=pu, bi=bi):
                        ins = None
                        for kc in range(8):
                            e.matmul(pg[:, 0:N], wbuf[bi][:, kc, 0:128], hT[:, kc, t0:t0 + N],
                                     start=(kc == 0), stop=(kc == 7))
                        for kc in range(8):
                            ins = e.matmul(pu[:, 0:N], wbuf[bi][:, kc, 128:256], hT[:, kc, t0:t0 + N],
                                           start=(kc == 0), stop=(kc == 7))
                        return ins
                    tr.op('pe', f, reads=["wb%d" % bi] + ["hT.%d.%d" % (ti, c_) for c_ in range(8)], writes=["P%d" % (2 * k), "P%d" % (2 * k + 1)])
                    tr.op('act', lambda e, pg=pg, k=k: e.activation(out=sg[k][:, 0:N], in_=pg[:, 0:N], func=AF.Silu),
                          reads=["P%d" % (2 * k)], writes=["sg%d" % k])
                    tr.op('dve', lambda e, pu=pu, k=k, j=j: e.tensor_tensor(
                        out=actT[:, j, t0:t0 + N], in0=sg[k][:, 0:N], in1=pu[:, 0:N], op=ALU.mult),
                        reads=["sg%d" % k, "P%d" % (2 * k + 1)], writes=["actT.%d" % ti])
                if hook is not None:
                    hook(j)
            for o in range(8):
                bi = cnt['wo'] % 2
                cnt['wo'] += 1
                tr.dma('pool', wobuf[bi][:], wout_d[o], "wo%d" % bi, writes=["wo%d" % bi])
                for ti, (t0, N, ci) in enumerate(TT):
                    k = 5 + cnt['oo'] % 2
                    cnt['oo'] += 1

                    def f(e, k=k, bi=bi, o=o):
                        ins = None
                        for kc in range(NJ):
                            ins = e.matmul(P[k][:, 0:N], wobuf[bi][:, kc, :], actT[:, kc, t0:t0 + N],
                                           start=(kc == 0), stop=(kc == NJ - 1))
                        return ins
                    tr.op('pe', f, reads=["wo%d" % bi, "actT.%d" % ti], writes=["P%d" % k])
                    tr.op('dve', lambda e, k=k, o=o: e.scalar_tensor_tensor(
                        out=xT[:, o, t0:t0 + N], in0=P[k][:, 0:N], scalar=Gmod[:, n, ci, o:o + 1],
                        in1=xT[:, o, t0:t0 + N], op0=ALU.mult, op1=ALU.add),
                        reads=["P%d" % k, "Gmod", "xT.%d.%d" % (o, ti)], writes=["xT.%d.%d" % (o, ti)])
                if n == 2:
                    tr.dma('sp', yT_d[:, o, :], xT[:, o, :], "out_y", reads=["xT.%d.%d" % (o, t) for t in range(3)])


        def barrier():
            for e in ('pe', 'act', 'dve'):
                for o in ('pe', 'act', 'dve'):
                    if o != e and tr.cnt[o] > tr.waited[e].get(o, 0):
                        tr.E[e].wait_ge(tr.sem[o], tr.cnt[o])
                        tr.waited[e][o] = tr.cnt[o]
                for name, (sem, val) in tr.dsem.items():
                    if name.startswith("out_") and val > tr.waited[e].get("d_" + name, 0):
                        tr.E[e].wait_ge(sem, val)
                        tr.waited[e]["d_" + name] = val

        ropeC = sb("ropeC_s", [128, 1024])
        ropeS = sb("ropeS_s", [128, 1024])
        mres = sb("mres_s", [128, T], BF16)
        perm_bf = sb("perm_bf", [128, 128], BF16)
        blk_bf = sb("blk_bf", [128, 128], BF16)
        ident_bf = sb("ident_bf", [128, 128], BF16)
        tri = sb("tri_s", [64, 2, 64])
        maskb = sb("maskb_s", [128, 48])
        flags = sb("flags_s", [128, 10])
        s0in = sb("s0in", [128, 2, 4, 128])
        lbl = sb("lbl_s", [128, 2, 4])
        small = sb("small_s", [128, 4])
        lamp = sb("lamp_s", [128, 4, 64])
        subln = sb("subln_s", [128, 128])
        lb = sb("lb", [128, 4])
        oml = sb("oml", [128, 4])
        lamt = sb("lamt", [128, 8])
        lamj = sb("lamj", [128, 64])
        oF = sb("oF", [128, T])
        mt = [sb("mt%d" % i, [128, 512]) for i in range(4)]
        mtb = [sb("mtb%d" % i, [128, 512], BF16) for i in range(2)]
        Sfm = sb("Sfm", [128, 2, 2, 128])
        Sbm = sb("Sbm", [128, 2, 2, 128], BF16)
        Asm = sb("Asm", [64, 2, 2, 64], BF16)
        dec = [sb("dec%d" % d, [128, NCH]) for d in range(2)]
        bend = sb("bend", [128, NCH])
        om = [[sb("om%d%d" % (m, q), [128, 128]) for q in range(2)] for m in range(2)]
        odt = [sb("od%d" % q, [128, 128]) for q in range(2)]
        onb = [sb("on%d" % q, [128, 128], BF16) for q in range(2)]
        sm = [sb("sm%d" % i, [128, 4]) for i in range(2)]
        sqj = sb("sqj", [128, 128])
        epsc = sb("epsc", [128, 1])
        mhalf = sb("mhalf", [128, 512])

        def load_mixer_consts():
            tr.op('dve', lambda e: e.memset(epsc[:], EPS), writes=["epsc"])
            tr.op('dve', lambda e: e.memset(mhalf[:], -0.5), writes=["mhalf"])
            for (dst, src_, nm) in [(ropeC[:], ropeC_d[:, 0:1024], "ropeC"), (ropeS[:], ropeS_d[:, 0:1024], "ropeS"),
                                    (tri[:], tri_d, "tri"), (maskb[:], maskb_d, "maskb"), (flags[:], flags_d, "flags"),
                                    (s0in[:], s0_d, "s0in"), (lbl[:], lbl_d, "lbl"), (small[:], small_d, "small"),
                                    (lamp[:], lamp_d, "lamp"), (subln[:], subln_d, "subln")]:
                tr.dma('sp', dst, src_, "c_" + nm, writes=[nm])
            for (dst, src_, nm) in [(mres[:], mres_d, "mres"), (perm_bf[:], perm_d, "perm"),
                                    (blk_bf[:], blk64_d, "blk"), (ident_bf[:], ident_d, "ident")]:
                tr.dma('pool', dst, src_, "c_" + nm, writes=[nm])
            tr.op('dve', lambda e: e.tensor_tensor(out=lb[:], in0=lbl[:, 0, :], in1=lbl[:, 1, :], op=ALU.subtract),
                  reads=["lbl"], writes=["lb"])
            tr.op('act', lambda e: e.activation(out=lb[:], in_=lb[:], func=AF.Sigmoid), reads=["lb"], writes=["lb"])
            tr.op('dve', lambda e: e.tensor_scalar(out=oml[:], in0=lb[:], scalar1=-1.0, scalar2=1.0,
                                                   op0=ALU.mult, op1=ALU.add), reads=["lb"], writes=["oml"])
            for i in range(2):
                tr.op('dve', lambda e, i=i: e.tensor_tensor(out=lamj[:], in0=lamp[:, 2 * i, :], in1=lamp[:, 2 * i + 1, :],
                                                           op=ALU.mult), reads=["lamp"], writes=["lamj"])
                tr.op('act', lambda e, i=i: e.activation(out=sqj[:, 0:64], in_=lamj[:], func=AF.Identity,
                                                        accum_out=lamt[:, i:i + 1]), reads=["lamj"], writes=["lamt", "sqj"])
            tr.op('act', lambda e: e.activation(out=lamt[:, 4:6], in_=lamt[:, 0:2], func=AF.Exp), reads=["lamt"], writes=["lamt"])
            tr.op('dve', lambda e: e.tensor_tensor(out=lamt[:, 6:7], in0=lamt[:, 5:6], in1=lamt[:, 4:5], op=ALU.subtract),
                  reads=["lamt"], writes=["lamt"])
            tr.op('dve', lambda e: e.tensor_scalar(out=lamt[:, 2:3], in0=lamt[:, 6:7], scalar1=-LAM_INIT, scalar2=None,
                                                   op0=ALU.add), reads=["lamt"], writes=["lamt"])
            tr.op('dve', lambda e: e.tensor_scalar(out=subln[:], in0=subln[:], scalar1=1.0 - LAM_INIT, scalar2=None,
                                                   op0=ALU.mult), reads=["subln"], writes=["subln"])

        def mixer():
            barrier()
            norm_mod(1)
            barrier()
            if MIXL < 1:
                return
            FFN_SCR = ["sq.a", "sq.b", "sd", "rstd", "tmpA0", "tmpA1", "sg0", "sg1", "F2", "BB2", "EE2", "KK2"]

            def gran(g, n=1):
                return arena[:, g * 1280:(g + n) * 1280]

            def gn(g, n=1):
                return ["g%d" % i for i in range(g, g + n)]
            oT = arena[:, 0:8 * 1280].rearrange("p (c t) -> p c t", c=8)
            Fv, BBv, EEv = gran(8, 2).bitcast(F32), gran(10, 2).bitcast(F32), gran(12, 2).bitcast(F32)
            Qd = [gran(14), gran(15)]
            F2v, BB2v, EE2v = scratch[:, 0:2560].bitcast(F32), scratch[:, 2560:5120].bitcast(F32), scratch[:, 5120:7680].bitcast(F32)
            KK2 = scratch[:, 7680:8960]
            Ktok = [arena[0:64, (16 + 2 * d) * 1280:(18 + 2 * d) * 1280].rearrange("p (a b) -> p a b", a=NCH)
                    for d in range(2)]
            Vtok = arena[0:64, 20 * 1280:22 * 1280].rearrange("p (a b) -> p a b", a=NCH)
            wo0 = wobuf[0][:, :, :].rearrange("p a b -> p (a b)")
            wo1 = wobuf[1][:, :, :].rearrange("p a b -> p (a b)")
            QT = wo0[:, 0:2560].bitcast(F32)
            GS = wo1[:, 0:1280]
            KK = wo1[:, 1280:2560]
            Vaug = scratch[:, 0:7280].rearrange("p (k h e) -> p k h e", k=14, h=4)
            kcT = scratch[:, 7280:7280 + 2048].rearrange("p (h k) -> p h k", h=4)
            AQ, AK = Fv, BBv
            qrs, krs = [gran(12), gran(20)], [gran(13), gran(21)]
            qrns, krns = [gn(12), gn(20)], [gn(13), gn(21)]
            ET = [arena[:, 14 * 1280 + i * 3072: 14 * 1280 + (i + 1) * 3072].rearrange("p (k q) -> p k q", k=12)
                  for i in range(2)]
            ETn = [["ET0"], ["ET1"]]
            ptb = P[7][:, :].bitcast(BF16)
            pc = {'p': 0, 'mt': 0, 'a': 0, 'sp': 0, 'po': 0, 'wo': 0}
            gp = {'banks': [0, 1, 2, 3]}

            def next_bank():
                banks = gp['banks']
                if gp.get('by_ctx') is not None:
                    banks = gp['by_ctx'][gp['ctx']]
                k = banks[pc['p'] % len(banks)]
                pc['p'] += 1
                return k

            def next_mt():
                k = pc['mt'] % 4
                pc['mt'] += 1
                return k

            def load_slab(src_ap):
                bi = cnt['wb'] % NWB
                cnt['wb'] += 1
                tr.dma('pool', wbuf[bi][:], src_ap, "wb%d" % bi, writes=["wb%d" % bi])
                return bi

            def vproj_and_caches():
                import os
                DBG = os.environ.get("KDBG", "")
                if "a" not in DBG:
                    tr.dma('pool', kcT, kcT_d, "kc", writes=["kcT"] + FFN_SCR)
                if "b" not in DBG:
                    tr.dma('pool', Vaug[:, 0:4, :, 0:128], vc_d, "vc", writes=["Vaug.c"] + FFN_SCR)
                if "c" not in DBG:
                    tr.op('dve', lambda e: e.memset(Vaug[:, :, :, 128:130], 1.0), writes=["Vaug.1"] + FFN_SCR)

                wav_v = wav_d.rearrange("p k (s n) -> s p k n", s=2)
                for hh in range(0 if "d" in DBG else 2):
                    bi = load_slab(wav_v[hh])
                    for ti in range(10):
                        k = 5 + pc['po'] % 2
                        pc['po'] += 1

                        def f(e, k=k, bi=bi, ti=ti):
                            ins = None
                            for kc in range(8):
                                ins = e.matmul(P[k][:, 0:256], hT[:, kc, ti * 128:(ti + 1) * 128], wbuf[bi][:, kc, :],
                                               start=(kc == 0), stop=(kc == 7))
                            return ins
                        tr.op('pe', f, reads=["wb%d" % bi] + ["hT.%d.%d" % (min(ti // 4, 2), c_) for c_ in range(8)], writes=["P%d" % k])
                        m_ = next_mt()
                        tr.op('act', lambda e, k=k, m_=m_: e.copy(out=mt[m_][:, 0:256], in_=P[k][:, 0:256]),
                              reads=["P%d" % k], writes=["mt%d" % m_])
                        tr.dma('sp', vo_d[:, ti, hh * 256:(hh + 1) * 256], mt[m_][:, 0:256], "out_v%d" % m_, reads=["mt%d" % m_])
                        tr.op('dve', lambda e, k=k, ti=ti, hh=hh: e.tensor_copy(
                            out=Vaug[:, 4 + ti, 2 * hh:2 * hh + 2, 0:128],
                            in_=P[k][:, 0:256].rearrange("p (h e) -> p h e", h=2)),
                            reads=["P%d" % k], writes=["Vaug.%d" % ti] + FFN_SCR)
                    yield


            def proj(ci, evac):
                if getattr(proj, "cur", None) != ci // 2:
                    proj.bi = load_slab(wmix_d[ci // 2])
                    proj.cur = ci // 2
                bi, half = proj.bi, ci % 2
                for ti, (t0, N, _) in enumerate(TT):
                    k = next_bank()

                    def f(e, k=k, bi=bi):
                        ins = None
                        for kc in range(8):
                            ins = e.matmul(P[k][:, 0:N], wbuf[bi][:, kc, half * 128:(half + 1) * 128],
                                           hT[:, kc, t0:t0 + N], start=(kc == 0), stop=(kc == 7))
                        return ins
                    tr.op('pe', f, reads=["wb%d" % bi] + ["hT.%d.%d" % (ti, c_) for c_ in range(8)], writes=["P%d" % k])
                    evac(P[k], "P%d" % k, t0, N, ti)
                    yield

            def transposes(srcT, src_names, dst, dst_names):
                for g0 in range(0, NCH, 8):
                    n = min(8, NCH - g0)
                    kt = next_bank()
                    ptk = P[kt][:, :].bitcast(BF16)

                    def f(e, g0=g0, n=n, ptk=ptk):
                        ins = None
                        for c in range(n):
                            ins = e.transpose(out=ptk[0:64, c * 128:(c + 1) * 128],
                                              in_=srcT[:, (g0 + c) * CH:(g0 + c + 1) * CH], identity=ident_bf[:])
                        return ins
                    tr.op('pe', f, reads=src_names + ["ident"], writes=["P%d" % kt])
                    tr.op('act', lambda e, g0=g0, n=n, ptk=ptk: e.copy(
                        out=dst[:, g0:g0 + n, :], in_=ptk[0:64, 0:n * 128].rearrange("p (a b) -> p a b", a=n)),
                        reads=["P%d" % kt], writes=dst_names)

            def prep(d, h):
                Fx, BBx, EEx, KKx = (Fv, BBv, EEv, KK) if d == 0 else (F2v, BB2v, EE2v, KK2)
                Fn, BBn, EEn, KKn = (gn(8, 2), gn(10, 2), gn(12, 2), ["KK"]) if d == 0 else (["F2"], ["BB2"], ["EE2"], ["KK2"])
                KH = gran(12) if d == 0 else scratch[:, 5120:6400]
                tr.op('dve', lambda e: e.tensor_scalar(out=KKx, in0=Fx, scalar1=-1.0, scalar2=1.0, op0=ALU.mult, op1=ALU.add),
                      reads=Fn, writes=KKn)
                yield
                tr.op('act', lambda e: e.activation(out=Fx, in_=Fx, func=AF.Ln), reads=Fn, writes=Fn)
                yield
                tr.op('dve', lambda e: e.tensor_tensor_scan(out=BBx, data0=mres[:], data1=Fx, initial=0.0,
                                                            op0=ALU.mult, op1=ALU.add),
                      reads=Fn + ["mres"], writes=BBn)
                yield
                if d == 1:
                    tr.op('dve', lambda e: e.tensor_copy(out=bend[:], in_=BBx[:, CH - 1::CH]), reads=BBn, writes=["bend"])
                    tr.op('dve', lambda e: e.tensor_tensor(out=Fx, in0=Fx, in1=BBx, op=ALU.subtract),
                          reads=Fn + BBn, writes=Fn)
                    yield
                    tr.op('dve', lambda e: e.tensor_tensor(
                        out=BBx.rearrange("p (a b) -> p a b", a=NCH), in0=Fx.rearrange("p (a b) -> p a b", a=NCH),
                        in1=bend[:].unsqueeze(2).to_broadcast([128, NCH, CH]), op=ALU.add),
                        reads=Fn + ["bend"], writes=BBn)
                    yield
                tr.op('act', lambda e: e.activation(out=EEx, in_=BBx, func=AF.Exp), reads=BBn, writes=EEn)
                yield
                col = (CH - 1) if d == 0 else 0
                tr.op('dve', lambda e: e.tensor_copy(out=dec[d][:], in_=EEx[:, col::CH]), reads=EEn, writes=["dec%d" % d])
                tr.op('dve', lambda e: e.tensor_tensor(out=Qd[d], in0=QT, in1=EEx, op=ALU.mult),
                      reads=EEn + ["QT"], writes=gn(14 + d))
                yield
                tr.op('act', lambda e: e.activation(out=EEx, in_=BBx, func=AF.Exp, scale=-1.0), reads=BBn, writes=EEn)
                yield
                tr.op('dve', lambda e: e.tensor_tensor(out=KTf[d], in0=KKx, in1=EEx, op=ALU.mult),
                      reads=EEn + KKn, writes=["KTf%d" % d])
                yield
                tr.op('dve', lambda e: e.tensor_tensor(
                    out=KH.rearrange("p (a b) -> p a b", a=NCH), in0=KTf[d].rearrange("p (a b) -> p a b", a=NCH),
                    in1=dec[d][:].unsqueeze(2).to_broadcast([128, NCH, CH]), op=ALU.mult),
                    reads=["KTf%d" % d, "dec%d" % d], writes=EEn)
                yield
                transposes(KH, EEn, Ktok[d], gn(16 + 2 * d, 2))

            KTf = [mtbig[:, 0:1280], mtbig[:, 1280:2560]]

            def chains(h):
                SfT = lambda buf, d: Sfm[:, buf, d, :]
                SbT = lambda buf, d: Sbm[:, buf, d, :]
                tr.op('dve', lambda e: e.tensor_copy(out=SfT(0, 0), in_=s0in[:, 0, h, :]), reads=["s0in"], writes=["Sf00"])
                tr.op('dve', lambda e: e.memset(SfT(0, 1), 0.0), writes=["Sf10"])
                tr.op('act', lambda e: e.copy(out=Sbm[:, 0, :, :], in_=Sfm[:, 0, :, :]), reads=["Sf00", "Sf10"], writes=["Sb00", "Sb10"])

                def cidx(s, d):
                    return s if d == 0 else NCH - 1 - s

                def stage1(s):
                    ks, kS = 4 + s % 2, 6 + s % 2

                    def f(e):
                        ins = None
                        for d in range(2):
                            c = cidx(s, d)
                            cs = slice(c * CH, (c + 1) * CH)
                            e.matmul(P[ks][0:64, d * 64:(d + 1) * 64], KTf[d][:, cs], Qd[d][:, cs], start=True, stop=True)
                            ins = e.matmul(P[kS][:, d * 128:(d + 1) * 128], Ktok[d][:, c, :], Vtok[:, c, :], start=True, stop=True)
                        return ins
                    tr.op('pe', f, reads=["KTf0", "KTf1"] + gn(14, 8), writes=["P%d" % ks, "P%d" % kS])

                def stage2(s):
                    ks, kS = 4 + s % 2, 6 + s % 2
                    a, b = s % 2, 1 - s % 2
                    tr.op('dve', lambda e: e.tensor_tensor(
                        out=Asm[:, s % 2, :, :], in0=P[ks][0:64, 0:128].rearrange("p (d t) -> p d t", d=2), in1=tri[:, :, :], op=ALU.mult),
                        reads=["P%d" % ks, "tri"], writes=["A%d" % (s % 2)])
                    for d in range(2):
                        c = cidx(s, d)
                        blk = c // 4
                        sfn, sbn = "Sf%d%d" % (d, a), "Sb%d%d" % (d, a)
                        enter = (d == 0 and c % 4 == 0 and c > 0) or (d == 1 and c % 4 == 3 and c < NCH - 1)
                        if enter:
                            if d == 0 and blk == 4:
                                tr.op('dve', lambda e, d=d: e.memset(SfT(a, d), 0.0), writes=[sfn])
                                tr.op('dve', lambda e, d=d: e.memset(SbT(a, d), 0.0), writes=[sbn])
                            elif d == 1 and blk == 3:
                                tr.op('dve', lambda e, d=d: e.tensor_copy(out=SfT(a, d), in_=s0in[:, 1, h, :]), reads=["s0in"], writes=[sfn])
                                tr.op('act', lambda e, d=d: e.copy(out=SbT(a, d), in_=s0in[:, 1, h, :]), reads=["s0in"], writes=[sbn])
                            else:
                                fl = flags[:, blk:blk + 1] if d == 0 else flags[:, 5 + blk:6 + blk]
                                tr.op('dve', lambda e, d=d, fl=fl: e.tensor_scalar(out=SfT(a, d), in0=SfT(a, d), scalar1=fl, scalar2=None, op0=ALU.mult),
                                      reads=[sfn, "flags"], writes=[sfn])
                                tr.op('dve', lambda e, d=d, fl=fl: e.tensor_scalar(out=SbT(a, d), in0=SbT(a, d), scalar1=fl, scalar2=None, op0=ALU.mult),
                                      reads=[sbn, "flags"], writes=[sbn])
                        tr.op('dve', lambda e, d=d, c=c: e.scalar_tensor_tensor(
                            out=SfT(b, d), in0=SfT(a, d), scalar=dec[d][:, c:c + 1], in1=P[kS][:, d * 128:(d + 1) * 128],
                            op0=ALU.mult, op1=ALU.add),
                            reads=["P%d" % kS, sfn, "dec%d" % d], writes=["Sf%d%d" % (d, b)])
                    tr.op('act', lambda e: e.copy(out=Sbm[:, b, :, :], in_=Sfm[:, b, :, :]),
                          reads=["Sf0%d" % b, "Sf1%d" % b], writes=["Sb0%d" % b, "Sb1%d" % b])
                    for d in range(2):
                        c = cidx(s, d)
                        if (d == 0 and c % 4 == 3) or (d == 1 and c % 4 == 0):
                            tr.dma('sp', so_d[:, c // 4, d, h, :], SfT(b, d), "out_s%d%d" % (d, b), reads=["Sf%d%d" % (d, b)])

                def stage3(s):
                    a = s % 2
                    for d in range(2):
                        c = cidx(s, d)
                        cs = slice(c * CH, (c + 1) * CH)
                        ti = min(c // 8, 2)
                        pk = d
                        po = P[pk][:, (c % 8) * CH:(c % 8 + 1) * CH]

                        def fo(e, d=d, c=c, po=po, cs=cs):
                            e.matmul(po, Vtok[:, c, :], Asm[:, s % 2, d, :], start=True, stop=False)
                            return e.matmul(po, SbT(a, d), Qd[d][:, cs], start=False, stop=True)
                        tr.op('pe', fo, reads=["A%d" % (s % 2), "Sb%d%d" % (d, a)] + gn(20, 2) + gn(14 + d), writes=["P%d" % pk])
                        if (d == 0 and c in (7, 15, 19)) or (d == 1 and c in (16, 8, 0)):
                            t0, N, _ = TT[ti]
                            first = (d == 0) if ti == 0 else (d == 1)
                            if first:
                                tr.op('act', lambda e, pk=pk, t0=t0, N=N: e.copy(out=oF[:, t0:t0 + N], in_=P[pk][:, 0:N]),
                                      reads=["P%d" % pk], writes=["oF.%d" % ti])
                            else:
                                tr.op('dve', lambda e, pk=pk, t0=t0, N=N: e.tensor_tensor(
                                    out=oF[:, t0:t0 + N], in0=P[pk][:, 0:N], in1=oF[:, t0:t0 + N], op=ALU.add),
                                    reads=["P%d" % pk, "oF.%d" % ti], writes=["oF.%d" % ti])

                stage1(0)
                for s in range(NCH):
                    if s + 1 < NCH:
                        stage1(s + 1)
                    stage2(s)
                    stage3(s)
                    yield

            def group_norm(src, src_names, ti, t0, N, ones_mat, ones_name, inv_n, pbank):
                pbank = next_bank()
                tr.op('dve', lambda e: e.tensor_tensor(out=mtb[0][:, 0:N], in0=src[:, t0:t0 + N], in1=src[:, t0:t0 + N], op=ALU.mult),
                      reads=src_names, writes=["mtb0"])
                tr.op('pe', lambda e: e.matmul(P[pbank][:, 0:N], ones_mat, mtb[0][:, 0:N], start=True, stop=True),
                      reads=["mtb0", ones_name], writes=["P%d" % pbank])
                m1 = next_mt()
                if False and ones_name == "blk":
                    tr.op('dve', lambda e: e.tensor_scalar(out=mt[m1][:, 0:N], in0=P[pbank][:, 0:N], scalar1=inv_n, scalar2=EPS,
                                                           op0=ALU.mult, op1=ALU.add),
                          reads=["P%d" % pbank], writes=["mt%d" % m1])
                    tr.op('dve', lambda e: e.tensor_tensor(out=mt[m1][:, 0:N], in0=mt[m1][:, 0:N], in1=mhalf[:, 0:N], op=ALU.pow),
                          reads=["mt%d" % m1, "mhalf"], writes=["mt%d" % m1])
                    return m1
                tr.op('act', lambda e: e.activation(out=mt[m1][:, 0:N], in_=P[pbank][:, 0:N], func=AF.Ln, bias=epsc[:, 0:1], scale=inv_n),
                      reads=["P%d" % pbank, "epsc"], writes=["mt%d" % m1])
                tr.op('act', lambda e: e.activation(out=mt[m1][:, 0:N], in_=mt[m1][:, 0:N], func=AF.Exp, scale=-0.5),
                      reads=["mt%d" % m1], writes=["mt%d" % m1])
                return m1

            def group_norm_g(src, src_names, ti, t0, N, ones_mat, ones_name, inv_n, out):
                pbank = next_bank()
                tr.op('dve', lambda e: e.tensor_tensor(out=mtb[0][:, 0:N], in0=src[:, t0:t0 + N], in1=src[:, t0:t0 + N], op=ALU.mult),
                      reads=src_names, writes=["mtb0"])
                yield
                tr.op('pe', lambda e: e.matmul(P[pbank][:, 0:N], ones_mat, mtb[0][:, 0:N], start=True, stop=True),
                      reads=["mtb0", ones_name], writes=["P%d" % pbank])
                yield
                m1 = next_mt()
                tr.op('act', lambda e: e.activation(out=mt[m1][:, 0:N], in_=P[pbank][:, 0:N], func=AF.Ln, bias=epsc[:, 0:1], scale=inv_n),
                      reads=["P%d" % pbank, "epsc"], writes=["mt%d" % m1])
                tr.op('act', lambda e: e.activation(out=mt[m1][:, 0:N], in_=mt[m1][:, 0:N], func=AF.Exp, scale=-0.5),
                      reads=["mt%d" % m1], writes=["mt%d" % m1])
                out.append(m1)
                yield

            def orec_final(h):
                for ti, (t0, N, _) in enumerate(TT):
                    m1 = group_norm(oF, ["oF.%d" % ti], ti, t0, N, ones_bf[:], "ones", 1.0 / 128, 4)
                    m2 = next_mt()
                    tr.op('dve', lambda e: e.scalar_tensor_tensor(out=mt[m2][:, 0:N], in0=oF[:, t0:t0 + N], scalar=small[:, 0:1],
                                                                  in1=mt[m1][:, 0:N], op0=ALU.mult, op1=ALU.mult),
                          reads=["oF.%d" % ti, "small", "mt%d" % m1], writes=["mt%d" % m2])
                    tr.op('dve', lambda e: e.tensor_tensor(out=oT[:, h, t0:t0 + N], in0=mt[m2][:, 0:N], in1=GS[:, t0:t0 + N], op=ALU.mult),
                          reads=["mt%d" % m2, "GS"], writes=["g%d" % h])
                    yield

            def qk_norm_rope(X, xg, gcol, dst, dstn, is_k, h):
                for ti, (t0, N, _) in enumerate(TT):
                    res = []
                    yield from group_norm_g(X, gn(xg, 2), ti, t0, N, blk_bf[:], "blk", 1.0 / 64, res)
                    m1 = res[0]
                    tr.op('dve', lambda e: e.scalar_tensor_tensor(out=X[:, t0:t0 + N], in0=X[:, t0:t0 + N], scalar=small[:, gcol:gcol + 1],
                                                                  in1=mt[m1][:, 0:N], op0=ALU.mult, op1=ALU.mult),
                          reads=gn(xg, 2) + ["small", "mt%d" % m1], writes=gn(xg, 2))
                    yield
                    if is_k:
                        tr.dma('sp', kTo_d[:, h, t0:t0 + N], X[:, t0:t0 + N], "out_k%d" % ti, reads=gn(xg, 2))
                    if ti == 2:
                        tr.op('act', lambda e: e.copy(out=dst[:, t0:t0 + N], in_=X[:, t0:t0 + N]), reads=gn(xg, 2), writes=dstn)
                        yield
                        continue
                    tr.op('dve', lambda e: e.tensor_copy(out=mtb[1][:, 0:N], in_=X[:, t0:t0 + N]), reads=gn(xg, 2), writes=["mtb1"])
                    m3 = next_mt()
                    tr.op('dve', lambda e: e.tensor_tensor(out=mt[m3][:, 0:N], in0=X[:, t0:t0 + N], in1=ropeC[:, t0:t0 + N], op=ALU.mult),
                          reads=gn(xg, 2) + ["ropeC"], writes=["mt%d" % m3])
                    yield
                    kp = next_bank()
                    tr.op('pe', lambda e: e.matmul(P[kp][:, 0:N], perm_bf[:], mtb[1][:, 0:N], start=True, stop=True),
                          reads=["mtb1", "perm"], writes=["P%d" % kp])
                    yield
                    m2 = next_mt()
                    tr.op('dve', lambda e: e.tensor_tensor(out=mt[m2][:, 0:N], in0=P[kp][:, 0:N], in1=ropeS[:, t0:t0 + N], op=ALU.mult),
                          reads=["P%d" % kp, "ropeS"], writes=["mt%d" % m2])
                    yield
                    tr.op('dve', lambda e: e.tensor_tensor(out=dst[:, t0:t0 + N], in0=mt[m2][:, 0:N], in1=mt[m3][:, 0:N], op=ALU.add),
                          reads=["mt%d" % m2, "mt%d" % m3], writes=dstn)
                    yield

            def attention(h, st):
                qr, kr, qrn, krn = qrs[st], krs[st], qrns[st], krns[st]
                blocks = []
                for qb in range(4):
                    specs = [(kcT[:, h, kc * 128:(kc + 1) * 128], ["kcT"], kc) for kc in range(4)]
                    specs += [(kr[:, j * 128:(j + 1) * 128], krn, 4 + j) for j in range(8)]
                    blocks.append((qb * 256, specs, (lambda ki, qb=qb: ki * 4 + qb)))
                specs = [(kr[:, (8 + j) * 128:(9 + j) * 128], krn, 12 + j) for j in range(2)]
                blocks.append((1024, specs, None))
                units = [(bi, m) for bi in range(len(blocks)) for m in range(2)]
                vnames = ["Vaug.c", "Vaug.1"] + ["Vaug.%d" % i for i in range(10)]

                def qk(ui, ki):
                    bi, m = units[ui]
                    q0, specs, maskcol = blocks[bi]
                    kap, knames, vidx = specs[ki]
                    pr = slice(64 * m, 64 * m + 64)
                    k = next_bank()
                    tr.op('pe', lambda e: e.matmul(P[k][:, 0:256], kap[pr, :], qr[pr, q0:q0 + 256], start=True, stop=True),
                          reads=knames + qrn, writes=["P%d" % k])
                    bias = maskb[:, maskcol(ki):maskcol(ki) + 1] if maskcol is not None else 0.0
                    extra = gn(14, 6) if (ui < 2 and ki == 0) else []
                    tr.op('act', lambda e: e.activation(out=ET[ui % 2][:, ki, :], in_=P[k][:, 0:256], func=AF.Exp,
                                                        bias=bias, scale=0.125),
                          reads=["P%d" % k, "maskb"], writes=["ET%d.%d" % (ui % 2, ki)] + extra)

                def pv(ui, ki):
                    bi, m = units[ui]
                    q0, specs, maskcol = blocks[bi]
                    nk = len(specs)
                    vidx = specs[ki][2]
                    kb = 4 + 2 * (ui % 2)

                    def f(e):
                        e.matmul(P[kb][:, 0:130], ET[ui % 2][:, ki, 0:128], Vaug[:, vidx, h, :], start=(ki == 0), stop=(ki == nk - 1))
                        return e.matmul(P[kb + 1][:, 0:130], ET[ui % 2][:, ki, 128:256], Vaug[:, vidx, h, :],
                                        start=(ki == 0), stop=(ki == nk - 1))
                    extra = gn(14, 6) if (ui >= len(units) - 2 and ki == nk - 1) else []
                    tr.op('pe', f, reads=["ET%d.%d" % (ui % 2, ki)] + vnames + extra, writes=["P%d" % kb, "P%d" % (kb + 1)])

                def post_pv(ui):
                    bi, m = units[ui]
                    kb = 4 + 2 * (ui % 2)
                    for qs in range(2):
                        k = kb + qs
                        si = pc['a'] % 2
                        pc['a'] += 1
                        tr.op('dve', lambda e, k=k, si=si: e.reciprocal(out=sm[si][:, 0:1], in_=P[k][:, 128:129]),
                              reads=["P%d" % k], writes=["sm%d" % si])
                        tr.op('dve', lambda e, k=k, si=si, qs=qs: e.tensor_scalar(
                            out=om[m][qs][:], in0=P[k][:, 0:128], scalar1=sm[si][:, 0:1], scalar2=None, op0=ALU.mult),
                            reads=["P%d" % k, "sm%d" % si], writes=["om%d%d" % (m, qs)])

                def post_block(bi):
                    q0 = blocks[bi][0]
                    for qs in range(2):
                        tr.op('dve', lambda e, qs=qs: e.scalar_tensor_tensor(out=odt[qs][:], in0=om[1][qs][:], scalar=lamt[:, 2:3],
                                                                            in1=om[0][qs][:], op0=ALU.mult, op1=ALU.add),
                              reads=["om0%d" % qs, "om1%d" % qs, "lamt"], writes=["od%d" % qs])
                        tr.op('dve', lambda e, qs=qs: e.scalar_tensor_tensor(out=sqj[:], in0=odt[qs][:], scalar=1.0, in1=odt[qs][:],
                                                                            op0=ALU.mult, op1=ALU.mult, accum_out=sm[qs][:, 1:2]),
                              reads=["od%d" % qs], writes=["sqj", "smq%d" % qs])
                        tr.op('act', lambda e, qs=qs: e.activation(out=sm[qs][:, 2:3], in_=sm[qs][:, 1:2], func=AF.Ln, bias=epsc[:, 0:1], scale=1.0 / 128),
                              reads=["smq%d" % qs, "epsc"], writes=["smq%d" % qs])
                        tr.op('act', lambda e, qs=qs: e.activation(out=sm[qs][:, 3:4], in_=sm[qs][:, 2:3], func=AF.Exp, scale=-0.5),
                              reads=["smq%d" % qs], writes=["smq%d" % qs])
                        tr.op('dve', lambda e, qs=qs: e.scalar_tensor_tensor(out=onb[qs][:], in0=odt[qs][:], scalar=sm[qs][:, 3:4],
                                                                            in1=subln[:], op0=ALU.mult, op1=ALU.mult),
                              reads=["od%d" % qs, "smq%d" % qs, "subln"], writes=["on%d" % qs])

                def post_block_b(bi):
                    q0 = blocks[bi][0]
                    k = next_bank()
                    pt = P[k][:, :].bitcast(BF16)

                    def f(e):
                        e.transpose(out=pt[:, 0:128], in_=onb[0][:], identity=ident_bf[:])
                        return e.transpose(out=pt[:, 128:256], in_=onb[1][:], identity=ident_bf[:])
                    tr.op('pe', f, reads=["on0", "on1", "ident"], writes=["P%d" % k])
                    tr.op('dve', lambda e: e.tensor_copy(out=oT[:, 4 + h, q0:q0 + 256], in_=pt[:, 0:256]),
                          reads=["P%d" % k], writes=["g%d" % (4 + h)])

                def nkeys(ui):
                    return len(blocks[units[ui][0]][1])

                pending, pending_a = [], []
                for ui in range(len(units) + 1):
                    yield
                    nu = nkeys(ui) if ui < len(units) else 0
                    npv = nkeys(ui - 1) if ui > 0 else 0
                    for ki in range(max(nu, npv)):
                        if ki and ki % 2 == 0:
                            yield
                        if ki < nu:
                            qk(ui, ki)
                        if ki < npv:
                            pv(ui - 1, ki)
                    if ui > 0:
                        for pb in pending:
                            post_block_b(pb)
                        pending.clear()
                        for pa in pending_a:
                            post_block(pa)
                            pending.append(pa)
                        pending_a.clear()
                        post_pv(ui - 1)
                        if units[ui - 1][1] == 1:
                            pending_a.append(units[ui - 1][0])
                for pa in pending_a:
                    post_block(pa)
                    pending.append(pa)
                for pb in pending:
                    post_block_b(pb)

            def run(*gens, weights=None, set_ctx=True):
                gens = list(gens)
                w = dict(zip([id(g) for g in gens], weights or [1] * len(gens)))
                order = {id(g): i for i, g in enumerate(gens)}
                while gens:
                    for g in list(gens):
                        for _ in range(w[id(g)]):
                            try:
                                if set_ctx:
                                    gp['ctx'] = order[id(g)]
                                next(g)
                            except StopIteration:
                                gens.remove(g)
                                break

            def seq(*gens):
                for g in gens:
                    yield from g

            def evac_copy(dst, dstn, eng='act'):
                if eng == 'act':
                    return lambda ps, pn, t0, N, ti: tr.op(
                        'act', lambda e: e.copy(out=dst[:, t0:t0 + N], in_=ps[:, 0:N]), reads=[pn], writes=dstn)
                return lambda ps, pn, t0, N, ti: tr.op(
                    'dve', lambda e: e.tensor_copy(out=dst[:, t0:t0 + N], in_=ps[:, 0:N]), reads=[pn], writes=dstn)

            def hgrn_front(h):
                base = 7 * h
                yield from proj(base + 0, evac_copy(QT, ["QT"]))
                for d in range(2):
                    def evf(ps, pn, t0, N, ti, h=h, d=d):
                        m1 = next_mt()
                        Fx, Fn = (Fv, gn(8, 2)) if d == 0 else (F2v, ["F2"])
                        tr.op('act', lambda e: e.activation(out=mt[m1][:, 0:N], in_=ps[:, 0:N], func=AF.Sigmoid),
                              reads=[pn], writes=["mt%d" % m1])
                        tr.op('dve', lambda e: e.tensor_scalar(out=Fx[:, t0:t0 + N], in0=mt[m1][:, 0:N], scalar1=oml[:, h:h + 1],
                                                               scalar2=lb[:, h:h + 1], op0=ALU.mult, op1=ALU.add),
                              reads=["mt%d" % m1, "oml", "lb"], writes=Fn)
                    yield from proj(base + 2 + d, evf)

                def rest():
                    VT = gran(16)
                    yield from proj(base + 1, evac_copy(VT, gn(16)))
                    transposes(VT, gn(16), Vtok, gn(20, 2))
                    yield
                    yield from proj(base + 4, lambda ps, pn, t0, N, ti: tr.op(
                        'act', lambda e: e.activation(out=GS[:, t0:t0 + N], in_=ps[:, 0:N], func=AF.Silu), reads=[pn], writes=["GS"]))
                run(rest(), prep(0, h), prep(1, h), set_ctx=False)
                yield

            def attn_front(h, st):
                base = 7 * h
                yield from proj(base + 5, evac_copy(AQ, gn(8, 2), 'dve'))
                yield from proj(base + 6, evac_copy(AK, gn(10, 2), 'dve'))
                yield from qk_norm_rope(AQ, 8, 1, qrs[st], qrns[st], False, h)
                yield from qk_norm_rope(AK, 10, 2, krs[st], krns[st], True, h)

            for h in range(4):
                if h == 0:
                    run(hgrn_front(h))
                if h == 3:
                    gp['by_ctx'] = {0: [2, 3], 1: [2, 3]}
                    run(chains(h), attn_front(0, 0), weights=[1, 3])
                    gp['by_ctx'] = None
                else:
                    run(chains(h))
                if h < 3:
                    run(hgrn_front(h + 1), orec_final(h))
                else:
                    run(orec_final(h))
            barrier()
            run(vproj_and_caches())
            gp['banks'] = [0, 1, 2, 3]
            for h in range(4):
                gens = [attention(h, h % 2)]
                if h < 3:
                    gens.append(attn_front(h + 1, (h + 1) % 2))
                gp['by_ctx'] = {0: [0, 1, 2], 1: [3]} if h < 3 else None
                run(*gens, weights=[1, 1])
                gp['by_ctx'] = None

            for o in range(8):
                bi = cnt['wo'] % 2
                cnt['wo'] += 1
                tr.dma('pool', wobuf[bi][:, 0:8, :], wmo_d[o], "wo%d" % bi, writes=["wo%d" % bi, "QT", "GS", "KK"])
                for ti, (t0, N, ci) in enumerate(TT):
                    k = 5 + cnt['oo'] % 2
                    cnt['oo'] += 1

                    def f(e, k=k, bi=bi):
                        ins = None
                        for kc in range(8):
                            ins = e.matmul(P[k][:, 0:N], wobuf[bi][:, kc, :], oT[:, kc, t0:t0 + N], start=(kc == 0), stop=(kc == 7))
                        return ins
                    tr.op('pe', f, reads=["wo%d" % bi] + gn(0, 8), writes=["P%d" % k])
                    tr.op('dve', lambda e, k=k, o=o: e.scalar_tensor_tensor(
                        out=xT[:, o, t0:t0 + N], in0=P[k][:, 0:N], scalar=Gmod[:, 1, ci, o:o + 1],
                        in1=xT[:, o, t0:t0 + N], op0=ALU.mult, op1=ALU.add),
                        reads=["P%d" % k, "Gmod", "xT.%d.%d" % (o, ti)], writes=["xT.%d.%d" % (o, ti)])
            barrier()

        mtbig = sb("mtbig", [128, 2560], BF16)
        load_mixer_consts()

        def ada_hook(j):
            for sbi in ([0, 1] if j == 0 else [j + 1]):
                if sbi < ADA_NSLAB:
                    ada_slab(sbi)
        ffn(0, w1in_d, w1out_d, hook=(ada_hook if STAGE >= 2 else None))
        if STAGE >= 2:
            mods_finalize(1)
            mods_finalize(2)
        if STAGE >= 2:
            mixer()
        if STAGE >= 3:
            ffn(2, w2in_d, w2out_d)

        if STAGE < 3:
            for c in range(8):
                tr.dma('sp', yT_d[:, c, :], xT[:, c, :], "out_y", reads=["xT.%d.%d" % (c, t) for t in range(3)])
        tr.finish()
    return nc


def _fm(x2d):
    t, f = x2d.shape
    return np.ascontiguousarray(x2d.T.reshape(f // 128, 128, t).transpose(1, 0, 2))


def _host_consts():
    c = {}
    perm = np.zeros((128, 128), np.float32)
    for m in range(128):
        r = m % 32
        partner = m + 16 if r < 16 else m - 16
        perm[partner, m] = 1.0
    c["perm"] = perm
    blk = np.zeros((128, 128), np.float32)
    blk[:64, :64] = 1.0
    blk[64:, 64:] = 1.0
    c["blk64"] = blk
    c["ident"] = np.eye(128, dtype=np.float32)
    tri = np.zeros((64, 2, 64), np.float32)
    s = np.arange(64)[:, None]
    t = np.arange(64)[None, :]
    tri[:, 0, :] = (s <= t)
    tri[:, 1, :] = (s >= t)
    c["tri"] = tri
    mres = np.ones((128, T), np.float32)
    mres[:, ::CH] = 0.0
    c["mres"] = mres
    return c


def _rope_tables(sample):
    C = np.ones((128, T), np.float32)
    S = np.zeros((128, T), np.float32)
    if sample:
        n = 1024
        half = 32
        inv = (10000.0 ** (-np.arange(0, half, 2, dtype=np.float32) / half)).astype(np.float32)
        pos_row = np.repeat(np.arange(n // 64, dtype=np.float32), 64)
        pos_col = np.tile(np.arange(64, dtype=np.float32), n // 64)
        ar = (pos_row[:, None] * inv).astype(np.float32)
        ac = (pos_col[:, None] * inv).astype(np.float32)
        cr, sr, cc, sc = [a.astype(np.float32) for a in (np.cos(ar), np.sin(ar), np.cos(ac), np.sin(ac))]
        for m in range(2):
            b = 64 * m
            C[b + 0:b + 16, :n] = cr.T
            C[b + 16:b + 32, :n] = cr.T
            C[b + 32:b + 48, :n] = cc.T
            C[b + 48:b + 64, :n] = cc.T
            S[b + 0:b + 16, :n] = -sr.T
            S[b + 16:b + 32, :n] = sr.T
            S[b + 32:b + 48, :n] = -sc.T
            S[b + 48:b + 64, :n] = sc.T
    return C, S


def kernel(x_prompt, x_sample, c, cache_attn_k, cache_attn_v, state_hgrn, c_ctx,
           w_ada, b_ada, norm_ffn1, w_ffn1_in, w_ffn1_out, norm_mix, w_mix_in, w_mix_out,
           hgrn_lb_logits, hgrn_out_norm, attn_q_norm, attn_k_norm, attn_lambda, attn_subln,
           norm_ffn2, w_ffn2_in, w_ffn2_out):
    f = lambda a: np.asarray(a, np.float32)
    x_prompt, x_sample, c, c_ctx = f(x_prompt), f(x_sample), f(c), f(c_ctx)
    cache_attn_k, cache_attn_v, state_hgrn = f(cache_attn_k), f(cache_attn_v), f(state_hgrn)

    def ffn_in_layout(w):
        w = f(w)[0]
        g = w[:, :DFF].reshape(8, 128, NJ, 128)
        u = w[:, DFF:].reshape(8, 128, NJ, 128)
        return np.ascontiguousarray(np.concatenate([g, u], axis=3).transpose(2, 1, 0, 3))

    def out_layout(w, nk):
        w = f(w)[0] if w.ndim == 3 else f(w)
        return np.ascontiguousarray(w.reshape(nk, 128, 8, 128).transpose(2, 1, 0, 3))

    wm = f(w_mix_in)[0]
    order = []
    for h in range(4):
        order += [h, 4 + h, 8 + h, 12 + h, 16 + h, 20 + h, 24 + h]
    wm_chunks = wm.reshape(8, 128, 32, 128)
    wmix = wm_chunks[:, :, order, :].reshape(8, 128, 14, 256).transpose(2, 1, 0, 3)
    wav = wm[:, 3584:].reshape(8, 128, 512).transpose(1, 0, 2)

    shared = {
        "badaT": np.ascontiguousarray(f(b_ada)[0].reshape(72, 128).T),
        "gT": np.ascontiguousarray(np.stack([f(norm_ffn1)[0], f(norm_mix)[0], f(norm_ffn2)[0]])
                                   .reshape(3, 8, 128).transpose(2, 0, 1)),
        "w_ada": np.ascontiguousarray(f(w_ada)[0]),
        "w1in": ffn_in_layout(w_ffn1_in), "w1out": out_layout(w_ffn1_out, NJ),
        "w2in": ffn_in_layout(w_ffn2_in), "w2out": out_layout(w_ffn2_out, NJ),
        "wmix": np.ascontiguousarray(wmix), "wav": np.ascontiguousarray(wav),
        "wmo": out_layout(w_mix_out, 8),
        "lbl": np.ascontiguousarray(f(hgrn_lb_logits).reshape(2, 4, 128).transpose(2, 0, 1)),
        "small": np.ascontiguousarray(np.stack([
            f(hgrn_out_norm)[0], np.tile(f(attn_q_norm)[0], 2), np.tile(f(attn_k_norm)[0], 2),
            np.zeros(128, np.float32)], axis=1)),
        "lamp": np.ascontiguousarray(np.broadcast_to(f(attn_lambda)[0][None], (128, 4, 64))),
        "subln": np.ascontiguousarray(np.broadcast_to(f(attn_subln)[0][None], (128, 128))),
    }
    shared.update(_host_consts())
    ropeP = _rope_tables(False)
    ropeS_ = _rope_tables(True)

    in_maps = []
    for core in range(NCORES):
        sample = core < 2
        if sample:
            xs = np.concatenate([x_sample[core], x_prompt[30 + core]], axis=0)
            cond = np.stack([c[core], c_ctx])
        else:
            p0 = 5 * (core - 2)
            xs = x_prompt[p0:p0 + 5].reshape(T, D)
            cond = np.stack([c_ctx, c_ctx])
        m = dict(shared)
        m["xT"] = _fm(xs)
        m["condT"] = np.ascontiguousarray(cond.reshape(2, 8, 128).transpose(2, 1, 0))
        C_, S_ = ropeS_ if sample else ropeP
        m["ropeC"], m["ropeS"] = C_, S_
        NEG = -30000.0
        maskb = np.full((12, 4), NEG, np.float32)
        flags = np.zeros((128, 10), np.float32)
        if sample:
            maskb[:, :] = 0.0
            flags[:, 1:4] = 1.0
            flags[:, 5:8] = 1.0
            kc_ = cache_attn_k[core, 0]
            m["kcT"] = np.ascontiguousarray(kc_.reshape(512, 4, 128).transpose(2, 1, 0))
            m["vc"] = np.ascontiguousarray(cache_attn_v[core, 0].reshape(4, 128, 4, 128).transpose(1, 0, 2, 3))
            m["s0"] = np.ascontiguousarray(state_hgrn[core, 0].transpose(2, 0, 1, 3))
        else:
            for qb in range(4):
                maskb[4 + 2 * qb: 6 + 2 * qb, qb] = 0.0
            m["kcT"] = np.zeros((128, 4, 512), np.float32)
            m["vc"] = np.zeros((128, 4, 4, 128), np.float32)
            m["s0"] = np.zeros((128, 2, 4, 128), np.float32)
        m["maskb"] = np.ascontiguousarray(np.broadcast_to(maskb.reshape(1, 48), (128, 48)))
        m["flags"] = flags
        in_maps.append(m)

    if _ONLY_PREPARE:
        return in_maps
    nc = build_program()
    if _DEBUG_CORES:
        res = run_bass_kernel_spmd(nc, [in_maps[i] for i in _DEBUG_CORES], core_ids=list(range(len(_DEBUG_CORES))))
        return [res.results[0]["yT"]]
    res = run_bass_kernel_spmd(nc, in_maps, core_ids=list(range(NCORES)))
    return _assemble(res.results)


_ONLY_PREPARE = False
_DEBUG_CORES = None


def _assemble(R):
    y_prompt = np.zeros((32, 256, D), np.float32)
    y_sample = np.zeros((2, 1024, D), np.float32)
    new_k = np.zeros((32, 1, 256, 4, 2, 64), np.float32)
    new_v = np.zeros((32, 1, 256, 4, 128), np.float32)
    new_s = np.zeros((32, 1, 2, 4, 128, 128), np.float32)
    for core in range(NCORES):
        r = R[core]
        y = r["yT"].transpose(2, 1, 0).reshape(T, D)
        kk = r["kTo"].transpose(2, 1, 0).reshape(T, 4, 2, 64)
        vv = r["vo"].transpose(1, 0, 2).reshape(T, 4, 128)
        ss = r["so"].transpose(1, 2, 3, 0, 4)
        if core < 2:
            y_sample[core] = y[:1024]
            blocks = [(4, 30 + core)]
        else:
            blocks = [(b, 5 * (core - 2) + b) for b in range(5)]
        for b, pi in blocks:
            sl = slice(256 * b, 256 * (b + 1))
            y_prompt[pi] = y[sl]
            new_k[pi, 0] = kk[sl]
            new_v[pi, 0] = vv[sl]
            new_s[pi, 0] = ss[b]
    return (y_prompt, y_sample, new_k, new_v, new_s)
```
